# Optimizing a Trainium2 kernel written in Bass

```python
import math
import jax, jax.numpy as jnp
from jax import lax
import numpy as np


D_MODEL = 2048
BATCH = 4
SEQ = 4096
DEPTH = 2

GRID_W = 64
CTX_LEN = 256
HEAD_DIM = 128
EPS = 1e-6
NEG_INF = -1e30

FOURIER_GROUPS = 4
FOURIER_WIDTH = D_MODEL // 4
FOURIER_GROUP_DIM = FOURIER_WIDTH // FOURIER_GROUPS
NA_HEADS = (D_MODEL - FOURIER_WIDTH) // HEAD_DIM
NA_WIDTH = NA_HEADS * HEAD_DIM
NA_WIN_H = 8
NA_WIN_W = 16
NA_KEY_W = 2 * NA_WIN_W
EVEN_IN = FOURIER_WIDTH + 3 * NA_WIDTH

POOL_WINDOWS = (2, 4, 8, 16)
POOL_GROUPS = 4
POOL_WIDTH = D_MODEL // 4
POOL_GROUP_DIM = POOL_WIDTH // POOL_GROUPS
MLA_HEADS = (D_MODEL - POOL_WIDTH) // HEAD_DIM
Q_LORA_RANK = 768
KV_LORA_RANK = 512
QK_NOPE_DIM = 128
QK_ROPE_DIM = 64
V_HEAD_DIM = 128
ODD_IN = POOL_WIDTH + Q_LORA_RANK + KV_LORA_RANK + QK_ROPE_DIM
ROPE_BASE = 10000.0
ATTN_BLOCK = 128

D_FF = -(-8 * D_MODEL // (3 * 256)) * 256

N_EVEN = (DEPTH + 1) // 2
N_ODD = DEPTH // 2

kernel_name = 'hybrid_fourier_natten_pool_mla_dit'


def rmsnorm(x, g):
    xf = x.astype(jnp.float32)
    y = xf * lax.rsqrt(jnp.mean(xf * xf, axis=-1, keepdims=True) + EPS)
    return (y * g.astype(jnp.float32)).astype(x.dtype)


def modulate(h, shift, scale):
    return h * (1.0 + scale) + shift


def swiglu(h, wg, wu, wd):
    return (jax.nn.silu(h @ wg) * (h @ wu)) @ wd


def rope_2d(x):
    n = x.shape[1]
    t = jnp.arange(n)
    half = QK_ROPE_DIM // 2
    nf = half // 2
    inv = ROPE_BASE ** (-jnp.arange(nf, dtype=jnp.float32) / nf)

    def rot(xa, pos):
        ang = pos.astype(jnp.float32)[:, None] * inv[None, :]
        cos = jnp.cos(ang)[None, :, None, :]
        sin = jnp.sin(ang)[None, :, None, :]
        x1 = xa[..., :nf].astype(jnp.float32)
        x2 = xa[..., nf:].astype(jnp.float32)
        return jnp.concatenate([x1 * cos - x2 * sin, x1 * sin + x2 * cos], axis=-1)

    out = jnp.concatenate([rot(x[..., :half], t // GRID_W), rot(x[..., half:], t % GRID_W)], axis=-1)
    return out.astype(x.dtype)


def fourier_mix(u, w_four):
    b, n, _ = u.shape
    ug = u.reshape(b, n, FOURIER_GROUPS, FOURIER_GROUP_DIM).astype(jnp.float32)
    f = jnp.fft.fft2(ug, axes=(1, 3), norm='ortho').real.astype(u.dtype)
    y = jnp.einsum('bngc,gcd->bngd', f, w_four)
    return y.reshape(b, n, FOURIER_WIDTH)


def pool_mix(u, w_pool, pool_scale):
    b, n, _ = u.shape
    ug = u.reshape(b, n, POOL_GROUPS, POOL_GROUP_DIM)
    cs = jnp.concatenate([jnp.zeros((b, 1, POOL_GROUPS, POOL_GROUP_DIM), jnp.float32),
                          jnp.cumsum(ug.astype(jnp.float32), axis=1)], axis=1)
    t = jnp.arange(n)
    outs = []
    for g, w in enumerate(POOL_WINDOWS):
        lo = jnp.maximum(t - w // 2, 0)
        hi = jnp.minimum(t + w // 2, n)
        s = cs[:, hi, g] - cs[:, lo, g]
        cnt = (hi - lo).astype(jnp.float32)[None, :, None]
        outs.append(s / cnt - ug[:, :, g].astype(jnp.float32))
    p = jnp.stack(outs, axis=2).astype(u.dtype)
    y = jnp.einsum('bngc,gcd->bngd', p, w_pool).reshape(b, n, POOL_WIDTH)
    return y * pool_scale


def dense_ctx_attention(q, k, v):
    b, l, h, _ = q.shape
    s = jnp.einsum('bqhd,bkhd->bhqk', q, k).astype(jnp.float32) * (q.shape[-1] ** -0.5)
    p = jax.nn.softmax(s, axis=-1).astype(v.dtype)
    return jnp.einsum('bhqk,bkhd->bqhd', p, v).reshape(b, l, h * v.shape[-1])


def dense_block_attention(q, k_all, v_all):
    b, n, h, dq = q.shape
    dv = v_all.shape[-1]
    scale = dq ** -0.5
    qb = jnp.moveaxis(q.reshape(b, n // ATTN_BLOCK, ATTN_BLOCK, h, dq), 1, 0)

    def blk(qi):
        s = jnp.einsum('bqhd,bkhd->bhqk', qi, k_all).astype(jnp.float32) * scale
        p = jax.nn.softmax(s, axis=-1).astype(v_all.dtype)
        return jnp.einsum('bhqk,bkhd->bqhd', p, v_all)

    o = lax.map(blk, qb)
    return jnp.moveaxis(o, 0, 1).reshape(b, n, h * dv)


def neighbourhood_attention(q, k, v, kc, vc, rpb):
    b, n, h, d = q.shape
    rows = n // GRID_W
    kh = min(NA_WIN_H, rows)
    n_cb = GRID_W // NA_WIN_W
    scale = d ** -0.5
    qg = q.reshape(b, rows, GRID_W, h, d)
    kg = k.reshape(b, rows, GRID_W, h, d)
    vg = v.reshape(b, rows, GRID_W, h, d)
    qcol = np.arange(n_cb)[:, None] * NA_WIN_W + np.arange(NA_WIN_W)[None, :]
    kstart = np.clip(np.arange(n_cb) * NA_WIN_W - NA_WIN_W // 2, 0, GRID_W - NA_KEY_W)
    kcol = kstart[:, None] + np.arange(NA_KEY_W)[None, :]
    cstart = np.clip(qcol - NA_WIN_W // 2, 0, GRID_W - NA_WIN_W)
    col_mask = jnp.asarray((kcol[:, None, :] >= cstart[..., None]) & (kcol[:, None, :] < cstart[..., None] + NA_WIN_W))
    col_idx = jnp.asarray(np.clip(kcol[:, None, :] - qcol[..., None] + NA_WIN_W - 1, 0, 2 * NA_WIN_W - 2))
    kcol_flat = kcol.reshape(-1)

    def row_block(r):
        rs = jnp.clip(r - kh // 2, 0, rows - kh)
        q_r = lax.dynamic_index_in_dim(qg, r, axis=1, keepdims=False).reshape(b, n_cb, NA_WIN_W, h, d)
        k_r = lax.dynamic_slice_in_dim(kg, rs, kh, axis=1)[:, :, kcol_flat].reshape(b, kh, n_cb, NA_KEY_W, h, d)
        v_r = lax.dynamic_slice_in_dim(vg, rs, kh, axis=1)[:, :, kcol_flat].reshape(b, kh, n_cb, NA_KEY_W, h, d)
        s_loc = jnp.einsum('bjqhd,brjkhd->bhjqrk', q_r, k_r).astype(jnp.float32) * scale
        row_idx = rs + jnp.arange(kh) - r + NA_WIN_H - 1
        bias = rpb[:, row_idx[:, None, None, None], col_idx[None]]
        bias = jnp.transpose(bias, (0, 2, 3, 1, 4)).astype(jnp.float32)
        s_loc = jnp.where(col_mask[:, :, None, :], s_loc + bias, NEG_INF)
        s_loc = s_loc.reshape(b, h, n_cb, NA_WIN_W, kh * NA_KEY_W)
        s_ctx = jnp.einsum('bjqhd,blhd->bhjql', q_r, kc).astype(jnp.float32) * scale
        p = jax.nn.softmax(jnp.concatenate([s_loc, s_ctx], axis=-1), axis=-1)
        p_loc = p[..., :kh * NA_KEY_W].reshape(b, h, n_cb, NA_WIN_W, kh, NA_KEY_W).astype(v.dtype)
        p_ctx = p[..., kh * NA_KEY_W:].astype(v.dtype)
        o = jnp.einsum('bhjqrk,brjkhd->bjqhd', p_loc, v_r) + jnp.einsum('bhjql,blhd->bjqhd', p_ctx, vc)
        return o.reshape(b, GRID_W, h, d)

    o = lax.map(row_block, jnp.arange(rows))
    return jnp.moveaxis(o, 0, 1).reshape(b, n, h * d)


def mla_queries(cq_raw, g_q, w_uq, rotate):
    b, n, _ = cq_raw.shape
    q = (rmsnorm(cq_raw, g_q) @ w_uq).reshape(b, n, MLA_HEADS, QK_NOPE_DIM + QK_ROPE_DIM)
    if rotate:
        q = jnp.concatenate([q[..., :QK_NOPE_DIM], rope_2d(q[..., QK_NOPE_DIM:])], axis=-1)
    return q


def mla_keys_values(ckv_raw, k_rope, g_kv, w_ukv, rotate):
    b, n, _ = ckv_raw.shape
    kv = (rmsnorm(ckv_raw, g_kv) @ w_ukv).reshape(b, n, MLA_HEADS, QK_NOPE_DIM + V_HEAD_DIM)
    k_nope, v = kv[..., :QK_NOPE_DIM], kv[..., QK_NOPE_DIM:]
    kr = k_rope[:, :, None, :]
    if rotate:
        kr = rope_2d(kr)
    k = jnp.concatenate([k_nope, jnp.broadcast_to(kr, (b, n, MLA_HEADS, QK_ROPE_DIM))], axis=-1)
    return k, v


def even_mixer(h, hc, w_in, w_four, rpb, w_out, with_ctx_out):
    b, n, _ = h.shape
    lc = hc.shape[1]
    p = h @ w_in
    u = p[..., :FOURIER_WIDTH]
    q, k, v = jnp.split(p[..., FOURIER_WIDTH:], 3, axis=-1)
    q = q.reshape(b, n, NA_HEADS, HEAD_DIM)
    k = k.reshape(b, n, NA_HEADS, HEAD_DIM)
    v = v.reshape(b, n, NA_HEADS, HEAD_DIM)
    if with_ctx_out:
        pc = hc @ w_in
        uc = pc[..., :FOURIER_WIDTH]
        qc, kc, vc = jnp.split(pc[..., FOURIER_WIDTH:], 3, axis=-1)
        qc = qc.reshape(b, lc, NA_HEADS, HEAD_DIM)
    else:
        kc, vc = jnp.split(hc @ w_in[:, FOURIER_WIDTH + NA_WIDTH:], 2, axis=-1)
    kc = kc.reshape(b, lc, NA_HEADS, HEAD_DIM)
    vc = vc.reshape(b, lc, NA_HEADS, HEAD_DIM)
    y = jnp.concatenate([fourier_mix(u, w_four), neighbourhood_attention(q, k, v, kc, vc, rpb)], axis=-1) @ w_out
    yc = None
    if with_ctx_out:
        yc = jnp.concatenate([fourier_mix(uc, w_four), dense_ctx_attention(qc, kc, vc)], axis=-1) @ w_out
    return y, yc


def odd_mixer(h, hc, w_in, w_pool, pool_scale, g_q, g_kv, w_uq, w_ukv, w_out, with_ctx_out):
    o_q = POOL_WIDTH
    o_kv = o_q + Q_LORA_RANK
    o_kr = o_kv + KV_LORA_RANK
    p = h @ w_in
    q = mla_queries(p[..., o_q:o_kv], g_q, w_uq, True)
    k, v = mla_keys_values(p[..., o_kv:o_kr], p[..., o_kr:], g_kv, w_ukv, True)
    if with_ctx_out:
        pc = hc @ w_in
        pc_kv = pc[..., o_kv:]
    else:
        pc_kv = hc @ w_in[:, o_kv:]
    kc, vc = mla_keys_values(pc_kv[..., :KV_LORA_RANK], pc_kv[..., KV_LORA_RANK:], g_kv, w_ukv, False)
    attn = dense_block_attention(q, jnp.concatenate([kc, k], axis=1), jnp.concatenate([vc, v], axis=1))
    y = jnp.concatenate([pool_mix(p[..., :o_q], w_pool, pool_scale), attn], axis=-1) @ w_out
    yc = None
    if with_ctx_out:
        qc = mla_queries(pc[..., o_q:o_kv], g_q, w_uq, False)
        yc = jnp.concatenate([pool_mix(pc[..., :o_q], w_pool, pool_scale), dense_ctx_attention(qc, kc, vc)], axis=-1) @ w_out
    return y, yc


def setup_inputs(seed: int = 0) -> dict:
    key = jax.random.key(seed)
    ks = jax.random.split(key, 24)
    f32 = jnp.float32

    def nrm(k, shape, scale):
        return jax.random.normal(k, shape, f32) * scale

    def gain(k, shape):
        return 1.0 + 0.02 * jax.random.normal(k, shape, f32)

    return {
        'x': nrm(ks[0], (BATCH, SEQ, D_MODEL), 1.0),
        'c': nrm(ks[1], (BATCH, D_MODEL), 1.0),
        'ctx': nrm(ks[2], (BATCH, CTX_LEN, D_MODEL), 1.0),
        'c_ctx': nrm(ks[3], (D_MODEL,), 1.0),
        'w_mod': nrm(ks[4], (DEPTH, D_MODEL, 6 * D_MODEL), 0.5 * D_MODEL ** -0.5),
        'b_mod': nrm(ks[5], (DEPTH, 6 * D_MODEL), 0.01),
        'norm1_g': gain(ks[6], (DEPTH, D_MODEL)),
        'norm2_g': gain(ks[7], (DEPTH, D_MODEL)),
        'w_in_ab': nrm(ks[8], (N_EVEN, D_MODEL, EVEN_IN), D_MODEL ** -0.5),
        'w_four': nrm(ks[9], (N_EVEN, FOURIER_GROUPS, FOURIER_GROUP_DIM, FOURIER_GROUP_DIM), FOURIER_GROUP_DIM ** -0.5),
        'na_rpb': nrm(ks[10], (N_EVEN, NA_HEADS, 2 * NA_WIN_H - 1, 2 * NA_WIN_W - 1), 0.1),
        'w_out_ab': nrm(ks[11], (N_EVEN, D_MODEL, D_MODEL), D_MODEL ** -0.5),
        'w_in_cd': nrm(ks[12], (N_ODD, D_MODEL, ODD_IN), D_MODEL ** -0.5),
        'w_pool': nrm(ks[13], (N_ODD, POOL_GROUPS, POOL_GROUP_DIM, POOL_GROUP_DIM), POOL_GROUP_DIM ** -0.5),
        'pool_scale': 1.0 + 0.1 * jax.random.normal(ks[14], (N_ODD, POOL_WIDTH), f32),
        'mla_gq': gain(ks[15], (N_ODD, Q_LORA_RANK)),
        'mla_gkv': gain(ks[16], (N_ODD, KV_LORA_RANK)),
        'w_uq': nrm(ks[17], (N_ODD, Q_LORA_RANK, MLA_HEADS * (QK_NOPE_DIM + QK_ROPE_DIM)), Q_LORA_RANK ** -0.5),
        'w_ukv': nrm(ks[18], (N_ODD, KV_LORA_RANK, MLA_HEADS * (QK_NOPE_DIM + V_HEAD_DIM)), KV_LORA_RANK ** -0.5),
        'w_out_cd': nrm(ks[19], (N_ODD, D_MODEL, D_MODEL), D_MODEL ** -0.5),
        'w_ffn_gate': nrm(ks[20], (DEPTH, D_MODEL, D_FF), D_MODEL ** -0.5),
        'w_ffn_up': nrm(ks[21], (DEPTH, D_MODEL, D_FF), D_MODEL ** -0.5),
        'w_ffn_down': nrm(ks[22], (DEPTH, D_FF, D_MODEL), D_FF ** -0.5),
        'final_g': gain(ks[23], (D_MODEL,)),
    }


def reference(x, c, ctx, c_ctx, w_mod, b_mod, norm1_g, norm2_g, w_in_ab, w_four, na_rpb, w_out_ab,
              w_in_cd, w_pool, pool_scale, mla_gq, mla_gkv, w_uq, w_ukv, w_out_cd,
              w_ffn_gate, w_ffn_up, w_ffn_down, final_g):
    silu_c = jax.nn.silu(c)
    silu_cc = jax.nn.silu(c_ctx)
    xc = ctx
    for l in range(DEPTH):
        last = l == DEPTH - 1
        mod = (silu_c @ w_mod[l] + b_mod[l])[:, None, :]
        modc = (silu_cc @ w_mod[l] + b_mod[l])[None, None, :]
        sh1, sc1, g1, sh2, sc2, g2 = jnp.split(mod, 6, axis=-1)
        csh1, csc1, cg1, csh2, csc2, cg2 = jnp.split(modc, 6, axis=-1)
        h = modulate(rmsnorm(x, norm1_g[l]), sh1, sc1)
        hc = modulate(rmsnorm(xc, norm1_g[l]), csh1, csc1)
        i = l // 2
        if l % 2 == 0:
            y, yc = even_mixer(h, hc, w_in_ab[i], w_four[i], na_rpb[i], w_out_ab[i], not last)
        else:
            y, yc = odd_mixer(h, hc, w_in_cd[i], w_pool[i], pool_scale[i], mla_gq[i], mla_gkv[i],
                              w_uq[i], w_ukv[i], w_out_cd[i], not last)
        x = x + g1 * y
        x = x + g2 * swiglu(modulate(rmsnorm(x, norm2_g[l]), sh2, sc2), w_ffn_gate[l], w_ffn_up[l], w_ffn_down[l])
        if not last:
            xc = xc + cg1 * yc
            xc = xc + cg2 * swiglu(modulate(rmsnorm(xc, norm2_g[l]), csh2, csc2), w_ffn_gate[l], w_ffn_up[l], w_ffn_down[l])
    return rmsnorm(x, final_g)
```

```python
import contextlib
import numpy as np
import ml_dtypes
import concourse.bass as bass
import concourse.mybir as mybir
from concourse.bass_utils import run_bass_kernel_spmd

F32 = mybir.dt.float32
BF16 = mybir.dt.bfloat16
AF = mybir.ActivationFunctionType
ALU = mybir.AluOpType

EPOCH = 4000
DMA_RING = 8

D = 2048
SEQ = 4096
HALF = 2048
CTX = 256
NTOK = HALF + CTX
NKV = NTOK + 512
DFF = 5632
NEG = -30000.0
EPS = 1e-6


class _Op:
    __slots__ = ("eng", "fn", "idx", "deps", "dma", "flag", "ring_slot", "ring_val")

    def __init__(self, eng, fn, dma):
        self.eng = eng
        self.fn = fn
        self.dma = dma
        self.deps = []
        self.flag = False


class Sched:
    ENGS = ("pe", "act", "dve", "pool", "sp")

    def __init__(self, nc):
        self.nc = nc
        self.ops = {e: [] for e in self.ENGS}
        self.state = {}
        self.ndma = {e: 0 for e in self.ENGS}
        self.bar = {e: [] for e in self.ENGS}

    def barrier(self):
        last = []
        for e in self.ENGS:
            ops = self.ops[e]
            if not ops:
                continue
            last.append(ops[-1])
            k = 0
            for op in reversed(ops):
                if op.dma:
                    last.append(op)
                    k += 1
                    if k >= DMA_RING:
                        break
        for e in self.ENGS:
            self.bar[e] = list(last)

    def add(self, eng, fn, reads=(), writes=(), dma=False):
        op = _Op(eng, fn, dma)
        op.idx = len(self.ops[eng])
        deps = []
        if self.bar[eng]:
            deps.extend(self.bar[eng])
            self.bar[eng] = []
        for b in reads:
            st = self.state.get(b)
            if st is None:
                st = self.state[b] = [None, {}, []]
            if st[0] is not None:
                deps.append(st[0])
            if dma:
                st[2].append(op)
            else:
                st[1][eng] = op
        for b in writes:
            st = self.state.get(b)
            if st is None:
                st = self.state[b] = [None, {}, []]
            if st[0] is not None:
                deps.append(st[0])
            for r in st[1].values():
                deps.append(r)
            deps.extend(st[2])
            st[0] = op
            st[1] = {}
            st[2] = []
        if dma:
            j = self.ndma[eng]
            self.ndma[eng] += 1
            op.ring_slot = j % DMA_RING
            op.ring_val = 16 * (j // DMA_RING + 1)
        seen = set()
        for d in deps:
            if d is op or id(d) in seen:
                continue
            seen.add(id(d))
            if d.eng == eng and not d.dma and eng == "pe":
                continue
            op.deps.append(d)
        self.ops[eng].append(op)
        return op

    def emit(self):
        nc = self.nc
        for e in self.ENGS:
            waited = {}
            waited_dma = set()
            for op in self.ops[e]:
                nd = []
                for d in op.deps:
                    if d.dma:
                        if id(d) in waited_dma:
                            continue
                        waited_dma.add(id(d))
                        nd.append(d)
                    else:
                        if waited.get(d.eng, -1) >= d.idx:
                            continue
                        waited[d.eng] = d.idx
                        d.flag = True
                        nd.append(d)
                op.deps = nd
        count = {}
        nsem = {}
        for e in self.ENGS:
            c = 0
            for op in self.ops[e]:
                if op.flag and not op.dma:
                    count[id(op)] = c
                    c += 1
            nsem[e] = (c + EPOCH - 1) // EPOCH
        with contextlib.ExitStack() as es:
            csem = {e: [es.enter_context(nc.semaphore(f"c_{e}_{k}")) for k in range(nsem[e])] for e in self.ENGS}
            dsem = {e: [es.enter_context(nc.semaphore(f"d_{e}_{k}")) for k in range(DMA_RING)]
                    for e in self.ENGS if self.ndma[e] > 0}
            block = es.enter_context(nc.Block())
            engobj = {"pe": block.tensor, "act": block.scalar, "dve": block.vector, "pool": block.gpsimd, "sp": block.sync}

            def make(e):
                def body(eng):
                    for op in self.ops[e]:
                        for d in op.deps:
                            if d.dma:
                                eng.wait_ge(dsem[d.eng][d.ring_slot], d.ring_val)
                            else:
                                c = count[id(d)]
                                eng.wait_ge(csem[d.eng][c // EPOCH], c % EPOCH + 1)
                        if op.dma:
                            if op.ring_val > 16:
                                eng.wait_ge(dsem[e][op.ring_slot], op.ring_val - 16)
                            ins = op.fn(eng)
                            ins.then_inc(dsem[e][op.ring_slot], 16)
                        else:
                            ins = op.fn(eng)
                            if op.flag:
                                c = count[id(op)]
                                ins.then_inc(csem[e][c // EPOCH], 1)
                    if self.ndma[e] > 0:
                        n = self.ndma[e]
                        for s in range(min(DMA_RING, n)):
                            last = ((n - 1 - s) // DMA_RING) + 1
                            eng.wait_ge(dsem[e][s], 16 * last)
                return body

            for e in self.ENGS:
                if self.ops[e]:
                    engobj[e](make(e))


def DMA(S, q, out, in_, reads, writes):
    return S.add(q, lambda e: e.dma_start(out=out, in_=in_), reads, writes, dma=True)


def MM(S, out, lhsT, rhs, start, stop, reads, writes):
    return S.add("pe", lambda e: e.matmul(out, lhsT=lhsT, rhs=rhs, start=start, stop=stop), reads, writes)


def TR(S, out, in_, ident, reads, writes):
    return S.add("pe", lambda e: e.transpose(out=out, in_=in_, identity=ident), reads, writes)


def ACT(S, out, in_, func, reads, writes, bias=None, scale=None):
    kw = {}
    if bias is not None:
        kw["bias"] = bias
    if scale is not None:
        kw["scale"] = scale
    return S.add("act", lambda e: e.activation(out=out, in_=in_, func=func, **kw), reads, writes)


def COPY(S, eng, out, in_, reads, writes):
    if eng == "act":
        return S.add("act", lambda e: e.copy(out=out, in_=in_), reads, writes)
    return S.add(eng, lambda e: e.tensor_copy(out=out, in_=in_), reads, writes)


def STT(S, eng, out, in0, scalar, in1, op0, op1, reads, writes):
    return S.add(eng, lambda e: e.scalar_tensor_tensor(out=out, in0=in0, scalar=scalar, in1=in1, op0=op0, op1=op1), reads, writes)


def TT(S, eng, out, in0, in1, op, reads, writes):
    return S.add(eng, lambda e: e.tensor_tensor(out=out, in0=in0, in1=in1, op=op), reads, writes)


def RECIP(S, out, in_, reads, writes):
    return S.add("dve", lambda e: e.reciprocal(out=out, in_=in_), reads, writes)


def _dft_tables(s):
    n_in = np.concatenate([np.arange(HALF) + HALF * s, np.arange(HALF) + HALF * (1 - s)]).astype(np.int64)[:, None]
    n_out = (np.arange(HALF, dtype=np.int64) + HALF * s)[None, :]
    ang = 2.0 * np.pi * ((n_in * n_out) % SEQ).astype(np.float64) / SEQ
    cs = np.stack([np.cos(ang) / 64.0, -np.sin(ang) / 64.0], axis=1)
    return cs.astype(ml_dtypes.bfloat16)


def _dft_ctx():
    n = np.arange(CTX, dtype=np.int64)
    ang = 2.0 * np.pi * ((n[:, None] * n[None, :]) % CTX).astype(np.float64) / CTX
    cs = np.stack([np.cos(ang) / 16.0, -np.sin(ang) / 16.0], axis=1)
    return cs.astype(ml_dtypes.bfloat16)


def _dft_ch():
    c = np.arange(128, dtype=np.int64)
    ang = 2.0 * np.pi * ((c[:, None] * c[None, :]) % 128).astype(np.float64) / 128
    cs = np.concatenate([np.cos(ang), np.sin(ang)], axis=1) / np.sqrt(128.0)
    return cs.astype(ml_dtypes.bfloat16)


def _na_win(ti):
    if ti == 0:
        return -4, 12
    if ti == 1:
        return -2, 10
    if ti == 15:
        return 24, 11
    return 2 * ti - 4, 9


NA_CLS_TILES = [0, 1, 2, 14, 15]


def _na_cls(ti):
    return {0: 0, 1: 1, 14: 3, 15: 4}.get(ti, 2)


def _na_class_tables(rpb, s):
    h = rpb.shape[0]
    flat = np.concatenate([rpb.reshape(h, -1), np.full((h, 1), NEG, np.float32)], axis=1)
    out = []
    q = np.arange(128)
    p = np.arange(128)
    for ti in NA_CLS_TILES:
        gi = 16 * s + ti
        w0, nrow = _na_win(ti)
        r = 2 * gi + q // 64
        qc = q % 64
        rs = np.clip(r - 4, 0, 56)
        cst = np.clip(qc - 8, 0, 48)
        idx = np.full((128, 6, 128), 15 * 31, np.int64)
        for c in range((nrow + 1) // 2):
            lrow = w0 + 2 * c + p // 64
            krow = 32 * s + lrow
            kcol = p % 64
            okr = (krow[:, None] >= rs[None, :]) & (krow[:, None] < rs[None, :] + 8) & (krow[:, None] <= 63) & (krow[:, None] >= 0)
            okr = okr & ((lrow < w0 + nrow)[:, None])
            okc = (kcol[:, None] >= cst[None, :]) & (kcol[:, None] < cst[None, :] + 16)
            ri = np.clip(krow[:, None] - r[None, :] + 7, 0, 14)
            ci = np.clip(kcol[:, None] - qc[None, :] + 15, 0, 30)
            lin = ri * 31 + ci
            idx[:, c, :] = np.where(okr & okc, lin, 15 * 31)
        out.append(flat[:, idx])
    return np.ascontiguousarray(np.stack(out, 0))


def _rope_tables(s):
    t = np.arange(HALF, dtype=np.float64) + HALF * s
    inv = 10000.0 ** (-np.arange(16, dtype=np.float64) / 16.0)
    out = np.zeros((2, 64, HALF), np.float64)
    for half, pos in enumerate([np.floor(t / 64.0), np.mod(t, 64.0)]):
        ang = inv[:, None] * pos[None, :]
        b = 32 * half
        out[0, b:b + 16] = np.cos(ang)
        out[0, b + 16:b + 32] = np.cos(ang)
        out[1, b:b + 16] = -np.sin(ang)
        out[1, b + 16:b + 32] = np.sin(ang)
    return out.astype(np.float32)


ROPE_PERM = list(range(16, 32)) + list(range(0, 16)) + list(range(48, 64)) + list(range(32, 48))


def _pool_tables(s):
    out = np.zeros((128, 3, 4, 3, 128), np.float64)
    i = np.arange(128)
    for ci, gt in enumerate([16 * s, 16 * s + 1 if s == 0 else 16 * s + 14, 16 * s + 15]):
        if ci == 1:
            gt = 8
        for g, w in enumerate((2, 4, 8, 16)):
            to = 128 * gt + i
            lo = np.maximum(to - w // 2, 0)
            hi = np.minimum(to + w // 2, SEQ)
            cnt = (hi - lo).astype(np.float64)
            for rel in range(3):
                tin = 128 * (gt + rel - 1) + i
                m = (tin[:, None] >= lo[None, :]) & (tin[:, None] < hi[None, :])
                val = np.where(m, 1.0 / cnt[None, :], 0.0) - (tin[:, None] == to[None, :]).astype(np.float64)
                out[:, ci, g, rel, :] = val
    return out.astype(ml_dtypes.bfloat16)


def _fm(v):
    return np.ascontiguousarray(np.asarray(v, np.float32).reshape(-1, 128).T)


IN_SHAPES = {
    "x_own": ([HALF, D], F32), "x_par": ([HALF, D], F32), "x_halo": ([512, D], F32), "ctxb": ([CTX, D], F32),
    "vecs": ([128, 400], F32), "ident": ([128, 128], F32), "w_mod": ([2, D, 6 * D], F32),
    "w_in_ab": ([D, 5120], F32), "w_four": ([4, 128, 128], F32), "w_out_ab": ([D, D], F32),
    "w_ffn_gate": ([2, D, DFF], F32), "w_ffn_up": ([2, D, DFF], F32), "w_ffn_down": ([2, DFF, D], F32),
    "csn": ([SEQ, 2, HALF], BF16), "csx": ([CTX, 2, CTX], BF16), "csc": ([128, 256], BF16),
    "nabias": ([5, 12, 128, 6, 128], F32),
    "w_in_cd": ([D, 1856], F32), "w_pool": ([4, 128, 128], F32), "w_uq": ([768, 2304], F32), "w_ukv": ([512, 3072], F32),
    "w_out_cd": ([D, D], F32), "rope": ([2, 64, HALF], F32), "ptab": ([128, 3, 4, 3, 128], BF16),
    "xT_in": ([D, NTOK], F32), "exg": ([2, 704, NTOK], BF16),
    "x_halo2": ([512, D], F32), "csn2": ([SEQ, 2, HALF], BF16), "nabias2": ([5, 12, 128, 6, 128], F32), "rope2": ([2, 64, HALF], F32),
}


class _LazyIn(dict):
    def __init__(self, nc):
        super().__init__()
        self.nc = nc

    def __missing__(self, name):
        shape, dt = IN_SHAPES[name]
        ap = self.nc.dram_tensor(name, list(shape), dt, kind="ExternalInput").ap()
        self[name] = ap
        return ap


class Prog:
    def __init__(self, mode, stage=3, dbg=False):
        self.mode = mode
        self.stage = stage
        self.dbg = dbg

    def build(self):
        nc = bass.Bass("TRN2", target_bir_lowering=False)
        self.nc = nc

        def dscr(name, shape, dt):
            return nc.dram_tensor(name, list(shape), dt).ap()

        def dout(name, shape, dt):
            return nc.dram_tensor(name, list(shape), dt, kind="ExternalOutput").ap()

        I = _LazyIn(nc)
        self.I = I
        A = self.mode == "A"
        Fm = self.mode == "F"
        self.xT = dout("xT_out", [D, NTOK], F32) if A else dscr("xT_s", [D, NTOK], F32)
        self.exb = dout("ex_out", [704, NTOK], BF16) if (A or self.stage < 6) else dscr("exb_s", [704, NTOK], BF16)
        self.xT2 = dscr("xT2_s", [D, NTOK], F32)
        self.exb2 = dscr("exb2_s", [704, NTOK], BF16)
        self.xTc, self.xk, self.role = self.xT, "xT", 0
        self.hT = dscr("hT_s", [D, NKV], BF16)
        self.mixT = dscr("mixT_s", [D, NTOK], BF16)
        self.qT = dscr("qT_s", [1536, NTOK], BF16)
        self.kT = dscr("kT_s", [1536, NKV], BF16)
        self.vS = dscr("v_s", [NKV, 1536], BF16)
        self.aT = dscr("aT_s", [DFF, NTOK], BF16)
        self.l1raw = dscr("l1raw_s", [1408, NTOK], F32)
        self.qn_s = dscr("qn_s", [1536, HALF], BF16)
        self.qr_s = dscr("qr_s", [768, HALF], BF16)
        self.kn_s = dscr("kn_s", [1536, SEQ + CTX], BF16)
        self.v1_s = dscr("v1_s", [SEQ + CTX, 1536], BF16)
        if not A:
            self.out = dout("out", [HALF, D], F32)
            if not Fm:
                self.exg = I["exg"]

        S = Sched(nc)
        self.S = S
        with contextlib.ExitStack() as es:
            self.es = es
            self.pp = [es.enter_context(nc.psum_tensor(f"pp{i}", [128, 1024], F32)) for i in range(4)]
            self.ident_f = self.sb("ident_f", [128, 128], F32)
            self.ones_b = self.sb("ones_b", [128, 128], BF16)
            self.vecs = self.sb("vecs_sb", [128, 400], F32)
            self.scb = self.sb("scb", [128, 16, 2], BF16)
            self.modT = self.sb("modT", [128, 2, 96, 2], F32)
            self.G = self.sb("Gmod", [128, 2, 2, 2, 16], F32)
            self.epsb = self.sb("epsb", [128, 1], F32)
            DMA(S, "sp", self.ident_f[:], I["ident"], [], ["ident_f"])
            DMA(S, "sp", self.vecs[:], I["vecs"], [], ["vecs"])
            S.add("pool", lambda e: e.memset(self.ones_b[:], 1.0), [], ["ones_b"])
            S.add("pool", lambda e: e.memset(self.epsb[:], EPS), [], ["epsb"])
            if not A and not Fm:
                for b in range(5):
                    c0 = 512 * b
                    nt = min(512, NTOK - c0)
                    DMA(S, "sp", self.xTc[:, c0:c0 + nt], I["xT_in"][:, c0:c0 + nt], [], [(self.xk, k, b) for k in range(16)])
            self.phase_mod()
            if Fm:
                self.set_role(0)
                self.layer0()
                self.set_role(1)
                self.layer0()
                with contextlib.ExitStack() as es_l1:
                    self.set_role(0)
                    self.l1_part1(es_l1)
                    self.set_role(1)
                    self.l1_part1(es_l1)
                    self.set_role(0)
                    self.l1_part2(es_l1)
                    self.l1_attn()
                self.barrier()
                self.outproj(1, I["w_out_cd"], HALF)
                self.barrier()
                self.ffn(1, HALF)
                self.barrier()
                self.final_norm()
            elif A:
                self.layer0()
                if self.stage >= 4:
                    with contextlib.ExitStack() as es_l1:
                        self.l1_part1(es_l1)
            else:
                st = self.stage
                with contextlib.ExitStack() as es_l1:
                    self.l1_part1(es_l1)
                    if st >= 2:
                        self.l1_part2(es_l1)
                    if st >= 3:
                        self.l1_attn()
                self.barrier()
                if st >= 4:
                    self.outproj(1, I["w_out_cd"], HALF)
                    self.barrier()
                if st >= 5:
                    self.ffn(1, HALF)
                    self.barrier()
                if st >= 6:
                    self.final_norm()
                elif self.dbg:
                    xd = dout("xT_dbg", [D, NTOK], F32)
                    md = dout("mix_dbg", [D, NTOK], BF16)
                    self.barrier()
                    DMA(S, "sp", xd, self.xT, [], ["xd"])
                    DMA(S, "sp", md, self.mixT, [], ["md"])
                    rd_ = dout("raw_dbg", [1408, NTOK], F32)
                    DMA(S, "sp", rd_, self.l1raw, [], ["rawd"])
            S.emit()
        _NAMES[self.mode] = list(I.keys())
        return nc

    def set_role(self, r):
        self.role = r
        self.xTc, self.xk = (self.xT, "xT") if r == 0 else (self.xT2, "xT2")

    def sb(self, name, shape, dt, es=None):
        self._uid = getattr(self, "_uid", 0) + 1
        return (es or self.es).enter_context(self.nc.sbuf_tensor(f"{name}_u{self._uid}", list(shape), dt))

    def bank(self, i):
        return self.pp[i // 2][:, (i % 2) * 512:(i % 2) * 512 + 512], ("ps", i)

    def barrier(self):
        self.S.barrier()

    V_C2 = 0
    V_N1 = 32
    V_N2 = 64
    V_FG = 96
    V_BM = 112
    V_PS = 304
    V_GQ = 308
    V_GKV = 314

    def phase_mod(self):
        S, nc, I = self.S, self.nc, self.I
        with contextlib.ExitStack() as es:
            wts = [self.sb(f"wmod{i}", [128, 16, 512], BF16, es) for i in range(2)]
            ACT(S, self.scb[:].rearrange("p k m -> p (k m)"), self.vecs[:, 0:32], AF.Silu, ["vecs"], ["scb"])
            n = 0
            for l in range(2):
                for cg in range(24):
                    wt = wts[n % 2]
                    wk = ("wmod", n % 2)
                    DMA(S, "pool", wt[:], I["w_mod"][l, :, cg * 512:(cg + 1) * 512].rearrange("(k p) n -> p k n", p=128), [], [wk])
                    ps, pk = self.bank(n % 2)
                    for j in range(4):
                        for kc in range(16):
                            MM(S, ps[:, j * 2:j * 2 + 2], wt[:, kc, j * 128:(j + 1) * 128], self.scb[:, kc, :], kc == 0, kc == 15,
                               [wk, "scb"], [pk])
                    for m in range(2):
                        TT(S, "dve", self.modT[:, l, cg * 4:cg * 4 + 4, m], ps[:, 0:8].rearrange("p (j m) -> p j m", m=2)[:, :, m],
                           self.vecs[:, self.V_BM + l * 96 + cg * 4:self.V_BM + l * 96 + cg * 4 + 4], ALU.add,
                           [pk, "vecs"], [("modT", l)])
                    n += 1
            for l in range(2):
                for nrm in range(2):
                    for m in range(2):
                        sc = self.modT[:, l, 16 + 48 * nrm:32 + 48 * nrm, m]
                        gcol = (self.V_N1 if nrm == 0 else self.V_N2) + l * 16
                        STT(S, "dve", self.G[:, l, nrm, m, :], sc, 1.0, self.vecs[:, gcol:gcol + 16], ALU.add, ALU.mult,
                            [("modT", l), "vecs"], [("G", l)])
            self.barrier()

    def mcol(self, l, which, kc, m):
        return self.modT[:, l, which * 16 + kc, m:m + 1]

    def layer0(self):
        S, nc, I = self.S, self.nc, self.I
        with contextlib.ExitStack() as es:
            AB = self.sb("AB", [128, 34, 4, 256], BF16, es)
            with contextlib.ExitStack() as es1:
                xin = self.sb("xin", [128, 4, D], F32, es1)
                xTb = self.sb("xTb", [128, 16, 512], F32, es1)
                sq = self.sb("sq", [128, 16, 512], BF16, es1)
                tmp = self.sb("tmp", [128, 2, 512], F32, es1)
                hTb = self.sb("hTb", [128, 16, 512], BF16, es1)
                uTb = self.sb("uTb", [128, 4, 512], BF16, es1)
                r1 = self.sb("r1", [128, 512], F32, es1)
                rstd = self.sb("rstd", [128, 512], F32, es1)
                Wu = self.sb("Wu", [128, 16, 512], BF16, es1)
                csc = self.sb("csc_sb", [128, 256], BF16, es1)
                DMA(S, "pool", Wu[:], I["w_in_ab"][:, 0:512].rearrange("(k p) n -> p k n", p=128), [], ["Wu"])
                DMA(S, "sp", csc[:], I["csc"], [], ["csc"])
                blocks = []
                role = self.role
                x_o, x_p = (I["x_own"], I["x_par"]) if role == 0 else (I["x_par"], I["x_own"])
                for j in range(4):
                    blocks.append(("own", x_o, 512 * j, 512, 512 * j, 4 * j, 0))
                blocks.append(("ctx", I["ctxb"], 0, 256, HALF, 32, 1))
                for j in range(4):
                    blocks.append(("par", x_p, 512 * j, 512, None, 16 + 4 * j, 0))
                blocks.append(("halo", I["x_halo"] if role == 0 else I["x_halo2"], 0, 512, NTOK, None, 0))
                for bi, (kind, src, t0, nt, lc, ab0, m) in enumerate(blocks):
                    ntile = nt // 128
                    DMA(S, "sp", xin[:, 0:ntile, :], src[t0:t0 + nt, :].rearrange("(t p) f -> p t f", p=128), [], ["xin"])
                    for kc in range(16):
                        ps, pk = self.bank(kc % 2)
                        for t in range(ntile):
                            TR(S, ps[:, t * 128:(t + 1) * 128], xin[:, t, kc * 128:(kc + 1) * 128], self.ident_f[:], ["xin", "ident_f"], [pk])
                        COPY(S, "act" if kc % 2 == 0 else "dve", xTb[:, kc, 0:nt], ps[:, 0:nt], [pk], [("xTb", kc)])
                    xk = "xTb_all"
                    allk = [("xTb", kc) for kc in range(16)]
                    if kind == "own" or (kind == "ctx" and self.role == 0):
                        DMA(S, "sp", self.xTc[:, lc:lc + nt].rearrange("(k p) t -> p k t", p=128), xTb[:, :, 0:nt], allk,
                            [(self.xk, kc, lc // 512) for kc in range(16)])

                    def out_fn(kc):
                        return hTb[:, kc, 0:nt], ("hTb", kc)
                    self._norm_multi(xTb, allk, nt, 0, 0, m, sq, tmp, rstd, r1, out_fn, 2)
                    hk = [("hTb", kc) for kc in range(16)]
                    if lc is not None:
                        DMA(S, "sp", self.hT[:, lc:lc + nt].rearrange("(k p) t -> p k t", p=128), hTb[:, :, 0:nt], hk, [("hT", lc // 512)])
                    if ab0 is None:
                        continue
                    for g in range(4):
                        ps, pk = self.bank(3 + (g % 2))
                        for kc in range(16):
                            MM(S, ps[:, 0:nt], Wu[:, kc, g * 128:(g + 1) * 128], hTb[:, kc, 0:nt], kc == 0, kc == 15, ["Wu", ("hTb", kc)], [pk])
                        COPY(S, "dve", uTb[:, g, 0:nt], ps[:, 0:nt], [pk], [("uTb", g)])
                    for t in range(ntile):
                        pA = self.pp[3 if t % 2 == 0 else 2]
                        pAk = ("ps", 6 if t % 2 == 0 else 4)
                        pAk2 = ("ps", 7 if t % 2 == 0 else 5)
                        for g in range(4):
                            MM(S, pA[:, g * 256:(g + 1) * 256], uTb[:, g, t * 128:(t + 1) * 128], csc[:], True, True,
                               [("uTb", g), "csc"], [pAk, pAk2])
                        COPY(S, "act", AB[:, ab0 + t, :, :].rearrange("p g c -> p (g c)"), pA[:, :], [pAk, pAk2], [("AB", ab0 + t)])
            self.barrier()
            with contextlib.ExitStack() as es1:
                cs = [self.sb(f"csn{i}", [128, 32, 2, 512], BF16, es1) for i in range(1)]
                Wf = self.sb("Wf", [128, 4, 128], BF16, es1)
                yT = [self.sb(f"yT{i}", [128, 512], BF16, es1) for i in range(2)]
                zst = [self.sb(f"zst{i}", [128, 512], BF16, es1) for i in range(2)]
                csx = self.sb("csx_sb", [128, 2, 2, 256], BF16, es1)
                DMA(S, "pool", Wf[:], I["w_four"].rearrange("g c d -> c g d"), [], ["Wf"])
                for t in range(2):
                    DMA(S, "sp", csx[:, :, t, :], I["csx"][:, t, :].rearrange("(k p) n -> p k n", p=128), [], ["csx"])
                n = 0
                csn_in = I["csn"] if self.role == 0 else I["csn2"]
                for ob in range(5 if self.role == 0 else 4):
                    if ob < 4:
                        nt, nk, col0 = 512, 32, 512 * ob
                        cst = cs[0]
                        ck = "csn0"
                        for t in range(2):
                            DMA(S, "sp", cst[:, :, t, :], csn_in[:, t, col0:col0 + 512].rearrange("(k p) n -> p k n", p=128), [], [ck])
                        ab_base = 0

                        def rhs_of(kc, t):
                            return cst[:, kc, t, :]
                    else:
                        nt, nk, col0 = 256, 2, HALF
                        ck = "csx"
                        ab_base = 32

                        def rhs_of(kc, t):
                            return csx[:, kc, t, :]
                    for g in range(4):
                        ps, pk = self.bank(n % 2)
                        for kc in range(nk):
                            for t in range(2):
                                MM(S, ps[:, 0:nt], AB[:, ab_base + kc, g, t * 128:(t + 1) * 128], rhs_of(kc, t),
                                   kc == 0 and t == 0, kc == nk - 1 and t == 1, [("AB", ab_base + kc), ck], [pk])
                        y = yT[n % 2]
                        COPY(S, "act", y[:, 0:nt], ps[:, 0:nt], [pk], [("yT", n % 2)])
                        ps2, pk2 = self.bank(2 + n % 2)
                        MM(S, ps2[:, 0:nt], Wf[:, g, :], y[:, 0:nt], True, True, ["Wf", ("yT", n % 2)], [pk2])
                        z = zst[n % 2]
                        COPY(S, "dve", z[:, 0:nt], ps2[:, 0:nt], [pk2], [("zst", n % 2)])
                        DMA(S, "sp", self.mixT[g * 128:(g + 1) * 128, col0:col0 + nt], z[:, 0:nt], [("zst", n % 2)], [("mixT", g, ob)])
                        n += 1
        self.barrier()
        if self.stage < 2:
            return
        self.l0_qkv()
        self.barrier()
        self.l0_attn()
        self.barrier()
        if self.stage < 3:
            return
        nt0 = NTOK if self.role == 0 else HALF
        self.outproj(0, self.I["w_out_ab"], nt0)
        self.barrier()
        self.ffn(0, nt0)
        self.barrier()

    def _norm_multi(self, xTb, xkeys, nt, l, nrm, m, sq, tmp, rstd, r1, out_fn, bank_i):
        S = self.S
        ACT(S, sq[:, :, 0:nt], xTb[:, :, 0:nt], AF.Square, xkeys, ["sq"])
        ps, pk = self.bank(bank_i)
        for kc in range(16):
            MM(S, ps[:, 0:nt], self.ones_b[:], sq[:, kc, 0:nt], kc == 0, kc == 15, ["ones_b", "sq"], [pk])
        ACT(S, r1[:, 0:nt], ps[:, 0:nt], AF.Sqrt, [pk, "epsb"], ["r1"], bias=self.epsb[:], scale=1.0 / D)
        RECIP(S, rstd[:, 0:nt], r1[:, 0:nt], ["r1"], ["rstd"])
        for kc in range(16):
            gap = self.G[:, l, nrm, m, kc:kc + 1]
            STT(S, "dve", tmp[:, kc % 2, 0:nt], xTb[:, kc, 0:nt], gap, rstd[:, 0:nt], ALU.mult, ALU.mult,
                list(xkeys) + [("G", l), "rstd"], [("tmp", kc % 2)])
            o, ok = out_fn(kc)
            ACT(S, o, tmp[:, kc % 2, 0:nt], AF.Identity, [("tmp", kc % 2), ("modT", l)], [ok], bias=self.mcol(l, 3 * nrm, kc, m), scale=1.0)

    def l0_qkv(self):
        S, I = self.S, self.I
        with contextlib.ExitStack() as es:
            hTr = self.sb("hTr", [128, 16, NKV], BF16, es)
            wts = [self.sb(f"wqkv{i}", [128, 16, 512], BF16, es) for i in range(2)]
            qst = [self.sb(f"qst{i}", [128, NKV], BF16, es) for i in range(2)]
            vst = [self.sb(f"vst{i}", [128, 22, 512], BF16, es) for i in range(1)]
            tbs = [(0, 512, 0), (512, 512, 1), (1024, 512, 2), (1536, 512, 3), (2048, 256, 4), (2304, 512, 5)]
            for (c0, nt, b) in tbs:
                DMA(S, "sp", hTr[:, :, c0:c0 + nt], self.hT[:, c0:c0 + nt].rearrange("(k p) t -> p k t", p=128), [("hT", 4 if b == 5 else b)] if False else [("hT", c0 // 512)], [("hTr", b)])
            n = 0
            e = 0
            for part, dst, ntb in (("q", self.qT, 5 if self.role == 0 else 4), ("k", self.kT, 6)):
                cbase = 512 if part == "q" else 2048
                for cg in range(3):
                    wt, wk = wts[n % 2], ("wqkv", n % 2)
                    DMA(S, "pool", wt[:], I["w_in_ab"][:, cbase + cg * 512:cbase + (cg + 1) * 512].rearrange("(k p) n -> p k n", p=128), [], [wk])
                    n += 1
                    for j in range(4):
                        hd = cg * 4 + j
                        st, sk = qst[hd % 2], ("qst", hd % 2)
                        for (c0, nt, b) in tbs[:ntb]:
                            ps, pk = self.bank(e % 4)
                            for kc in range(16):
                                MM(S, ps[:, 0:nt], wt[:, kc, j * 128:(j + 1) * 128], hTr[:, kc, c0:c0 + nt], kc == 0, kc == 15, [wk, ("hTr", b)], [pk])
                            COPY(S, "act" if e % 2 == 0 else "dve", st[:, c0:c0 + nt], ps[:, 0:nt], [pk], [sk])
                            e += 1
                        ncol = (NTOK if self.role == 0 else HALF) if part == "q" else NKV
                        DMA(S, "sp", dst[hd * 128:(hd + 1) * 128, 0:ncol], st[:, 0:ncol], [sk], [(part + "T", hd)])
            for cg in range(3):
                wt, wk = wts[n % 2], ("wqkv", n % 2)
                DMA(S, "pool", wt[:], I["w_in_ab"][:, 3584 + cg * 512:3584 + (cg + 1) * 512].rearrange("(k p) n -> p k n", p=128), [], [wk])
                n += 1
                v = vst[0]
                for t in range(22):
                    ps, pk = self.bank(4 + e % 4)
                    hb = t // 4 if t < 16 else (4 if t < 18 else 5)
                    for kc in range(16):
                        MM(S, ps[:, :], hTr[:, kc, t * 128:(t + 1) * 128], wt[:, kc, :], kc == 0, kc == 15, [wk, ("hTr", hb)], [pk])
                    COPY(S, "act" if e % 2 == 0 else "dve", v[:, t, :], ps[:, :], [pk], ["vst"])
                    e += 1
                DMA(S, "sp", self.vS[:, cg * 512:(cg + 1) * 512].rearrange("(t p) c -> p t c", p=128), v[:], ["vst"], [("vS", cg)])

    @staticmethod
    def _loc(lrow):
        if lrow < 0:
            return NTOK + (lrow + 4) * 64
        if lrow >= 32:
            return NTOK + 256 + (lrow - 32) * 64
        return lrow * 64

    def l0_attn(self):
        S, I = self.S, self.I
        scale = 128.0 ** -0.5
        with contextlib.ExitStack() as es:
            kwin = [self.sb(f"kwin{i}", [128, 12, 1024], BF16, es) for i in range(2)]
            vwin = [self.sb(f"vwin{i}", [128, 8, 1536], BF16, es) for i in range(2)]
            bia = [self.sb(f"bia{i}", [128, 12, 768], F32, es) for i in range(2)]
            qw = [self.sb(f"qw{i}", [128, 12, 128], BF16, es) for i in range(2)]
            ost = [self.sb(f"ost{i}", [128, 12, 128], BF16, es) for i in range(2)]
            sbs = [self.sb(f"sbs{i}", [128, 768], F32, es) for i in range(2)]
            pT = [self.sb(f"pT{i}", [128, 1024], BF16, es) for i in range(2)]
            rd = [self.sb(f"rd{i}", [128, 128], F32, es) for i in range(2)]
            allq = [("qT", h) for h in range(12)]
            allk = [("kT", h) for h in range(12)]
            allv = [("vS", c) for c in range(3)]
            n = 0
            nab = I["nabias"] if self.role == 0 else I["nabias2"]
            for ti in range(18 if self.role == 0 else 16):
                i2 = ti % 2
                isctx = ti >= 16
                kw_, vw_, bi_, qw_, os_ = kwin[i2], vwin[i2], bia[i2], qw[i2], ost[i2]
                kk, vk, bk, qk, ok = ("kwin", i2), ("vwin", i2), ("bia", i2), ("qw", i2), ("ost", i2)
                qc0 = ti * 128
                if not isctx:
                    w0, nrow = _na_win(ti)
                    chunks = []
                    for c in range((nrow + 1) // 2):
                        chunks.append((self._loc(w0 + 2 * c), 128 if 2 * c + 1 < nrow else 64))
                    nl = len(chunks)
                    chunks += [(HALF, 128), (HALF + 128, 128)]
                    DMA(S, "sp", bi_[:, :, 0:nl * 128].rearrange("p h (c q) -> p h c q", q=128),
                        nab[_na_cls(ti)][:, :, 0:nl, :].rearrange("h p c q -> p h c q"), [], [bk])
                else:
                    chunks = [(HALF, 128), (HALF + 128, 128)]
                nch = len(chunks)
                for c, (l0, nk) in enumerate(chunks):
                    DMA(S, "sp", kw_[:, :, c * 128:c * 128 + nk], self.kT[:, l0:l0 + nk].rearrange("(h p) t -> p h t", p=128), allk, [kk])
                    DMA(S, "sp", vw_[0:nk, c, :], self.vS[l0:l0 + nk, :], allv, [vk])
                DMA(S, "sp", qw_[:], self.qT[:, qc0:qc0 + 128].rearrange("(h p) t -> p h t", p=128), allq, [qk])
                nloc = nch - 2
                for h in range(12):
                    j2 = n % 2
                    pS = self.pp[j2]
                    pSk = [("ps", 2 * j2), ("ps", 2 * j2 + 1)]
                    for c, (l0, nk) in enumerate(chunks):
                        MM(S, pS[0:nk, c * 128:(c + 1) * 128], kw_[:, h, c * 128:c * 128 + nk], qw_[:, h, :], True, True, [kk, qk], pSk)
                    p_ = pT[j2]
                    pk_ = ("pT", j2)
                    if nloc > 0:
                        sb_ = sbs[j2]
                        nlc = nloc * 128
                        STT(S, "dve", sb_[:, 0:nlc], pS[:, 0:nlc], scale, bi_[:, h, 0:nlc], ALU.mult, ALU.add, pSk + [bk], [("sbs", j2)])
                        ACT(S, p_[:, 0:nlc], sb_[:, 0:nlc], AF.Exp, [("sbs", j2)], [pk_])
                    ACT(S, p_[:, nloc * 128:nch * 128], pS[:, nloc * 128:nch * 128], AF.Exp, pSk, [pk_], scale=scale)
                    pO, pOk = self.bank(4 + 2 * j2)
                    pD, pDk = self.bank(5 + 2 * j2)
                    for c, (l0, nk) in enumerate(chunks):
                        MM(S, pO[:, 0:128], vw_[0:nk, c, h * 128:(h + 1) * 128], p_[0:nk, c * 128:(c + 1) * 128], c == 0, c == nch - 1, [vk, pk_], [pOk])
                    for c, (l0, nk) in enumerate(chunks):
                        MM(S, pD[:, 0:128], self.ones_b[0:nk, :], p_[0:nk, c * 128:(c + 1) * 128], c == 0, c == nch - 1, ["ones_b", pk_], [pDk])
                    RECIP(S, rd[j2][:, :], pD[:, 0:128], [pDk], [("rd", j2)])
                    TT(S, "dve", os_[:, h, :], pO[:, 0:128], rd[j2][:, :], ALU.mult, [pOk, ("rd", j2)], [ok])
                    n += 1
                DMA(S, "sp", self.mixT[512:2048, qc0:qc0 + 128].rearrange("(h p) t -> p h t", p=128), os_[:], [ok],
                    [("mixT", 4 + h, ti // 4) for h in range(12)])

    def l1_part1(self, es_l1):
        S, I = self.S, self.I
        role = self.role
        lite = role == 1
        if not lite:
            self.cqn = self.sb("cqn", [128, 6, HALF], BF16, es_l1)
            self.upool = self.sb("upool", [128, 18, 512], BF16, es_l1)
        exb = self.exb if not lite else self.exb2
        exk = "exb" if not lite else "exb2"
        nblk = 5 if not lite else 4
        W = I["w_in_cd"]
        with contextlib.ExitStack() as es:
            h1 = self.sb("h1r", [128, 16, NTOK], BF16, es)
            with contextlib.ExitStack() as es2:
                xTb = self.sb("n_xTb", [128, 16, 512], F32, es2)
                sq = self.sb("n_sq", [128, 16, 512], BF16, es2)
                tmp = self.sb("n_tmp", [128, 2, 512], F32, es2)
                r1 = self.sb("n_r1", [128, 512], F32, es2)
                rstd = self.sb("n_rstd", [128, 512], F32, es2)
                for b in range(nblk):
                    c0 = 512 * b
                    nt = min(512, NTOK - c0)
                    m = 1 if c0 >= HALF else 0
                    DMA(S, "sp", xTb[:, :, 0:nt], self.xTc[:, c0:c0 + nt].rearrange("(k p) t -> p k t", p=128),
                        [(self.xk, k, b) for k in range(16)], ["n_xTb"])

                    def out_fn(kc, c0=c0, nt=nt, b=b):
                        return h1[:, kc, c0:c0 + nt], ("h1r", b)
                    self._norm_multi(xTb, ["n_xTb"], nt, 1, 0, m, sq, tmp, rstd, r1, out_fn, 7)
            self.barrier()
            wts = [self.sb(f"w1_{i}", [128, 16, 512], BF16, es) for i in range(2)]
            stg = [self.sb(f"stg{i}", [128, NTOK], F32, es) for i in range(2)]
            wv = lambda a, b_: W[:, a:b_].rearrange("(k p) n -> p k n", p=128)
            DMA(S, "pool", wts[0][:], wv(0, 512), [], [("w1", 0)])
            e = 0
            for t in (range(16) if not lite else (0, 15)):
                ps, pk = self.bank(4 + e % 4)
                for kc in range(16):
                    MM(S, ps[:, :], h1[:, kc, t * 128:(t + 1) * 128], wts[0][:, kc, :], kc == 0, kc == 15, [("w1", 0), ("h1r", t // 4)], [pk])
                ui = 1 + t if not lite else (17 if t == 0 else 0)
                COPY(S, "act" if e % 2 == 0 else "dve", self.upool[:, ui, :], ps[:, :], [pk], [("upool", ui)])
                e += 1
            jobs = []
            for j in range(4):
                jobs.append((1, j * 128, 128, j * 128, 4))
            jobs += [(2, 0, 128, 512, 4), (2, 128, 128, 640, 4), (2, 256, 128, 768, 5), (2, 384, 128, 896, 5)]
            jobs += [(3, 0, 128, 1024, 5), (3, 128, 128, 1152, 5), (3, 256, 64, 1280, 5), (3, 320, 64, 1344, 5)]
            loaded = {}
            n = 0
            for (ti, co, M, r0, ntb) in jobs:
                if lite and r0 < 768:
                    continue
                ntb = min(ntb, nblk)
                if ti not in loaded:
                    w_, wk = wts[ti % 2], ("w1", ti % 2)
                    if ti == 1:
                        DMA(S, "pool", w_[:], wv(512, 1024), [], [wk])
                    elif ti == 2:
                        DMA(S, "pool", w_[:], wv(1024, 1536), [], [wk])
                    else:
                        DMA(S, "pool", w_[:, :, 0:320], wv(1536, 1856), [], [wk])
                        for q4 in range(4):
                            src0 = 1792 + ROPE_PERM[q4 * 16]
                            DMA(S, "pool", w_[:, :, 320 + q4 * 16:336 + q4 * 16], wv(src0, src0 + 16), [], [wk])
                    loaded[ti] = True
                w_, wk = wts[ti % 2], ("w1", ti % 2)
                st, sk = stg[n % 2], ("stg", n % 2)
                n += 1
                for b in range(ntb):
                    c0 = 512 * b
                    nt = min(512, NTOK - c0)
                    ps, pk = self.bank(e % 4)
                    for kc in range(16):
                        MM(S, ps[0:M, 0:nt], w_[:, kc, co:co + M], h1[:, kc, c0:c0 + nt], kc == 0, kc == 15, [wk, ("h1r", b)], [pk])
                    COPY(S, "act" if e % 2 == 0 else "dve", st[0:M, c0:c0 + nt], ps[0:M, 0:nt], [pk], [sk])
                    e += 1
                ncol = HALF if ntb == 4 else NTOK
                DMA(S, "sp", self.l1raw[r0:r0 + M, 0:ncol], st[0:M, 0:ncol], [sk], [("l1raw", r0 // 128)])
        self.barrier()
        with contextlib.ExitStack() as es:
            cqb = self.sb("cqb", [128, 6, 512], F32, es)
            ckb = self.sb("ckb", [128, 4, 512], F32, es)
            krb = self.sb("krb", [64, 2, 512], F32, es)
            sq6 = self.sb("sq6", [128, 6, 512], BF16, es)
            r1 = self.sb("p_r1", [128, 512], F32, es)
            rstd = self.sb("p_rstd", [128, 512], F32, es)
            ckn = self.sb("ckn", [128, 4, 512], BF16, es)
            kro = self.sb("kro", [64, 512], BF16, es)
            t1 = self.sb("rt1", [64, 512], F32, es)
            t2 = self.sb("rt2", [64, 512], F32, es)
            rope = self.sb("rope_sb", [64, 2, HALF], F32, es)
            DMA(S, "sp", rope[:], (I["rope"] if not lite else I["rope2"]).rearrange("a p t -> p a t"), [], ["rope"])
            rawk = [("l1raw", i) for i in range(11)]
            for b in range(nblk):
                c0 = 512 * b
                nt = min(512, NTOK - c0)
                for (nch, buf, bk, r0, dim, gcol) in ((6, cqb, "cqb", 0, 768, self.V_GQ), (4, ckb, "ckb", 768, 512, self.V_GKV)):
                    if nch == 6 and (b == 4 or lite):
                        continue
                    DMA(S, "sp", buf[:, :, 0:nt], self.l1raw[r0:r0 + nch * 128, c0:c0 + nt].rearrange("(k p) t -> p k t", p=128), rawk, [bk])
                    ACT(S, sq6[:, 0:nch, 0:nt], buf[:, :, 0:nt], AF.Square, [bk], ["sq6"])
                    ps, pk = self.bank(6)
                    for kc in range(nch):
                        MM(S, ps[:, 0:nt], self.ones_b[:], sq6[:, kc, 0:nt], kc == 0, kc == nch - 1, ["ones_b", "sq6"], [pk])
                    ACT(S, r1[:, 0:nt], ps[:, 0:nt], AF.Sqrt, [pk, "epsb"], ["p_r1"], bias=self.epsb[:], scale=1.0 / dim)
                    RECIP(S, rstd[:, 0:nt], r1[:, 0:nt], ["p_r1"], ["p_rstd"])
                    for kc in range(nch):
                        if nch == 6:
                            o, ok = self.cqn[:, kc, c0:c0 + nt], ("cqn", b)
                        else:
                            o, ok = ckn[:, kc, 0:nt], "ckn"
                        STT(S, "dve", o, buf[:, kc, 0:nt], self.vecs[:, gcol + kc:gcol + kc + 1], rstd[:, 0:nt], ALU.mult, ALU.mult,
                            [bk, "vecs", "p_rstd"], [ok])
                    if nch == 4:
                        DMA(S, "sp", exb[0:512, c0:c0 + nt].rearrange("(k p) t -> p k t", p=128), ckn[:, :, 0:nt], ["ckn"], [(exk, b)])
                DMA(S, "sp", krb[:, :, 0:nt], self.l1raw[1280:1408, c0:c0 + nt].rearrange("(a p) t -> p a t", p=64), rawk, ["krb"])
                if b < 4:
                    TT(S, "dve", t1[:, 0:nt], krb[:, 0, 0:nt], rope[:, 0, c0:c0 + nt], ALU.mult, ["krb", "rope"], ["rt1"])
                    TT(S, "pool", t2[:, 0:nt], krb[:, 1, 0:nt], rope[:, 1, c0:c0 + nt], ALU.mult, ["krb", "rope"], ["rt2"])
                    TT(S, "dve", kro[:, 0:nt], t1[:, 0:nt], t2[:, 0:nt], ALU.add, ["rt1", "rt2"], ["kro"])
                else:
                    COPY(S, "dve", kro[:, 0:nt], krb[:, 0, 0:nt], ["krb"], ["kro"])
                DMA(S, "sp", exb[512:576, c0:c0 + nt], kro[:, 0:nt], ["kro"], [(exk, b)])
            if not lite and self.mode != "F":
                DMA(S, "sp", exb[576:704, 0:512], self.upool[:, 1, :], [("upool", 1)], [("exb", 5)])
                DMA(S, "sp", exb[576:704, 512:1024], self.upool[:, 16, :], [("upool", 16)], [("exb", 5)])
        self.barrier()

    def l1_part2(self, es_l1):
        S, I = self.S, self.I
        fused = self.mode == "F"
        if fused:
            G = [self.exb, self.exb2]
            gk = [[("exb", b) for b in range(4)], [("exb2", b) for b in range(4)]]
        else:
            G = [self.exg[0], self.exg[1]]
            gk = [[("exg", 0)], [("exg", 0)]]
        self.krr = self.sb("krr", [64, SEQ + CTX], BF16, es_l1)
        for r in range(2):
            DMA(S, "sp", self.krr[:, r * HALF:(r + 1) * HALF], G[r][512:576, 0:HALF], gk[r], ["krr"])
        DMA(S, "sp", self.krr[:, SEQ:SEQ + CTX], self.exb[512:576, HALF:NTOK], [("exb", 4)], ["krr"])
        if not fused:
            DMA(S, "sp", self.upool[:, 0, :], G[0][576:704, 512:1024], gk[0], [("upool", 0)])
            DMA(S, "sp", self.upool[:, 17, :], G[1][576:704, 0:512], gk[1], [("upool", 17)])
        with contextlib.ExitStack() as es:
            ckv = self.sb("ckv_all", [128, 4, SEQ + CTX], BF16, es)
            for r in range(2):
                DMA(S, "sp", ckv[:, :, r * HALF:(r + 1) * HALF], G[r][0:512, 0:HALF].rearrange("(k p) t -> p k t", p=128), gk[r], [("ckv", r)])
            DMA(S, "sp", ckv[:, :, SEQ:SEQ + CTX], self.exb[0:512, HALF:NTOK].rearrange("(k p) t -> p k t", p=128), [("exb", 4)], [("ckv", 2)])
            with contextlib.ExitStack() as es1:
                PT = self.sb("PT", [128, 3, 4, 3, 128], BF16, es1)
                Wp = self.sb("Wpool", [128, 4, 128], BF16, es1)
                pu = [self.sb(f"pu{i}", [128, 512], BF16, es1) for i in range(2)]
                zst = [self.sb(f"pz{i}", [128, 512], BF16, es1) for i in range(2)]
                DMA(S, "sp", PT[:].rearrange("p a g r t -> p (a g r t)"), I["ptab"].rearrange("p a g r t -> p (a g r t)"), [], ["PT"])
                DMA(S, "pool", Wp[:], I["w_pool"].rearrange("g c d -> c g d"), [], ["Wpool"])
                n = 0
                for tb in range(4):
                    for g in range(4):
                        ps, pk = self.bank(n % 2)
                        for tt in range(4):
                            ti = tb * 4 + tt
                            cls = 0 if ti == 0 else (2 if ti == 15 else 1)
                            for rel in range(3):
                                MM(S, ps[:, tt * 128:(tt + 1) * 128], self.upool[:, ti + rel, g * 128:(g + 1) * 128], PT[:, cls, g, rel, :],
                                   rel == 0, rel == 2, [("upool", ti + rel), "PT"], [pk])
                        p_, pk_ = pu[n % 2], ("pu", n % 2)
                        COPY(S, "act", p_[:, :], ps[:, :], [pk], [pk_])
                        ps2, pk2 = self.bank(2 + n % 2)
                        MM(S, ps2[:, :], Wp[:, g, :], p_[:, :], True, True, ["Wpool", pk_], [pk2])
                        z_, zk = zst[n % 2], ("pz", n % 2)
                        S.add("dve", lambda e, z_=z_, ps2=ps2, g=g: e.tensor_scalar(out=z_[:, :], in0=ps2[:, :], scalar1=self.vecs[:, self.V_PS + g:self.V_PS + g + 1],
                                                                             scalar2=None, op0=ALU.mult), [pk2, "vecs"], [zk])
                        DMA(S, "sp", self.mixT[g * 128:(g + 1) * 128, tb * 512:(tb + 1) * 512], z_[:, :], [zk], [("mixT", g, tb)])
                        n += 1
            self.barrier()
            with contextlib.ExitStack() as es1:
                Wq = self.sb("Wuq", [128, 6, 2304], BF16, es1)
                Wqs = self.sb("Wuqs", [128, 6, 12, 64], BF16, es1)
                rope = self.sb("rope_sb2", [64, 2, HALF], F32, es1)
                qnst = [self.sb(f"qnst{i}", [128, HALF], BF16, es1) for i in range(2)]
                qrst = [self.sb(f"qrst{i}", [64, HALF], BF16, es1) for i in range(2)]
                t1 = [self.sb(f"qt1_{i}", [64, 512], F32, es1) for i in range(2)]
                t2 = [self.sb(f"qt2_{i}", [64, 512], F32, es1) for i in range(2)]
                DMA(S, "sp", rope[:], I["rope"].rearrange("a p t -> p a t"), [], ["rope2"])
                DMA(S, "pool", Wq[:], I["w_uq"].rearrange("(k p) n -> p k n", p=128), [], ["Wuq"])
                wq4 = I["w_uq"].rearrange("(k p) (h c) -> k p h c", p=128, c=192)
                for k in range(6):
                    for q4 in range(4):
                        src0 = 128 + ROPE_PERM[q4 * 16]
                        DMA(S, "pool", Wqs[:, k, :, q4 * 16:q4 * 16 + 16], wq4[k, :, :, src0:src0 + 16], [], ["Wuqs"])
                e = 0
                for h in range(12):
                    qn_, qnk = qnst[h % 2], ("qnst", h % 2)
                    qr_, qrk = qrst[h % 2], ("qrst", h % 2)
                    for b in range(4):
                        c0 = 512 * b
                        ps, pk = self.bank(e % 2)
                        for kc in range(6):
                            MM(S, ps[:, :], Wq[:, kc, h * 192:h * 192 + 128], self.cqn[:, kc, c0:c0 + 512], kc == 0, kc == 5, ["Wuq", ("cqn", b)], [pk])
                        COPY(S, "act", qn_[:, c0:c0 + 512], ps[:, :], [pk], [qnk])
                        pr, prk = self.bank(2 + e % 2)
                        pw, pwk = self.bank(4 + e % 2)
                        for kc in range(6):
                            MM(S, pr[0:64, :], Wq[:, kc, h * 192 + 128:h * 192 + 192], self.cqn[:, kc, c0:c0 + 512], kc == 0, kc == 5, ["Wuq", ("cqn", b)], [prk])
                        for kc in range(6):
                            MM(S, pw[0:64, :], Wqs[:, kc, h, :], self.cqn[:, kc, c0:c0 + 512], kc == 0, kc == 5, ["Wuqs", ("cqn", b)], [pwk])
                        a1, a1k = t1[e % 2], ("qt1", e % 2)
                        a2, a2k = t2[e % 2], ("qt2", e % 2)
                        TT(S, "dve", a1[:, :], pr[0:64, :], rope[:, 0, c0:c0 + 512], ALU.mult, [prk, "rope2"], [a1k])
                        TT(S, "dve", a2[:, :], pw[0:64, :], rope[:, 1, c0:c0 + 512], ALU.mult, [pwk, "rope2"], [a2k])
                        TT(S, "pool", qr_[:, c0:c0 + 512], a1[:, :], a2[:, :], ALU.add, [a1k, a2k], [qrk])
                        e += 1
                    DMA(S, "sp", self.qn_s[h * 128:(h + 1) * 128, :], qn_[:, :], [qnk], [("qn_s", h)])
                    DMA(S, "sp", self.qr_s[h * 64:(h + 1) * 64, :], qr_[:, :], [qrk], [("qr_s", h)])
            self.barrier()
            with contextlib.ExitStack() as es1:
                NK = SEQ + CTX
                Wk = self.sb("Wukv", [128, 4, 3072], BF16, es1)
                knst = [self.sb(f"knst{i}", [128, NK], BF16, es1) for i in range(2)]
                vst = [self.sb(f"v1st{i}", [128, 1536], BF16, es1) for i in range(2)]
                DMA(S, "pool", Wk[:], I["w_ukv"].rearrange("(k p) n -> p k n", p=128), [], ["Wukv"])
                Wk4 = Wk[:].rearrange("p k (h t d) -> p k h t d", t=2, d=128)
                e = 0
                for h in range(12):
                    kn_, knk = knst[h % 2], ("knst", h % 2)
                    for b in range(9):
                        c0 = 512 * b
                        nt = min(512, NK - c0)
                        ps, pk = self.bank(e % 4)
                        for kc in range(4):
                            MM(S, ps[:, 0:nt], Wk[:, kc, h * 256:h * 256 + 128], ckv[:, kc, c0:c0 + nt], kc == 0, kc == 3, ["Wukv", ("ckv", min(c0 // HALF, 2))], [pk])
                        COPY(S, "act" if e % 2 == 0 else "dve", kn_[:, c0:c0 + nt], ps[:, 0:nt], [pk], [knk])
                        e += 1
                    DMA(S, "sp", self.kn_s[h * 128:(h + 1) * 128, :], kn_[:, :], [knk], [("kn_s", h)])
                for t in range(34):
                    v_, vk_ = vst[t % 2], ("v1st", t % 2)
                    for hg in range(3):
                        ps, pk = self.bank(4 + e % 4)
                        for kc in range(4):
                            MM(S, ps[:, :].rearrange("p (h d) -> p h d", d=128), ckv[:, kc, t * 128:(t + 1) * 128], Wk4[:, kc, hg * 4:hg * 4 + 4, 1, :],
                               kc == 0, kc == 3, ["Wukv", ("ckv", min(t // 16, 2))], [pk])
                        COPY(S, "act" if e % 2 == 0 else "dve", v_[:, hg * 512:(hg + 1) * 512], ps[:, :], [pk], [vk_])
                        e += 1
                    DMA(S, "sp", self.v1_s[t * 128:(t + 1) * 128, :], v_[:, :], [vk_], [("v1_s", t)])
        self.barrier()

    def l1_attn(self):
        S = self.S
        NK = SEQ + CTX
        scale = 192.0 ** -0.5
        with contextlib.ExitStack() as es:
            kn = [self.sb(f"a_kn{i}", [128, NK], BF16, es) for i in range(2)]
            vh = [self.sb(f"a_v{i}", [128, 34, 128], BF16, es) for i in range(2)]
            qn = [self.sb(f"a_qn{i}", [128, HALF], BF16, es) for i in range(2)]
            qr = [self.sb(f"a_qr{i}", [64, HALF], BF16, es) for i in range(2)]
            pT = [self.sb(f"a_pT{i}", [128, 512], BF16, es) for i in range(3)]
            rd = [self.sb(f"a_rd{i}", [128, 512], F32, es) for i in range(2)]
            ost = [self.sb(f"a_o{i}", [128, 512], BF16, es) for i in range(2)]
            allv = [("v1_s", t) for t in range(34)]
            n = 0
            e = 0
            for h in range(12):
                i2 = h % 2
                DMA(S, "sp", kn[i2][:, :], self.kn_s[h * 128:(h + 1) * 128, :], [("kn_s", h)], [("a_kn", i2)])
                DMA(S, "sp", vh[i2][:, :, :], self.v1_s[:, h * 128:(h + 1) * 128].rearrange("(t p) d -> p t d", p=128), allv, [("a_v", i2)])
                DMA(S, "sp", qn[i2][:, :], self.qn_s[h * 128:(h + 1) * 128, :], [("qn_s", h)], [("a_qn", i2)])
                DMA(S, "sp", qr[i2][:, :], self.qr_s[h * 64:(h + 1) * 64, :], [("qr_s", h)], [("a_qr", i2)])
                for qb in range(4):
                    c0 = 512 * qb
                    pO, pOk = self.bank(4 + 2 * (n % 2))
                    pD, pDk = self.bank(5 + 2 * (n % 2))
                    for kc in range(34):
                        pS, pSk = self.bank(e % 4)
                        MM(S, pS[:, :], kn[i2][:, kc * 128:(kc + 1) * 128], qn[i2][:, c0:c0 + 512], True, False, [("a_kn", i2), ("a_qn", i2)], [pSk])
                        MM(S, pS[:, :], self.krr[:, kc * 128:(kc + 1) * 128], qr[i2][:, c0:c0 + 512], False, True, ["krr", ("a_qr", i2)], [pSk])
                        p_, pk_ = pT[e % 3], ("a_pT", e % 3)
                        ACT(S, p_[:, :], pS[:, :], AF.Exp, [pSk], [pk_], scale=scale)
                        MM(S, pO[:, :], vh[i2][:, kc, :], p_[:, :], kc == 0, kc == 33, [("a_v", i2), pk_], [pOk])
                        MM(S, pD[:, :], self.ones_b[:], p_[:, :], kc == 0, kc == 33, ["ones_b", pk_], [pDk])
                        e += 1
                    RECIP(S, rd[n % 2][:, :], pD[:, :], [pDk], [("a_rd", n % 2)])
                    TT(S, "dve", ost[n % 2][:, :], pO[:, :], rd[n % 2][:, :], ALU.mult, [pOk, ("a_rd", n % 2)], [("a_o", n % 2)])
                    DMA(S, "sp", self.mixT[512 + h * 128:640 + h * 128, c0:c0 + 512], ost[n % 2][:, :], [("a_o", n % 2)], [("mixT", 4 + h, qb)])
                    n += 1

    def final_norm(self):
        S = self.S
        with contextlib.ExitStack() as es:
            xTb = self.sb("z_xTb", [128, 16, 512], F32, es)
            sq = self.sb("z_sq", [128, 16, 512], BF16, es)
            yT = self.sb("z_yT", [128, 16, 512], F32, es)
            r1 = self.sb("z_r1", [128, 512], F32, es)
            rstd = self.sb("z_rstd", [128, 512], F32, es)
            ot = [self.sb(f"z_ot{i}", [128, D], F32, es) for i in range(2)]
            e = 0
            n = 0
            for b in range(4):
                c0 = 512 * b
                DMA(S, "sp", xTb[:], self.xTc[:, c0:c0 + 512].rearrange("(k p) t -> p k t", p=128), [(self.xk, k, b) for k in range(16)], ["z_xTb"])
                ACT(S, sq[:], xTb[:], AF.Square, ["z_xTb"], ["z_sq"])
                ps, pk = self.bank(7)
                for kc in range(16):
                    MM(S, ps[:, :], self.ones_b[:], sq[:, kc, :], kc == 0, kc == 15, ["ones_b", "z_sq"], [pk])
                ACT(S, r1[:], ps[:, :], AF.Sqrt, [pk, "epsb"], ["z_r1"], bias=self.epsb[:], scale=1.0 / D)
                RECIP(S, rstd[:], r1[:], ["z_r1"], ["z_rstd"])
                for kc in range(16):
                    STT(S, "dve", yT[:, kc, :], xTb[:, kc, :], self.vecs[:, self.V_FG + kc:self.V_FG + kc + 1], rstd[:], ALU.mult, ALU.mult,
                        ["z_xTb", "vecs", "z_rstd"], [("z_yT", kc)])
                for t in range(4):
                    o_, ok_ = ot[n % 2], ("z_ot", n % 2)
                    n += 1
                    for k4 in range(4):
                        pt, ptk = self.bank(e % 4)
                        for j in range(4):
                            kc = k4 * 4 + j
                            TR(S, pt[:, j * 128:(j + 1) * 128], yT[:, kc, t * 128:(t + 1) * 128], self.ident_f[:], [("z_yT", kc), "ident_f"], [ptk])
                        COPY(S, "act" if e % 2 == 0 else "dve", o_[:, k4 * 512:(k4 + 1) * 512], pt[:, :], [ptk], [ok_])
                        e += 1
                    DMA(S, "sp", self.out[c0 + t * 128:c0 + (t + 1) * 128, :], o_[:, :], [ok_], [("out", b, t)])

    def outproj(self, l, W, ntok):
        S = self.S
        nb = (ntok + 511) // 512
        with contextlib.ExitStack() as es:
            mixr = self.sb("mixr", [128, 16, NTOK], BF16, es)
            wts = [self.sb(f"wo{i}", [128, 16, 512], BF16, es) for i in range(2)]
            xres = [self.sb(f"xres{i}", [128, NTOK], F32, es) for i in range(2)]
            for b in range(nb):
                c0 = 512 * b
                nt = min(512, ntok - c0)
                DMA(S, "sp", mixr[:, :, c0:c0 + nt], self.mixT[:, c0:c0 + nt].rearrange("(k p) t -> p k t", p=128),
                    [("mixT", k, b) for k in range(16)], [("mixr", b)])
            e = 0
            for cg in range(4):
                wt, wk = wts[cg % 2], ("wo", cg % 2)
                DMA(S, "pool", wt[:], W[:, cg * 512:(cg + 1) * 512].rearrange("(k p) n -> p k n", p=128), [], [wk])
                for j in range(4):
                    dc = cg * 4 + j
                    xr, xk = xres[dc % 2], ("xres", dc % 2)
                    DMA(S, "sp", xr[:, 0:ntok], self.xTc[dc * 128:(dc + 1) * 128, 0:ntok], [(self.xk, dc, b) for b in range(nb)], [xk])
                    for b in range(nb):
                        c0 = 512 * b
                        nt = min(512, ntok - c0)
                        m = 1 if c0 >= HALF else 0
                        ps, pk = self.bank(e % 4)
                        for kc in range(16):
                            MM(S, ps[:, 0:nt], wt[:, kc, j * 128:(j + 1) * 128], mixr[:, kc, c0:c0 + nt], kc == 0, kc == 15, [wk, ("mixr", b)], [pk])
                        STT(S, "dve", xr[:, c0:c0 + nt], ps[:, 0:nt], self.mcol(l, 2, dc, m), xr[:, c0:c0 + nt], ALU.mult, ALU.add,
                            [pk, xk, ("modT", l)], [xk])
                        e += 1
                    DMA(S, "sp", self.xTc[dc * 128:(dc + 1) * 128, 0:ntok], xr[:, 0:ntok], [xk], [(self.xk, dc, b) for b in range(nb)])

    def ffn(self, l, ntok):
        S, I = self.S, self.I
        nb = (ntok + 511) // 512
        with contextlib.ExitStack() as es:
            with contextlib.ExitStack() as es1:
                h2 = self.sb("h2r", [128, 16, NTOK], BF16, es1)
                with contextlib.ExitStack() as es2:
                    xTb = self.sb("f_xTb", [128, 16, 512], F32, es2)
                    sq = self.sb("f_sq", [128, 16, 512], BF16, es2)
                    tmp = self.sb("f_tmp", [128, 2, 512], F32, es2)
                    r1 = self.sb("f_r1", [128, 512], F32, es2)
                    rstd = self.sb("f_rstd", [128, 512], F32, es2)
                    for b in range(nb):
                        c0 = 512 * b
                        nt = min(512, ntok - c0)
                        m = 1 if c0 >= HALF else 0
                        DMA(S, "sp", xTb[:, :, 0:nt], self.xTc[:, c0:c0 + nt].rearrange("(k p) t -> p k t", p=128),
                            [(self.xk, k, b) for k in range(16)], ["f_xTb"])

                        def out_fn(kc, c0=c0, nt=nt, b=b):
                            return h2[:, kc, c0:c0 + nt], ("h2r", b)
                        self._norm_multi(xTb, ["f_xTb"], nt, l, 1, m, sq, tmp, rstd, r1, out_fn, 7)
                    self.barrier()
                wg = [self.sb(f"wg{i}", [128, 16, 512], BF16, es1) for i in range(2)]
                wu = [self.sb(f"wu{i}", [128, 16, 512], BF16, es1) for i in range(2)]
                sg = [self.sb(f"sg{i}", [128, 512], F32, es1) for i in range(2)]
                ast = [self.sb(f"ast{i}", [128, NTOK], BF16, es1) for i in range(2)]
                e = 0
                for cg in range(11):
                    g_, gk = wg[cg % 2], ("wg", cg % 2)
                    u_, uk = wu[cg % 2], ("wu", cg % 2)
                    DMA(S, "pool", g_[:], I["w_ffn_gate"][l, :, cg * 512:(cg + 1) * 512].rearrange("(k p) n -> p k n", p=128), [], [gk])
                    DMA(S, "pool", u_[:], I["w_ffn_up"][l, :, cg * 512:(cg + 1) * 512].rearrange("(k p) n -> p k n", p=128), [], [uk])
                    for j in range(4):
                        fc = cg * 4 + j
                        a_, ak = ast[fc % 2], ("ast", fc % 2)
                        for b in range(nb):
                            c0 = 512 * b
                            nt = min(512, ntok - c0)
                            pg, pgk = self.bank(2 * (e % 4))
                            pu, puk = self.bank(2 * (e % 4) + 1)
                            for kc in range(16):
                                MM(S, pg[:, 0:nt], g_[:, kc, j * 128:(j + 1) * 128], h2[:, kc, c0:c0 + nt], kc == 0, kc == 15, [gk, ("h2r", b)], [pgk])
                            for kc in range(16):
                                MM(S, pu[:, 0:nt], u_[:, kc, j * 128:(j + 1) * 128], h2[:, kc, c0:c0 + nt], kc == 0, kc == 15, [uk, ("h2r", b)], [puk])
                            s_, sk = sg[e % 2], ("sg", e % 2)
                            ACT(S, s_[:, 0:nt], pg[:, 0:nt], AF.Silu, [pgk], [sk])
                            TT(S, "dve", a_[:, c0:c0 + nt], s_[:, 0:nt], pu[:, 0:nt], ALU.mult, [sk, puk], [ak])
                            e += 1
                        DMA(S, "sp", self.aT[fc * 128:(fc + 1) * 128, 0:ntok], a_[:, 0:ntok], [ak], [("aT", fc)])
            self.barrier()
            with contextlib.ExitStack() as es1:
                aTr = self.sb("aTr", [128, 44, 768], BF16, es1)
                wd = [self.sb(f"wd{i}", [128, 44, 256], BF16, es1) for i in range(2)]
                xres = [self.sb(f"dxres{i}", [128, 768], F32, es1) for i in range(2)]
                allA = [("aT", fc) for fc in range(44)]
                nsb = (ntok + 767) // 768
                n = 0
                e = 0
                for sbk in range(nsb):
                    c0 = 768 * sbk
                    nt = min(768, ntok - c0)
                    DMA(S, "sp", aTr[:, :, 0:nt], self.aT[:, c0:c0 + nt].rearrange("(k p) t -> p k t", p=128), allA, ["aTr"])
                    segs = [(0, min(512, nt))] + ([(512, nt - 512)] if nt > 512 else [])
                    blks = sorted(set([(c0 + a) // 512 for a, _ in segs] + [(c0 + a + w - 1) // 512 for a, w in segs]))
                    for cg in range(8):
                        w_, wk = wd[n % 2], ("wd", n % 2)
                        n += 1
                        DMA(S, "pool", w_[:], I["w_ffn_down"][l, :, cg * 256:(cg + 1) * 256].rearrange("(k p) n -> p k n", p=128), [], [wk])
                        for j in range(2):
                            dc = cg * 2 + j
                            xr, xk = xres[dc % 2], ("dxres", dc % 2)
                            xkeys = [(self.xk, dc, b) for b in blks]
                            DMA(S, "sp", xr[:, 0:nt], self.xTc[dc * 128:(dc + 1) * 128, c0:c0 + nt], xkeys, [xk])
                            pS = self.pp[e % 4]
                            pks = [("ps", 2 * (e % 4)), ("ps", 2 * (e % 4) + 1)]
                            for kc in range(44):
                                for (a, w) in segs:
                                    MM(S, pS[:, a:a + w], w_[:, kc, j * 128:(j + 1) * 128], aTr[:, kc, a:a + w], kc == 0, kc == 43, [wk, "aTr"],
                                       [pks[0] if a == 0 else pks[1]])
                            for (a, w) in segs:
                                m = 1 if c0 + a >= HALF else 0
                                STT(S, "dve", xr[:, a:a + w], pS[:, a:a + w], self.mcol(l, 5, dc, m), xr[:, a:a + w], ALU.mult, ALU.add,
                                    [pks[0] if a == 0 else pks[1], xk, ("modT", l)], [xk])
                            e += 1
                            DMA(S, "sp", self.xTc[dc * 128:(dc + 1) * 128, c0:c0 + nt], xr[:, 0:nt], [xk], xkeys)


def make_inputs(core, inp):
    b, s = core // 2, core % 2
    f = np.float32
    vecs = np.zeros((128, 400), f)
    c2 = np.stack([_fm(inp["c"][b]), _fm(inp["c_ctx"])], axis=-1)
    vecs[:, 0:32] = c2.reshape(128, 32)
    for l in range(2):
        vecs[:, 32 + 16 * l:48 + 16 * l] = _fm(inp["norm1_g"][l])
        vecs[:, 64 + 16 * l:80 + 16 * l] = _fm(inp["norm2_g"][l])
        vecs[:, 112 + 96 * l:208 + 96 * l] = _fm(inp["b_mod"][l])
    vecs[:, 96:112] = _fm(inp["final_g"])
    vecs[:, 304:308] = _fm(inp["pool_scale"][0])
    vecs[:, 308:314] = _fm(inp["mla_gq"][0])
    vecs[:, 314:318] = _fm(inp["mla_gkv"][0])
    xb = inp["x"][b]
    own0 = HALF * s

    def halo(s_):
        o = HALF * s_
        hb = xb[o - 256:o] if s_ == 1 else xb[0:256]
        ha = xb[o + HALF:o + HALF + 256] if s_ == 0 else xb[SEQ - 256:SEQ]
        return np.ascontiguousarray(np.concatenate([hb, ha], 0))
    rpb = np.asarray(inp["na_rpb"][0], f)
    return {
        "x_halo2": halo(1 - s),
        "csn2": _dft_tables(1 - s),
        "nabias2": _na_class_tables(rpb, 1 - s),
        "rope2": _rope_tables(1 - s),
        "x_own": np.ascontiguousarray(xb[own0:own0 + HALF]),
        "x_par": np.ascontiguousarray(xb[HALF * (1 - s):HALF * (1 - s) + HALF]),
        "x_halo": halo(s),
        "ctxb": np.ascontiguousarray(inp["ctx"][b]),
        "vecs": vecs,
        "ident": np.eye(128, dtype=f),
        "w_mod": inp["w_mod"],
        "w_in_ab": inp["w_in_ab"][0],
        "w_four": inp["w_four"][0],
        "w_out_ab": inp["w_out_ab"][0],
        "w_ffn_gate": inp["w_ffn_gate"],
        "w_ffn_up": inp["w_ffn_up"],
        "w_ffn_down": inp["w_ffn_down"],
        "csn": _dft_tables(s),
        "csx": _dft_ctx(),
        "csc": _dft_ch(),
        "nabias": _na_class_tables(np.asarray(inp["na_rpb"][0], f), s),
        "w_in_cd": inp["w_in_cd"][0],
        "w_pool": inp["w_pool"][0],
        "w_uq": inp["w_uq"][0],
        "w_ukv": inp["w_ukv"][0],
        "w_out_cd": inp["w_out_cd"][0],
        "rope": _rope_tables(s),
        "ptab": _pool_tables(s),
    }


def run_all(inp, cores):
    inp = {k: np.asarray(v) for k, v in inp.items()}
    full = {c: make_inputs(c, inp) for c in cores}
    ids = list(range(len(cores)))
    ncA = Prog("A", 3).build()
    namesA = list(_NAMES["A"])
    rA = run_bass_kernel_spmd(ncA, [{k: full[c][k] for k in namesA} for c in cores], core_ids=ids)
    xT = {c: np.asarray(r["xT_out"]) for c, r in zip(cores, rA.results)}
    ncP = Prog("B", 1).build()
    namesP = list(_NAMES["B"])
    dummy = np.zeros((2, 704, NTOK), ml_dtypes.bfloat16)
    mapsP = []
    for c in cores:
        d = dict(full[c])
        d["xT_in"] = xT[c]
        d["exg"] = dummy
        mapsP.append({k: d[k] for k in namesP})
    rP = run_bass_kernel_spmd(ncP, mapsP, core_ids=ids)
    ex = {c: np.asarray(r["ex_out"]) for c, r in zip(cores, rP.results)}
    ncB = Prog("B", 6).build()
    namesB = list(_NAMES["B"])
    mapsB = []
    for c in cores:
        d = dict(full[c])
        d["xT_in"] = xT[c]
        d["exg"] = np.ascontiguousarray(np.stack([ex[c - c % 2], ex[c - c % 2 + 1]], 0))
        mapsB.append({k: d[k] for k in namesB})
    rB = run_bass_kernel_spmd(ncB, mapsB, core_ids=ids)
    return {c: np.asarray(r["out"]) for c, r in zip(cores, rB.results)}


_NAMES = {}


def run_b(inp, cores, xT, ex, stage):
    inp = {k: np.asarray(v) for k, v in inp.items()}
    ncB = Prog("B", stage, dbg=True).build()
    namesB = list(_NAMES["B"])
    mapsB = []
    for c in cores:
        d = make_inputs(c, inp)
        d["xT_in"] = xT[c]
        d["exg"] = np.ascontiguousarray(np.stack([ex[c - c % 2], ex[c - c % 2 + 1]], 0))
        mapsB.append({k: d[k] for k in namesB})
    rB = run_bass_kernel_spmd(ncB, mapsB, core_ids=list(range(len(cores))))
    return rB.results


def run_fused(inp, cores):
    inp = {k: np.asarray(v) for k, v in inp.items()}
    nc = Prog("F", 6).build()
    names = list(_NAMES["F"])
    maps = []
    for c in cores:
        d = make_inputs(c, inp)
        maps.append({k: d[k] for k in names})
    r = run_bass_kernel_spmd(nc, maps, core_ids=list(range(len(cores))))
    return {c: np.asarray(rr["out"]) for c, rr in zip(cores, r.results)}


def kernel(**inputs):
    cores = list(range(8))
    res = run_fused(inputs, cores)
    out = np.zeros((4, SEQ, D), np.float32)
    for c in cores:
        out[c // 2, HALF * (c % 2):HALF * (c % 2) + HALF] = res[c]
    return out
```

```python
import contextlib
import numpy as np
import ml_dtypes
import concourse.bass as bass
import concourse.mybir as mybir
from concourse.bass_utils import run_bass_kernel_spmd

F32 = mybir.dt.float32
BF16 = mybir.dt.bfloat16
AF = mybir.ActivationFunctionType
ALU = mybir.AluOpType

EPOCH = 4000
DMA_RING = 8

D = 2048
SEQ = 4096
HALF = 2048
CTX = 256
NTOK = HALF + CTX
NKV = NTOK + 512
DFF = 5632
NEG = -30000.0
EPS = 1e-6


class _Op:
    __slots__ = ("eng", "fn", "idx", "deps", "dma", "flag", "ring_slot", "ring_val")

    def __init__(self, eng, fn, dma):
        self.eng = eng
        self.fn = fn
        self.dma = dma
        self.deps = []
        self.flag = False


class Sched:
    ENGS = ("pe", "act", "dve", "pool", "sp")

    def __init__(self, nc):
        self.nc = nc
        self.ops = {e: [] for e in self.ENGS}
        self.state = {}
        self.ndma = {e: 0 for e in self.ENGS}
        self.bar = {e: [] for e in self.ENGS}

    def barrier(self):
        last = []
        for e in self.ENGS:
            ops = self.ops[e]
            if not ops:
                continue
            last.append(ops[-1])
            k = 0
            for op in reversed(ops):
                if op.dma:
                    last.append(op)
                    k += 1
                    if k >= DMA_RING:
                        break
        for e in self.ENGS:
            self.bar[e] = list(last)

    def add(self, eng, fn, reads=(), writes=(), dma=False):
        op = _Op(eng, fn, dma)
        op.idx = len(self.ops[eng])
        deps = []
        if self.bar[eng]:
            deps.extend(self.bar[eng])
            self.bar[eng] = []
        for b in reads:
            st = self.state.get(b)
            if st is None:
                st = self.state[b] = [None, {}, []]
            if st[0] is not None:
                deps.append(st[0])
            if dma:
                st[2].append(op)
            else:
                st[1][eng] = op
        for b in writes:
            st = self.state.get(b)
            if st is None:
                st = self.state[b] = [None, {}, []]
            if st[0] is not None:
                deps.append(st[0])
            for r in st[1].values():
                deps.append(r)
            deps.extend(st[2])
            st[0] = op
            st[1] = {}
            st[2] = []
        if dma:
            j = self.ndma[eng]
            self.ndma[eng] += 1
            op.ring_slot = j % DMA_RING
            op.ring_val = 16 * (j // DMA_RING + 1)
        seen = set()
        for d in deps:
            if d is op or id(d) in seen:
                continue
            seen.add(id(d))
            if d.eng == eng and not d.dma and eng == "pe":
                continue
            op.deps.append(d)
        self.ops[eng].append(op)
        return op

    def emit(self):
        nc = self.nc
        for e in self.ENGS:
            waited = {}
            waited_dma = set()
            for op in self.ops[e]:
                nd = []
                for d in op.deps:
                    if d.dma:
                        if id(d) in waited_dma:
                            continue
                        waited_dma.add(id(d))
                        nd.append(d)
                    else:
                        if waited.get(d.eng, -1) >= d.idx:
                            continue
                        waited[d.eng] = d.idx
                        d.flag = True
                        nd.append(d)
                op.deps = nd
        count = {}
        nsem = {}
        for e in self.ENGS:
            c = 0
            for op in self.ops[e]:
                if op.flag and not op.dma:
                    count[id(op)] = c
                    c += 1
            nsem[e] = (c + EPOCH - 1) // EPOCH
        with contextlib.ExitStack() as es:
            csem = {e: [es.enter_context(nc.semaphore(f"c_{e}_{k}")) for k in range(nsem[e])] for e in self.ENGS}
            dsem = {e: [es.enter_context(nc.semaphore(f"d_{e}_{k}")) for k in range(DMA_RING)]
                    for e in self.ENGS if self.ndma[e] > 0}
            block = es.enter_context(nc.Block())
            engobj = {"pe": block.tensor, "act": block.scalar, "dve": block.vector, "pool": block.gpsimd, "sp": block.sync}

            def make(e):
                def body(eng):
                    for op in self.ops[e]:
                        for d in op.deps:
                            if d.dma:
                                eng.wait_ge(dsem[d.eng][d.ring_slot], d.ring_val)
                            else:
                                c = count[id(d)]
                                eng.wait_ge(csem[d.eng][c // EPOCH], c % EPOCH + 1)
                        if op.dma:
                            if op.ring_val > 16:
                                eng.wait_ge(dsem[e][op.ring_slot], op.ring_val - 16)
                            ins = op.fn(eng)
                            ins.then_inc(dsem[e][op.ring_slot], 16)
                        else:
                            ins = op.fn(eng)
                            if op.flag:
                                c = count[id(op)]
                                ins.then_inc(csem[e][c // EPOCH], 1)
                    if self.ndma[e] > 0:
                        n = self.ndma[e]
                        for s in range(min(DMA_RING, n)):
                            last = ((n - 1 - s) // DMA_RING) + 1
                            eng.wait_ge(dsem[e][s], 16 * last)
                return body

            for e in self.ENGS:
                if self.ops[e]:
                    engobj[e](make(e))


def DMA(S, q, out, in_, reads, writes):
    return S.add(q, lambda e: e.dma_start(out=out, in_=in_), reads, writes, dma=True)


def MM(S, out, lhsT, rhs, start, stop, reads, writes):
    return S.add("pe", lambda e: e.matmul(out, lhsT=lhsT, rhs=rhs, start=start, stop=stop), reads, writes)


def TR(S, out, in_, ident, reads, writes):
    return S.add("pe", lambda e: e.transpose(out=out, in_=in_, identity=ident), reads, writes)


def ACT(S, out, in_, func, reads, writes, bias=None, scale=None):
    kw = {}
    if bias is not None:
        kw["bias"] = bias
    if scale is not None:
        kw["scale"] = scale
    return S.add("act", lambda e: e.activation(out=out, in_=in_, func=func, **kw), reads, writes)


def COPY(S, eng, out, in_, reads, writes):
    if eng == "act":
        return S.add("act", lambda e: e.copy(out=out, in_=in_), reads, writes)
    return S.add(eng, lambda e: e.tensor_copy(out=out, in_=in_), reads, writes)


def STT(S, eng, out, in0, scalar, in1, op0, op1, reads, writes):
    return S.add(eng, lambda e: e.scalar_tensor_tensor(out=out, in0=in0, scalar=scalar, in1=in1, op0=op0, op1=op1), reads, writes)


def TT(S, eng, out, in0, in1, op, reads, writes):
    return S.add(eng, lambda e: e.tensor_tensor(out=out, in0=in0, in1=in1, op=op), reads, writes)


def RECIP(S, out, in_, reads, writes):
    return S.add("dve", lambda e: e.reciprocal(out=out, in_=in_), reads, writes)


def _dft_tables(s):
    n_in = np.concatenate([np.arange(HALF) + HALF * s, np.arange(HALF) + HALF * (1 - s)]).astype(np.int64)[:, None]
    n_out = (np.arange(HALF, dtype=np.int64) + HALF * s)[None, :]
    ang = 2.0 * np.pi * ((n_in * n_out) % SEQ).astype(np.float64) / SEQ
    cs = np.stack([np.cos(ang) / 64.0, -np.sin(ang) / 64.0], axis=1)
    return cs.astype(ml_dtypes.bfloat16)


def _dft_ctx():
    n = np.arange(CTX, dtype=np.int64)
    ang = 2.0 * np.pi * ((n[:, None] * n[None, :]) % CTX).astype(np.float64) / CTX
    cs = np.stack([np.cos(ang) / 16.0, -np.sin(ang) / 16.0], axis=1)
    return cs.astype(ml_dtypes.bfloat16)


def _dft_ch():
    c = np.arange(128, dtype=np.int64)
    ang = 2.0 * np.pi * ((c[:, None] * c[None, :]) % 128).astype(np.float64) / 128
    cs = np.concatenate([np.cos(ang), np.sin(ang)], axis=1) / np.sqrt(128.0)
    return cs.astype(ml_dtypes.bfloat16)


def _na_win(ti):
    if ti == 0:
        return -4, 12
    if ti == 1:
        return -2, 10
    if ti == 15:
        return 24, 11
    return 2 * ti - 4, 9


NA_CLS_TILES = [0, 1, 2, 14, 15]


def _na_cls(ti):
    return {0: 0, 1: 1, 14: 3, 15: 4}.get(ti, 2)


def _na_class_tables(rpb, s):
    h = rpb.shape[0]
    flat = np.concatenate([rpb.reshape(h, -1), np.full((h, 1), NEG, np.float32)], axis=1)
    out = []
    q = np.arange(128)
    p = np.arange(128)
    for ti in NA_CLS_TILES:
        gi = 16 * s + ti
        w0, nrow = _na_win(ti)
        r = 2 * gi + q // 64
        qc = q % 64
        rs = np.clip(r - 4, 0, 56)
        cst = np.clip(qc - 8, 0, 48)
        idx = np.full((128, 6, 128), 15 * 31, np.int64)
        for c in range((nrow + 1) // 2):
            lrow = w0 + 2 * c + p // 64
            krow = 32 * s + lrow
            kcol = p % 64
            okr = (krow[:, None] >= rs[None, :]) & (krow[:, None] < rs[None, :] + 8) & (krow[:, None] <= 63) & (krow[:, None] >= 0)
            okr = okr & ((lrow < w0 + nrow)[:, None])
            okc = (kcol[:, None] >= cst[None, :]) & (kcol[:, None] < cst[None, :] + 16)
            ri = np.clip(krow[:, None] - r[None, :] + 7, 0, 14)
            ci = np.clip(kcol[:, None] - qc[None, :] + 15, 0, 30)
            lin = ri * 31 + ci
            idx[:, c, :] = np.where(okr & okc, lin, 15 * 31)
        out.append(flat[:, idx])
    return np.ascontiguousarray(np.stack(out, 0))


def _rope_tables(s):
    t = np.arange(HALF, dtype=np.float64) + HALF * s
    inv = 10000.0 ** (-np.arange(16, dtype=np.float64) / 16.0)
    out = np.zeros((2, 64, HALF), np.float64)
    for half, pos in enumerate([np.floor(t / 64.0), np.mod(t, 64.0)]):
        ang = inv[:, None] * pos[None, :]
        b = 32 * half
        out[0, b:b + 16] = np.cos(ang)
        out[0, b + 16:b + 32] = np.cos(ang)
        out[1, b:b + 16] = -np.sin(ang)
        out[1, b + 16:b + 32] = np.sin(ang)
    return out.astype(np.float32)


ROPE_PERM = list(range(16, 32)) + list(range(0, 16)) + list(range(48, 64)) + list(range(32, 48))


def _pool_tables(s):
    out = np.zeros((128, 3, 4, 3, 128), np.float64)
    i = np.arange(128)
    for ci, gt in enumerate([16 * s, 16 * s + 1 if s == 0 else 16 * s + 14, 16 * s + 15]):
        if ci == 1:
            gt = 8
        for g, w in enumerate((2, 4, 8, 16)):
            to = 128 * gt + i
            lo = np.maximum(to - w // 2, 0)
            hi = np.minimum(to + w // 2, SEQ)
            cnt = (hi - lo).astype(np.float64)
            for rel in range(3):
                tin = 128 * (gt + rel - 1) + i
                m = (tin[:, None] >= lo[None, :]) & (tin[:, None] < hi[None, :])
                val = np.where(m, 1.0 / cnt[None, :], 0.0) - (tin[:, None] == to[None, :]).astype(np.float64)
                out[:, ci, g, rel, :] = val
    return out.astype(ml_dtypes.bfloat16)


def _fm(v):
    return np.ascontiguousarray(np.asarray(v, np.float32).reshape(-1, 128).T)


IN_SHAPES = {
    "x_own": ([HALF, D], F32), "x_par": ([HALF, D], F32), "x_halo": ([512, D], F32), "ctxb": ([CTX, D], F32),
    "vecs": ([128, 400], F32), "ident": ([128, 128], F32), "w_mod": ([2, D, 6 * D], F32),
    "w_in_ab": ([D, 5120], F32), "w_four": ([4, 128, 128], F32), "w_out_ab": ([D, D], F32),
    "w_ffn_gate": ([2, D, DFF], F32), "w_ffn_up": ([2, D, DFF], F32), "w_ffn_down": ([2, DFF, D], F32),
    "csn": ([SEQ, 2, HALF], BF16), "csx": ([CTX, 2, CTX], BF16), "csc": ([128, 256], BF16),
    "nabias": ([5, 12, 128, 6, 128], F32),
    "w_in_cd": ([D, 1856], F32), "w_pool": ([4, 128, 128], F32), "w_uq": ([768, 2304], F32), "w_ukv": ([512, 3072], F32),
    "w_out_cd": ([D, D], F32), "rope": ([2, 64, HALF], F32), "ptab": ([128, 3, 4, 3, 128], BF16),
    "xT_in": ([D, NTOK], F32), "exg": ([2, 704, NTOK], BF16),
    "x_halo2": ([512, D], F32), "csn2": ([SEQ, 2, HALF], BF16), "nabias2": ([5, 12, 128, 6, 128], F32), "rope2": ([2, 64, HALF], F32),
}


class _LazyIn(dict):
    def __init__(self, nc):
        super().__init__()
        self.nc = nc

    def __missing__(self, name):
        shape, dt = IN_SHAPES[name]
        ap = self.nc.dram_tensor(name, list(shape), dt, kind="ExternalInput").ap()
        self[name] = ap
        return ap


class Prog:
    def __init__(self, mode, stage=3, dbg=False):
        self.mode = mode
        self.stage = stage
        self.dbg = dbg

    def build(self):
        nc = bass.Bass("TRN2", target_bir_lowering=False)
        self.nc = nc

        def dscr(name, shape, dt):
            return nc.dram_tensor(name, list(shape), dt).ap()

        def dout(name, shape, dt):
            return nc.dram_tensor(name, list(shape), dt, kind="ExternalOutput").ap()

        I = _LazyIn(nc)
        self.I = I
        A = self.mode == "A"
        Fm = self.mode == "F"
        self.xT = dout("xT_out", [D, NTOK], F32) if A else dscr("xT_s", [D, NTOK], F32)
        self.exb = dout("ex_out", [704, NTOK], BF16) if (A or self.stage < 6) else dscr("exb_s", [704, NTOK], BF16)
        self.xT2 = dscr("xT2_s", [D, NTOK], F32)
        self.exb2 = dscr("exb2_s", [704, NTOK], BF16)
        self.xTc, self.xk, self.role = self.xT, "xT", 0
        self.hT = dscr("hT_s", [D, NKV], BF16)
        self.mixT = dscr("mixT_s", [D, NTOK], BF16)
        self.qT = dscr("qT_s", [1536, NTOK], BF16)
        self.kT = dscr("kT_s", [1536, NKV], BF16)
        self.vS = dscr("v_s", [NKV, 1536], BF16)
        self.aT = dscr("aT_s", [DFF, NTOK], BF16)
        self.l1raw = dscr("l1raw_s", [1408, NTOK], F32)
        self.qn_s = dscr("qn_s", [1536, HALF], BF16)
        self.qr_s = dscr("qr_s", [768, HALF], BF16)
        self.kn_s = dscr("kn_s", [1536, SEQ + CTX], BF16)
        self.v1_s = dscr("v1_s", [SEQ + CTX, 1536], BF16)
        if not A:
            self.out = dout("out", [HALF, D], F32)
            if not Fm:
                self.exg = I["exg"]

        S = Sched(nc)
        self.S = S
        with contextlib.ExitStack() as es:
            self.es = es
            self.pp = [es.enter_context(nc.psum_tensor(f"pp{i}", [128, 1024], F32)) for i in range(4)]
            self.ident_f = self.sb("ident_f", [128, 128], F32)
            self.ones_b = self.sb("ones_b", [128, 128], BF16)
            self.vecs = self.sb("vecs_sb", [128, 400], F32)
            self.scb = self.sb("scb", [128, 16, 2], BF16)
            self.modT = self.sb("modT", [128, 2, 96, 2], F32)
            self.G = self.sb("Gmod", [128, 2, 2, 2, 16], F32)
            self.epsb = self.sb("epsb", [128, 1], F32)
            DMA(S, "sp", self.ident_f[:], I["ident"], [], ["ident_f"])
            DMA(S, "sp", self.vecs[:], I["vecs"], [], ["vecs"])
            S.add("pool", lambda e: e.memset(self.ones_b[:], 1.0), [], ["ones_b"])
            S.add("pool", lambda e: e.memset(self.epsb[:], EPS), [], ["epsb"])
            if not A and not Fm:
                for b in range(5):
                    c0 = 512 * b
                    nt = min(512, NTOK - c0)
                    DMA(S, "sp", self.xTc[:, c0:c0 + nt], I["xT_in"][:, c0:c0 + nt], [], [(self.xk, k, b) for k in range(16)])
            self.phase_mod()
            if Fm:
                self.set_role(0)
                self.layer0()
                self.set_role(1)
                self.layer0()
                with contextlib.ExitStack() as es_l1:
                    self.set_role(0)
                    self.l1_part1(es_l1)
                    self.set_role(1)
                    self.l1_part1(es_l1)
                    self.set_role(0)
                    self.l1_part2(es_l1)
                    self.l1_attn()
                self.barrier()
                self.outproj(1, I["w_out_cd"], HALF)
                self.barrier()
                self.ffn(1, HALF)
                self.barrier()
                self.final_norm()
            elif A:
                self.layer0()
                if self.stage >= 4:
                    with contextlib.ExitStack() as es_l1:
                        self.l1_part1(es_l1)
            else:
                st = self.stage
                with contextlib.ExitStack() as es_l1:
                    self.l1_part1(es_l1)
                    if st >= 2:
                        self.l1_part2(es_l1)
                    if st >= 3:
                        self.l1_attn()
                self.barrier()
                if st >= 4:
                    self.outproj(1, I["w_out_cd"], HALF)
                    self.barrier()
                if st >= 5:
                    self.ffn(1, HALF)
                    self.barrier()
                if st >= 6:
                    self.final_norm()
                elif self.dbg:
                    xd = dout("xT_dbg", [D, NTOK], F32)
                    md = dout("mix_dbg", [D, NTOK], BF16)
                    self.barrier()
                    DMA(S, "sp", xd, self.xT, [], ["xd"])
                    DMA(S, "sp", md, self.mixT, [], ["md"])
                    rd_ = dout("raw_dbg", [1408, NTOK], F32)
                    DMA(S, "sp", rd_, self.l1raw, [], ["rawd"])
            S.emit()
        _NAMES[self.mode] = list(I.keys())
        return nc

    def set_role(self, r):
        self.role = r
        self.xTc, self.xk = (self.xT, "xT") if r == 0 else (self.xT2, "xT2")

    def sb(self, name, shape, dt, es=None):
        self._uid = getattr(self, "_uid", 0) + 1
        return (es or self.es).enter_context(self.nc.sbuf_tensor(f"{name}_u{self._uid}", list(shape), dt))

    def bank(self, i):
        return self.pp[i // 2][:, (i % 2) * 512:(i % 2) * 512 + 512], ("ps", i)

    def barrier(self):
        self.S.barrier()

    V_C2 = 0
    V_N1 = 32
    V_N2 = 64
    V_FG = 96
    V_BM = 112
    V_PS = 304
    V_GQ = 308
    V_GKV = 314

    def phase_mod(self):
        S, nc, I = self.S, self.nc, self.I
        with contextlib.ExitStack() as es:
            wts = [self.sb(f"wmod{i}", [128, 16, 512], BF16, es) for i in range(2)]
            ACT(S, self.scb[:].rearrange("p k m -> p (k m)"), self.vecs[:, 0:32], AF.Silu, ["vecs"], ["scb"])
            n = 0
            for l in range(2):
                for cg in range(24):
                    wt = wts[n % 2]
                    wk = ("wmod", n % 2)
                    DMA(S, "pool", wt[:], I["w_mod"][l, :, cg * 512:(cg + 1) * 512].rearrange("(k p) n -> p k n", p=128), [], [wk])
                    ps, pk = self.bank(n % 2)
                    for j in range(4):
                        for kc in range(16):
                            MM(S, ps[:, j * 2:j * 2 + 2], wt[:, kc, j * 128:(j + 1) * 128], self.scb[:, kc, :], kc == 0, kc == 15,
                               [wk, "scb"], [pk])
                    for m in range(2):
                        TT(S, "dve", self.modT[:, l, cg * 4:cg * 4 + 4, m], ps[:, 0:8].rearrange("p (j m) -> p j m", m=2)[:, :, m],
                           self.vecs[:, self.V_BM + l * 96 + cg * 4:self.V_BM + l * 96 + cg * 4 + 4], ALU.add,
                           [pk, "vecs"], [("modT", l)])
                    n += 1
            for l in range(2):
                for nrm in range(2):
                    for m in range(2):
                        sc = self.modT[:, l, 16 + 48 * nrm:32 + 48 * nrm, m]
                        gcol = (self.V_N1 if nrm == 0 else self.V_N2) + l * 16
                        STT(S, "dve", self.G[:, l, nrm, m, :], sc, 1.0, self.vecs[:, gcol:gcol + 16], ALU.add, ALU.mult,
                            [("modT", l), "vecs"], [("G", l)])
            self.barrier()

    def mcol(self, l, which, kc, m):
        return self.modT[:, l, which * 16 + kc, m:m + 1]

    def layer0(self):
        S, nc, I = self.S, self.nc, self.I
        with contextlib.ExitStack() as es:
            AB = self.sb("AB", [128, 34, 4, 256], BF16, es)
            with contextlib.ExitStack() as es1:
                xin = self.sb("xin", [128, 4, D], F32, es1)
                xTb = self.sb("xTb", [128, 16, 512], F32, es1)
                sq = self.sb("sq", [128, 16, 512], BF16, es1)
                tmp = self.sb("tmp", [128, 2, 512], F32, es1)
                hTb = self.sb("hTb", [128, 16, 512], BF16, es1)
                uTb = self.sb("uTb", [128, 4, 512], BF16, es1)
                r1 = self.sb("r1", [128, 512], F32, es1)
                rstd = self.sb("rstd", [128, 512], F32, es1)
                Wu = self.sb("Wu", [128, 16, 512], BF16, es1)
                csc = self.sb("csc_sb", [128, 256], BF16, es1)
                DMA(S, "pool", Wu[:], I["w_in_ab"][:, 0:512].rearrange("(k p) n -> p k n", p=128), [], ["Wu"])
                DMA(S, "sp", csc[:], I["csc"], [], ["csc"])
                blocks = []
                role = self.role
                x_o, x_p = (I["x_own"], I["x_par"]) if role == 0 else (I["x_par"], I["x_own"])
                for j in range(4):
                    blocks.append(("own", x_o, 512 * j, 512, 512 * j, 4 * j, 0))
                blocks.append(("ctx", I["ctxb"], 0, 256, HALF, 32, 1))
                for j in range(4):
                    blocks.append(("par", x_p, 512 * j, 512, None, 16 + 4 * j, 0))
                blocks.append(("halo", I["x_halo"] if role == 0 else I["x_halo2"], 0, 512, NTOK, None, 0))
                for bi, (kind, src, t0, nt, lc, ab0, m) in enumerate(blocks):
                    ntile = nt // 128
                    DMA(S, "sp", xin[:, 0:ntile, :], src[t0:t0 + nt, :].rearrange("(t p) f -> p t f", p=128), [], ["xin"])
                    for kc in range(16):
                        ps, pk = self.bank(kc % 2)
                        for t in range(ntile):
                            TR(S, ps[:, t * 128:(t + 1) * 128], xin[:, t, kc * 128:(kc + 1) * 128], self.ident_f[:], ["xin", "ident_f"], [pk])
                        COPY(S, "act" if kc % 2 == 0 else "dve", xTb[:, kc, 0:nt], ps[:, 0:nt], [pk], [("xTb", kc)])
                    xk = "xTb_all"
                    allk = [("xTb", kc) for kc in range(16)]
                    if kind == "own" or (kind == "ctx" and self.role == 0):
                        DMA(S, "sp", self.xTc[:, lc:lc + nt].rearrange("(k p) t -> p k t", p=128), xTb[:, :, 0:nt], allk,
                            [(self.xk, kc, lc // 512) for kc in range(16)])

                    def out_fn(kc):
                        return hTb[:, kc, 0:nt], ("hTb", kc)
                    self._norm_multi(xTb, allk, nt, 0, 0, m, sq, tmp, rstd, r1, out_fn, 2)
                    hk = [("hTb", kc) for kc in range(16)]
                    if lc is not None:
                        DMA(S, "sp", self.hT[:, lc:lc + nt].rearrange("(k p) t -> p k t", p=128), hTb[:, :, 0:nt], hk, [("hT", lc // 512)])
                    if ab0 is None:
                        continue
                    for g in range(4):
                        ps, pk = self.bank(3 + (g % 2))
                        for kc in range(16):
                            MM(S, ps[:, 0:nt], Wu[:, kc, g * 128:(g + 1) * 128], hTb[:, kc, 0:nt], kc == 0, kc == 15, ["Wu", ("hTb", kc)], [pk])
                        COPY(S, "dve", uTb[:, g, 0:nt], ps[:, 0:nt], [pk], [("uTb", g)])
                    for t in range(ntile):
                        pA = self.pp[3 if t % 2 == 0 else 2]
                        pAk = ("ps", 6 if t % 2 == 0 else 4)
                        pAk2 = ("ps", 7 if t % 2 == 0 else 5)
                        for g in range(4):
                            MM(S, pA[:, g * 256:(g + 1) * 256], uTb[:, g, t * 128:(t + 1) * 128], csc[:], True, True,
                               [("uTb", g), "csc"], [pAk, pAk2])
                        COPY(S, "act", AB[:, ab0 + t, :, :].rearrange("p g c -> p (g c)"), pA[:, :], [pAk, pAk2], [("AB", ab0 + t)])
            self.barrier()
            with contextlib.ExitStack() as es1:
                cs = [self.sb(f"csn{i}", [128, 32, 2, 512], BF16, es1) for i in range(1)]
                Wf = self.sb("Wf", [128, 4, 128], BF16, es1)
                yT = [self.sb(f"yT{i}", [128, 512], BF16, es1) for i in range(2)]
                zst = [self.sb(f"zst{i}", [128, 512], BF16, es1) for i in range(2)]
                csx = self.sb("csx_sb", [128, 2, 2, 256], BF16, es1)
                DMA(S, "pool", Wf[:], I["w_four"].rearrange("g c d -> c g d"), [], ["Wf"])
                for t in range(2):
                    DMA(S, "sp", csx[:, :, t, :], I["csx"][:, t, :].rearrange("(k p) n -> p k n", p=128), [], ["csx"])
                n = 0
                csn_in = I["csn"] if self.role == 0 else I["csn2"]
                for ob in range(5 if self.role == 0 else 4):
                    if ob < 4:
                        nt, nk, col0 = 512, 32, 512 * ob
                        cst = cs[0]
                        ck = "csn0"
                        for t in range(2):
                            DMA(S, "sp", cst[:, :, t, :], csn_in[:, t, col0:col0 + 512].rearrange("(k p) n -> p k n", p=128), [], [ck])
                        ab_base = 0

                        def rhs_of(kc, t):
                            return cst[:, kc, t, :]
                    else:
                        nt, nk, col0 = 256, 2, HALF
                        ck = "csx"
                        ab_base = 32

                        def rhs_of(kc, t):
                            return csx[:, kc, t, :]
                    for g in range(4):
                        ps, pk = self.bank(n % 2)
                        for kc in range(nk):
                            for t in range(2):
                                MM(S, ps[:, 0:nt], AB[:, ab_base + kc, g, t * 128:(t + 1) * 128], rhs_of(kc, t),
                                   kc == 0 and t == 0, kc == nk - 1 and t == 1, [("AB", ab_base + kc), ck], [pk])
                        y = yT[n % 2]
                        COPY(S, "act", y[:, 0:nt], ps[:, 0:nt], [pk], [("yT", n % 2)])
                        ps2, pk2 = self.bank(2 + n % 2)
                        MM(S, ps2[:, 0:nt], Wf[:, g, :], y[:, 0:nt], True, True, ["Wf", ("yT", n % 2)], [pk2])
                        z = zst[n % 2]
                        COPY(S, "dve", z[:, 0:nt], ps2[:, 0:nt], [pk2], [("zst", n % 2)])
                        DMA(S, "sp", self.mixT[g * 128:(g + 1) * 128, col0:col0 + nt], z[:, 0:nt], [("zst", n % 2)], [("mixT", g, ob)])
                        n += 1
        self.barrier()
        if self.stage < 2:
            return
        self.l0_qkv()
        self.barrier()
        self.l0_attn()
        self.barrier()
        if self.stage < 3:
            return
        nt0 = NTOK if self.role == 0 else HALF
        self.outproj(0, self.I["w_out_ab"], nt0)
        self.barrier()
        self.ffn(0, nt0)
        self.barrier()

    def _norm_multi(self, xTb, xkeys, nt, l, nrm, m, sq, tmp, rstd, r1, out_fn, bank_i):
        S = self.S
        ACT(S, sq[:, :, 0:nt], xTb[:, :, 0:nt], AF.Square, xkeys, ["sq"])
        ps, pk = self.bank(bank_i)
        for kc in range(16):
            MM(S, ps[:, 0:nt], self.ones_b[:], sq[:, kc, 0:nt], kc == 0, kc == 15, ["ones_b", "sq"], [pk])
        ACT(S, r1[:, 0:nt], ps[:, 0:nt], AF.Sqrt, [pk, "epsb"], ["r1"], bias=self.epsb[:], scale=1.0 / D)
        RECIP(S, rstd[:, 0:nt], r1[:, 0:nt], ["r1"], ["rstd"])
        for kc in range(16):
            gap = self.G[:, l, nrm, m, kc:kc + 1]
            STT(S, "dve", tmp[:, kc % 2, 0:nt], xTb[:, kc, 0:nt], gap, rstd[:, 0:nt], ALU.mult, ALU.mult,
                list(xkeys) + [("G", l), "rstd"], [("tmp", kc % 2)])
            o, ok = out_fn(kc)
            ACT(S, o, tmp[:, kc % 2, 0:nt], AF.Identity, [("tmp", kc % 2), ("modT", l)], [ok], bias=self.mcol(l, 3 * nrm, kc, m), scale=1.0)

    def l0_qkv(self):
        S, I = self.S, self.I
        with contextlib.ExitStack() as es:
            hTr = self.sb("hTr", [128, 16, NKV], BF16, es)
            wts = [self.sb(f"wqkv{i}", [128, 16, 512], BF16, es) for i in range(2)]
            qst = [self.sb(f"qst{i}", [128, NKV], BF16, es) for i in range(2)]
            vst = [self.sb(f"vst{i}", [128, 22, 512], BF16, es) for i in range(1)]
            tbs = [(0, 512, 0), (512, 512, 1), (1024, 512, 2), (1536, 512, 3), (2048, 256, 4), (2304, 512, 5)]
            for (c0, nt, b) in tbs:
                DMA(S, "sp", hTr[:, :, c0:c0 + nt], self.hT[:, c0:c0 + nt].rearrange("(k p) t -> p k t", p=128), [("hT", 4 if b == 5 else b)] if False else [("hT", c0 // 512)], [("hTr", b)])
            n = 0
            e = 0
            for part, dst, ntb in (("q", self.qT, 5 if self.role == 0 else 4), ("k", self.kT, 6)):
                cbase = 512 if part == "q" else 2048
                for cg in range(3):
                    wt, wk = wts[n % 2], ("wqkv", n % 2)
                    DMA(S, "pool", wt[:], I["w_in_ab"][:, cbase + cg * 512:cbase + (cg + 1) * 512].rearrange("(k p) n -> p k n", p=128), [], [wk])
                    n += 1
                    for j in range(4):
                        hd = cg * 4 + j
                        st, sk = qst[hd % 2], ("qst", hd % 2)
                        for (c0, nt, b) in tbs[:ntb]:
                            ps, pk = self.bank(e % 4)
                            for kc in range(16):
                                MM(S, ps[:, 0:nt], wt[:, kc, j * 128:(j + 1) * 128], hTr[:, kc, c0:c0 + nt], kc == 0, kc == 15, [wk, ("hTr", b)], [pk])
                            COPY(S, "act" if e % 2 == 0 else "dve", st[:, c0:c0 + nt], ps[:, 0:nt], [pk], [sk])
                            e += 1
                        ncol = (NTOK if self.role == 0 else HALF) if part == "q" else NKV
                        DMA(S, "sp", dst[hd * 128:(hd + 1) * 128, 0:ncol], st[:, 0:ncol], [sk], [(part + "T", hd)])
            for cg in range(3):
                wt, wk = wts[n % 2], ("wqkv", n % 2)
                DMA(S, "pool", wt[:], I["w_in_ab"][:, 3584 + cg * 512:3584 + (cg + 1) * 512].rearrange("(k p) n -> p k n", p=128), [], [wk])
                n += 1
                v = vst[0]
                for t in range(22):
                    ps, pk = self.bank(4 + e % 4)
                    hb = t // 4 if t < 16 else (4 if t < 18 else 5)
                    for kc in range(16):
                        MM(S, ps[:, :], hTr[:, kc, t * 128:(t + 1) * 128], wt[:, kc, :], kc == 0, kc == 15, [wk, ("hTr", hb)], [pk])
                    COPY(S, "act" if e % 2 == 0 else "dve", v[:, t, :], ps[:, :], [pk], ["vst"])
                    e += 1
                DMA(S, "sp", self.vS[:, cg * 512:(cg + 1) * 512].rearrange("(t p) c -> p t c", p=128), v[:], ["vst"], [("vS", cg)])

    @staticmethod
    def _loc(lrow):
        if lrow < 0:
            return NTOK + (lrow + 4) * 64
        if lrow >= 32:
            return NTOK + 256 + (lrow - 32) * 64
        return lrow * 64

    def l0_attn(self):
        S, I = self.S, self.I
        scale = 128.0 ** -0.5
        with contextlib.ExitStack() as es:
            kwin = [self.sb(f"kwin{i}", [128, 12, 1024], BF16, es) for i in range(2)]
            vwin = [self.sb(f"vwin{i}", [128, 8, 1536], BF16, es) for i in range(2)]
            bia = [self.sb(f"bia{i}", [128, 12, 768], F32, es) for i in range(2)]
            qw = [self.sb(f"qw{i}", [128, 12, 128], BF16, es) for i in range(2)]
            ost = [self.sb(f"ost{i}", [128, 12, 128], BF16, es) for i in range(2)]
            sbs = [self.sb(f"sbs{i}", [128, 768], F32, es) for i in range(2)]
            pT = [self.sb(f"pT{i}", [128, 1024], BF16, es) for i in range(3)]
            rd = [self.sb(f"rd{i}", [128, 128], F32, es) for i in range(2)]
            allq = [("qT", h) for h in range(12)]
            allk = [("kT", h) for h in range(12)]
            allv = [("vS", c) for c in range(3)]
            nab = I["nabias"] if self.role == 0 else I["nabias2"]
            ntile = 18 if self.role == 0 else 16
            info = {}

            def load_tile(ti):
                i2 = ti % 2
                isctx = ti >= 16
                kw_, vw_, bi_, qw_ = kwin[i2], vwin[i2], bia[i2], qw[i2]
                kk, vk, bk, qk = ("kwin", i2), ("vwin", i2), ("bia", i2), ("qw", i2)
                qc0 = ti * 128
                if not isctx:
                    w0, nrow = _na_win(ti)
                    chunks = []
                    for c in range((nrow + 1) // 2):
                        chunks.append((self._loc(w0 + 2 * c), 128 if 2 * c + 1 < nrow else 64))
                    nl = len(chunks)
                    chunks += [(HALF, 128), (HALF + 128, 128)]
                    DMA(S, "sp", bi_[:, :, 0:nl * 128].rearrange("p h (c q) -> p h c q", q=128),
                        nab[_na_cls(ti)][:, :, 0:nl, :].rearrange("h p c q -> p h c q"), [], [bk])
                else:
                    chunks = [(HALF, 128), (HALF + 128, 128)]
                for c, (l0, nk) in enumerate(chunks):
                    DMA(S, "sp", kw_[:, :, c * 128:c * 128 + nk], self.kT[:, l0:l0 + nk].rearrange("(h p) t -> p h t", p=128), allk, [kk])
                    DMA(S, "sp", vw_[0:nk, c, :], self.vS[l0:l0 + nk, :], allv, [vk])
                DMA(S, "sp", qw_[:], self.qT[:, qc0:qc0 + 128].rearrange("(h p) t -> p h t", p=128), allq, [qk])
                info[ti] = chunks

            def front(n, ti, h):
                i2 = ti % 2
                chunks = info[ti]
                nch = len(chunks)
                nloc = nch - 2
                kw_, bi_, qw_ = kwin[i2], bia[i2], qw[i2]
                kk, bk, qk = ("kwin", i2), ("bia", i2), ("qw", i2)
                j2 = n % 2
                pS = self.pp[j2]
                pSk = [("ps", 2 * j2), ("ps", 2 * j2 + 1)]
                for c, (l0, nk) in enumerate(chunks):
                    MM(S, pS[0:nk, c * 128:(c + 1) * 128], kw_[:, h, c * 128:c * 128 + nk], qw_[:, h, :], True, True, [kk, qk], pSk)
                p_, pk_ = pT[n % 3], ("pT", n % 3)
                if nloc > 0:
                    sb_ = sbs[j2]
                    nlc = nloc * 128
                    STT(S, "dve", sb_[:, 0:nlc], pS[:, 0:nlc], scale, bi_[:, h, 0:nlc], ALU.mult, ALU.add, pSk + [bk], [("sbs", j2)])
                    ACT(S, p_[:, 0:nlc], sb_[:, 0:nlc], AF.Exp, [("sbs", j2)], [pk_])
                ACT(S, p_[:, nloc * 128:nch * 128], pS[:, nloc * 128:nch * 128], AF.Exp, pSk, [pk_], scale=scale)

            def back(n, ti, h):
                i2 = ti % 2
                chunks = info[ti]
                nch = len(chunks)
                vw_, os_ = vwin[i2], ost[i2]
                vk, ok = ("vwin", i2), ("ost", i2)
                j2 = n % 2
                p_, pk_ = pT[n % 3], ("pT", n % 3)
                pO, pOk = self.bank(4 + 2 * j2)
                pD, pDk = self.bank(5 + 2 * j2)
                for c, (l0, nk) in enumerate(chunks):
                    MM(S, pO[:, 0:128], vw_[0:nk, c, h * 128:(h + 1) * 128], p_[0:nk, c * 128:(c + 1) * 128], c == 0, c == nch - 1, [vk, pk_], [pOk])
                for c, (l0, nk) in enumerate(chunks):
                    MM(S, pD[:, 0:128], self.ones_b[0:nk, :], p_[0:nk, c * 128:(c + 1) * 128], c == 0, c == nch - 1, ["ones_b", pk_], [pDk])
                RECIP(S, rd[j2][:, :], pD[:, 0:128], [pDk], [("rd", j2)])
                TT(S, "dve", os_[:, h, :], pO[:, 0:128], rd[j2][:, :], ALU.mult, [pOk, ("rd", j2)], [ok])
                if h == 11:
                    qc0 = ti * 128
                    DMA(S, "sp", self.mixT[512:2048, qc0:qc0 + 128].rearrange("(h p) t -> p h t", p=128), os_[:], [ok],
                        [("mixT", 4 + hh, ti // 4) for hh in range(12)])

            items = [(ti, h) for ti in range(ntile) for h in range(12)]
            for n in range(len(items) + 1):
                if n < len(items):
                    ti, h = items[n]
                    if h == 0:
                        load_tile(ti)
                    front(n, ti, h)
                if n >= 1:
                    ti, h = items[n - 1]
                    back(n - 1, ti, h)

    def l1_part1(self, es_l1):
        S, I = self.S, self.I
        role = self.role
        lite = role == 1
        if not lite:
            self.cqn = self.sb("cqn", [128, 6, HALF], BF16, es_l1)
            self.upool = self.sb("upool", [128, 18, 512], BF16, es_l1)
        exb = self.exb if not lite else self.exb2
        exk = "exb" if not lite else "exb2"
        nblk = 5 if not lite else 4
        W = I["w_in_cd"]
        with contextlib.ExitStack() as es:
            h1 = self.sb("h1r", [128, 16, NTOK], BF16, es)
            with contextlib.ExitStack() as es2:
                xTb = self.sb("n_xTb", [128, 16, 512], F32, es2)
                sq = self.sb("n_sq", [128, 16, 512], BF16, es2)
                tmp = self.sb("n_tmp", [128, 2, 512], F32, es2)
                r1 = self.sb("n_r1", [128, 512], F32, es2)
                rstd = self.sb("n_rstd", [128, 512], F32, es2)
                for b in range(nblk):
                    c0 = 512 * b
                    nt = min(512, NTOK - c0)
                    m = 1 if c0 >= HALF else 0
                    DMA(S, "sp", xTb[:, :, 0:nt], self.xTc[:, c0:c0 + nt].rearrange("(k p) t -> p k t", p=128),
                        [(self.xk, k, b) for k in range(16)], ["n_xTb"])

                    def out_fn(kc, c0=c0, nt=nt, b=b):
                        return h1[:, kc, c0:c0 + nt], ("h1r", b)
                    self._norm_multi(xTb, ["n_xTb"], nt, 1, 0, m, sq, tmp, rstd, r1, out_fn, 7)
            self.barrier()
            wts = [self.sb(f"w1_{i}", [128, 16, 512], BF16, es) for i in range(2)]
            stg = [self.sb(f"stg{i}", [128, NTOK], F32, es) for i in range(2)]
            wv = lambda a, b_: W[:, a:b_].rearrange("(k p) n -> p k n", p=128)
            DMA(S, "pool", wts[0][:], wv(0, 512), [], [("w1", 0)])
            e = 0
            for t in (range(16) if not lite else (0, 15)):
                ps, pk = self.bank(4 + e % 4)
                for kc in range(16):
                    MM(S, ps[:, :], h1[:, kc, t * 128:(t + 1) * 128], wts[0][:, kc, :], kc == 0, kc == 15, [("w1", 0), ("h1r", t // 4)], [pk])
                ui = 1 + t if not lite else (17 if t == 0 else 0)
                COPY(S, "act" if e % 2 == 0 else "dve", self.upool[:, ui, :], ps[:, :], [pk], [("upool", ui)])
                e += 1
            jobs = []
            for j in range(4):
                jobs.append((1, j * 128, 128, j * 128, 4))
            jobs += [(2, 0, 128, 512, 4), (2, 128, 128, 640, 4), (2, 256, 128, 768, 5), (2, 384, 128, 896, 5)]
            jobs += [(3, 0, 128, 1024, 5), (3, 128, 128, 1152, 5), (3, 256, 64, 1280, 5), (3, 320, 64, 1344, 5)]
            loaded = {}
            n = 0
            for (ti, co, M, r0, ntb) in jobs:
                if lite and r0 < 768:
                    continue
                ntb = min(ntb, nblk)
                if ti not in loaded:
                    w_, wk = wts[ti % 2], ("w1", ti % 2)
                    if ti == 1:
                        DMA(S, "pool", w_[:], wv(512, 1024), [], [wk])
                    elif ti == 2:
                        DMA(S, "pool", w_[:], wv(1024, 1536), [], [wk])
                    else:
                        DMA(S, "pool", w_[:, :, 0:320], wv(1536, 1856), [], [wk])
                        for q4 in range(4):
                            src0 = 1792 + ROPE_PERM[q4 * 16]
                            DMA(S, "pool", w_[:, :, 320 + q4 * 16:336 + q4 * 16], wv(src0, src0 + 16), [], [wk])
                    loaded[ti] = True
                w_, wk = wts[ti % 2], ("w1", ti % 2)
                st, sk = stg[n % 2], ("stg", n % 2)
                n += 1
                for b in range(ntb):
                    c0 = 512 * b
                    nt = min(512, NTOK - c0)
                    ps, pk = self.bank(e % 4)
                    for kc in range(16):
                        MM(S, ps[0:M, 0:nt], w_[:, kc, co:co + M], h1[:, kc, c0:c0 + nt], kc == 0, kc == 15, [wk, ("h1r", b)], [pk])
                    COPY(S, "act" if e % 2 == 0 else "dve", st[0:M, c0:c0 + nt], ps[0:M, 0:nt], [pk], [sk])
                    e += 1
                ncol = HALF if ntb == 4 else NTOK
                DMA(S, "sp", self.l1raw[r0:r0 + M, 0:ncol], st[0:M, 0:ncol], [sk], [("l1raw", r0 // 128)])
        self.barrier()
        with contextlib.ExitStack() as es:
            cqb = self.sb("cqb", [128, 6, 512], F32, es)
            ckb = self.sb("ckb", [128, 4, 512], F32, es)
            krb = self.sb("krb", [64, 2, 512], F32, es)
            sq6 = self.sb("sq6", [128, 6, 512], BF16, es)
            r1 = self.sb("p_r1", [128, 512], F32, es)
            rstd = self.sb("p_rstd", [128, 512], F32, es)
            ckn = self.sb("ckn", [128, 4, 512], BF16, es)
            kro = self.sb("kro", [64, 512], BF16, es)
            t1 = self.sb("rt1", [64, 512], F32, es)
            t2 = self.sb("rt2", [64, 512], F32, es)
            rope = self.sb("rope_sb", [64, 2, HALF], F32, es)
            DMA(S, "sp", rope[:], (I["rope"] if not lite else I["rope2"]).rearrange("a p t -> p a t"), [], ["rope"])
            rawk = [("l1raw", i) for i in range(11)]
            for b in range(nblk):
                c0 = 512 * b
                nt = min(512, NTOK - c0)
                for (nch, buf, bk, r0, dim, gcol) in ((6, cqb, "cqb", 0, 768, self.V_GQ), (4, ckb, "ckb", 768, 512, self.V_GKV)):
                    if nch == 6 and (b == 4 or lite):
                        continue
                    DMA(S, "sp", buf[:, :, 0:nt], self.l1raw[r0:r0 + nch * 128, c0:c0 + nt].rearrange("(k p) t -> p k t", p=128), rawk, [bk])
                    ACT(S, sq6[:, 0:nch, 0:nt], buf[:, :, 0:nt], AF.Square, [bk], ["sq6"])
                    ps, pk = self.bank(6)
                    for kc in range(nch):
                        MM(S, ps[:, 0:nt], self.ones_b[:], sq6[:, kc, 0:nt], kc == 0, kc == nch - 1, ["ones_b", "sq6"], [pk])
                    ACT(S, r1[:, 0:nt], ps[:, 0:nt], AF.Sqrt, [pk, "epsb"], ["p_r1"], bias=self.epsb[:], scale=1.0 / dim)
                    RECIP(S, rstd[:, 0:nt], r1[:, 0:nt], ["p_r1"], ["p_rstd"])
                    for kc in range(nch):
                        if nch == 6:
                            o, ok = self.cqn[:, kc, c0:c0 + nt], ("cqn", b)
                        else:
                            o, ok = ckn[:, kc, 0:nt], "ckn"
                        STT(S, "dve", o, buf[:, kc, 0:nt], self.vecs[:, gcol + kc:gcol + kc + 1], rstd[:, 0:nt], ALU.mult, ALU.mult,
                            [bk, "vecs", "p_rstd"], [ok])
                    if nch == 4:
                        DMA(S, "sp", exb[0:512, c0:c0 + nt].rearrange("(k p) t -> p k t", p=128), ckn[:, :, 0:nt], ["ckn"], [(exk, b)])
                DMA(S, "sp", krb[:, :, 0:nt], self.l1raw[1280:1408, c0:c0 + nt].rearrange("(a p) t -> p a t", p=64), rawk, ["krb"])
                if b < 4:
                    TT(S, "dve", t1[:, 0:nt], krb[:, 0, 0:nt], rope[:, 0, c0:c0 + nt], ALU.mult, ["krb", "rope"], ["rt1"])
                    TT(S, "pool", t2[:, 0:nt], krb[:, 1, 0:nt], rope[:, 1, c0:c0 + nt], ALU.mult, ["krb", "rope"], ["rt2"])
                    TT(S, "dve", kro[:, 0:nt], t1[:, 0:nt], t2[:, 0:nt], ALU.add, ["rt1", "rt2"], ["kro"])
                else:
                    COPY(S, "dve", kro[:, 0:nt], krb[:, 0, 0:nt], ["krb"], ["kro"])
                DMA(S, "sp", exb[512:576, c0:c0 + nt], kro[:, 0:nt], ["kro"], [(exk, b)])
            if not lite and self.mode != "F":
                DMA(S, "sp", exb[576:704, 0:512], self.upool[:, 1, :], [("upool", 1)], [("exb", 5)])
                DMA(S, "sp", exb[576:704, 512:1024], self.upool[:, 16, :], [("upool", 16)], [("exb", 5)])
        self.barrier()

    def l1_part2(self, es_l1):
        S, I = self.S, self.I
        fused = self.mode == "F"
        if fused:
            G = [self.exb, self.exb2]
            gk = [[("exb", b) for b in range(4)], [("exb2", b) for b in range(4)]]
        else:
            G = [self.exg[0], self.exg[1]]
            gk = [[("exg", 0)], [("exg", 0)]]
        self.krr = self.sb("krr", [128, SEQ + CTX], BF16, es_l1)
        S.add("pool", lambda e: e.memset(self.krr[:], 0.0), [], ["krr"])
        for r in range(2):
            DMA(S, "sp", self.krr[0:64, r * HALF:(r + 1) * HALF], G[r][512:576, 0:HALF], gk[r], ["krr"])
        DMA(S, "sp", self.krr[0:64, SEQ:SEQ + CTX], self.exb[512:576, HALF:NTOK], [("exb", 4)], ["krr"])
        if not fused:
            DMA(S, "sp", self.upool[:, 0, :], G[0][576:704, 512:1024], gk[0], [("upool", 0)])
            DMA(S, "sp", self.upool[:, 17, :], G[1][576:704, 0:512], gk[1], [("upool", 17)])
        with contextlib.ExitStack() as es:
            ckv = self.sb("ckv_all", [128, 4, SEQ + CTX], BF16, es)
            for r in range(2):
                DMA(S, "sp", ckv[:, :, r * HALF:(r + 1) * HALF], G[r][0:512, 0:HALF].rearrange("(k p) t -> p k t", p=128), gk[r], [("ckv", r)])
            DMA(S, "sp", ckv[:, :, SEQ:SEQ + CTX], self.exb[0:512, HALF:NTOK].rearrange("(k p) t -> p k t", p=128), [("exb", 4)], [("ckv", 2)])
            with contextlib.ExitStack() as es1:
                PT = self.sb("PT", [128, 3, 4, 3, 128], BF16, es1)
                Wp = self.sb("Wpool", [128, 4, 128], BF16, es1)
                pu = [self.sb(f"pu{i}", [128, 512], BF16, es1) for i in range(2)]
                zst = [self.sb(f"pz{i}", [128, 512], BF16, es1) for i in range(2)]
                DMA(S, "sp", PT[:].rearrange("p a g r t -> p (a g r t)"), I["ptab"].rearrange("p a g r t -> p (a g r t)"), [], ["PT"])
                DMA(S, "pool", Wp[:], I["w_pool"].rearrange("g c d -> c g d"), [], ["Wpool"])
                n = 0
                for tb in range(4):
                    for g in range(4):
                        ps, pk = self.bank(n % 2)
                        for tt in range(4):
                            ti = tb * 4 + tt
                            cls = 0 if ti == 0 else (2 if ti == 15 else 1)
                            for rel in range(3):
                                MM(S, ps[:, tt * 128:(tt + 1) * 128], self.upool[:, ti + rel, g * 128:(g + 1) * 128], PT[:, cls, g, rel, :],
                                   rel == 0, rel == 2, [("upool", ti + rel), "PT"], [pk])
                        p_, pk_ = pu[n % 2], ("pu", n % 2)
                        COPY(S, "act", p_[:, :], ps[:, :], [pk], [pk_])
                        ps2, pk2 = self.bank(2 + n % 2)
                        MM(S, ps2[:, :], Wp[:, g, :], p_[:, :], True, True, ["Wpool", pk_], [pk2])
                        z_, zk = zst[n % 2], ("pz", n % 2)
                        S.add("dve", lambda e, z_=z_, ps2=ps2, g=g: e.tensor_scalar(out=z_[:, :], in0=ps2[:, :], scalar1=self.vecs[:, self.V_PS + g:self.V_PS + g + 1],
                                                                             scalar2=None, op0=ALU.mult), [pk2, "vecs"], [zk])
                        DMA(S, "sp", self.mixT[g * 128:(g + 1) * 128, tb * 512:(tb + 1) * 512], z_[:, :], [zk], [("mixT", g, tb)])
                        n += 1
            self.barrier()
            with contextlib.ExitStack() as es1:
                Wq = self.sb("Wuq", [128, 6, 2304], BF16, es1)
                Wqs = self.sb("Wuqs", [128, 6, 12, 64], BF16, es1)
                rope = self.sb("rope_sb2", [64, 2, HALF], F32, es1)
                qnst = [self.sb(f"qnst{i}", [128, HALF], BF16, es1) for i in range(2)]
                qrst = [self.sb(f"qrst{i}", [64, HALF], BF16, es1) for i in range(2)]
                t1 = [self.sb(f"qt1_{i}", [64, 512], F32, es1) for i in range(2)]
                t2 = [self.sb(f"qt2_{i}", [64, 512], F32, es1) for i in range(2)]
                DMA(S, "sp", rope[:], I["rope"].rearrange("a p t -> p a t"), [], ["rope2"])
                DMA(S, "pool", Wq[:], I["w_uq"].rearrange("(k p) n -> p k n", p=128), [], ["Wuq"])
                wq4 = I["w_uq"].rearrange("(k p) (h c) -> k p h c", p=128, c=192)
                for k in range(6):
                    for q4 in range(4):
                        src0 = 128 + ROPE_PERM[q4 * 16]
                        DMA(S, "pool", Wqs[:, k, :, q4 * 16:q4 * 16 + 16], wq4[k, :, :, src0:src0 + 16], [], ["Wuqs"])
                e = 0
                for h in range(12):
                    qn_, qnk = qnst[h % 2], ("qnst", h % 2)
                    qr_, qrk = qrst[h % 2], ("qrst", h % 2)
                    for b in range(4):
                        c0 = 512 * b
                        ps, pk = self.bank(e % 2)
                        for kc in range(6):
                            MM(S, ps[:, :], Wq[:, kc, h * 192:h * 192 + 128], self.cqn[:, kc, c0:c0 + 512], kc == 0, kc == 5, ["Wuq", ("cqn", b)], [pk])
                        COPY(S, "act", qn_[:, c0:c0 + 512], ps[:, :], [pk], [qnk])
                        pr, prk = self.bank(2 + e % 2)
                        pw, pwk = self.bank(4 + e % 2)
                        for kc in range(6):
                            MM(S, pr[0:64, :], Wq[:, kc, h * 192 + 128:h * 192 + 192], self.cqn[:, kc, c0:c0 + 512], kc == 0, kc == 5, ["Wuq", ("cqn", b)], [prk])
                        for kc in range(6):
                            MM(S, pw[0:64, :], Wqs[:, kc, h, :], self.cqn[:, kc, c0:c0 + 512], kc == 0, kc == 5, ["Wuqs", ("cqn", b)], [pwk])
                        a1, a1k = t1[e % 2], ("qt1", e % 2)
                        a2, a2k = t2[e % 2], ("qt2", e % 2)
                        TT(S, "dve", a1[:, :], pr[0:64, :], rope[:, 0, c0:c0 + 512], ALU.mult, [prk, "rope2"], [a1k])
                        TT(S, "dve", a2[:, :], pw[0:64, :], rope[:, 1, c0:c0 + 512], ALU.mult, [pwk, "rope2"], [a2k])
                        TT(S, "pool", qr_[:, c0:c0 + 512], a1[:, :], a2[:, :], ALU.add, [a1k, a2k], [qrk])
                        e += 1
                    DMA(S, "sp", self.qn_s[h * 128:(h + 1) * 128, :], qn_[:, :], [qnk], [("qn_s", h)])
                    DMA(S, "sp", self.qr_s[h * 64:(h + 1) * 64, :], qr_[:, :], [qrk], [("qr_s", h)])
            self.barrier()
            with contextlib.ExitStack() as es1:
                NK = SEQ + CTX
                Wk = self.sb("Wukv", [128, 4, 3072], BF16, es1)
                knst = [self.sb(f"knst{i}", [128, NK], BF16, es1) for i in range(2)]
                vst = [self.sb(f"v1st{i}", [128, 1536], BF16, es1) for i in range(2)]
                DMA(S, "pool", Wk[:], I["w_ukv"].rearrange("(k p) n -> p k n", p=128), [], ["Wukv"])
                Wk4 = Wk[:].rearrange("p k (h t d) -> p k h t d", t=2, d=128)
                e = 0
                for h in range(12):
                    kn_, knk = knst[h % 2], ("knst", h % 2)
                    for b in range(9):
                        c0 = 512 * b
                        nt = min(512, NK - c0)
                        ps, pk = self.bank(e % 4)
                        for kc in range(4):
                            MM(S, ps[:, 0:nt], Wk[:, kc, h * 256:h * 256 + 128], ckv[:, kc, c0:c0 + nt], kc == 0, kc == 3, ["Wukv", ("ckv", min(c0 // HALF, 2))], [pk])
                        COPY(S, "act" if e % 2 == 0 else "dve", kn_[:, c0:c0 + nt], ps[:, 0:nt], [pk], [knk])
                        e += 1
                    DMA(S, "sp", self.kn_s[h * 128:(h + 1) * 128, :], kn_[:, :], [knk], [("kn_s", h)])
                for t in range(34):
                    v_, vk_ = vst[t % 2], ("v1st", t % 2)
                    for hg in range(3):
                        ps, pk = self.bank(4 + e % 4)
                        for kc in range(4):
                            MM(S, ps[:, :].rearrange("p (h d) -> p h d", d=128), ckv[:, kc, t * 128:(t + 1) * 128], Wk4[:, kc, hg * 4:hg * 4 + 4, 1, :],
                               kc == 0, kc == 3, ["Wukv", ("ckv", min(t // 16, 2))], [pk])
                        COPY(S, "act" if e % 2 == 0 else "dve", v_[:, hg * 512:(hg + 1) * 512], ps[:, :], [pk], [vk_])
                        e += 1
                    DMA(S, "sp", self.v1_s[t * 128:(t + 1) * 128, :], v_[:, :], [vk_], [("v1_s", t)])
        self.barrier()

    def l1_attn(self):
        S = self.S
        NK = SEQ + CTX
        scale = 192.0 ** -0.5
        with contextlib.ExitStack() as es:
            kn = [self.sb(f"a_kn{i}", [128, NK], BF16, es) for i in range(2)]
            vh = [self.sb(f"a_v{i}", [128, 34, 128], BF16, es) for i in range(2)]
            qn = [self.sb(f"a_qn{i}", [128, HALF], BF16, es) for i in range(2)]
            qr = [self.sb(f"a_qr{i}", [128, HALF], BF16, es) for i in range(2)]
            pT = [self.sb(f"a_pT{i}", [128, 512], BF16, es) for i in range(4)]
            acc = [self.sb(f"a_acc{i}", [128, 512], F32, es) for i in range(4)]
            rd = [self.sb(f"a_rd{i}", [128, 512], F32, es) for i in range(2)]
            ost = [self.sb(f"a_o{i}", [128, 512], BF16, es) for i in range(2)]
            ones_f = self.sb("ones_f", [128, 128], F32, es)
            S.add("pool", lambda e: e.memset(ones_f[:], 1.0), [], ["ones_f"])
            for i in range(2):
                S.add("pool", lambda e, i=i: e.memset(qr[i][:], 0.0), [], [("a_qr", i)])
            allv = [("v1_s", t) for t in range(34)]

            def load_head(h):
                i2 = h % 2
                DMA(S, "sp", kn[i2][:, :], self.kn_s[h * 128:(h + 1) * 128, :], [("kn_s", h)], [("a_kn", i2)])
                DMA(S, "sp", vh[i2][:, :, :], self.v1_s[:, h * 128:(h + 1) * 128].rearrange("(t p) d -> p t d", p=128), allv, [("a_v", i2)])
                DMA(S, "sp", qn[i2][:, :], self.qn_s[h * 128:(h + 1) * 128, :], [("qn_s", h)], [("a_qn", i2)])
                DMA(S, "sp", qr[i2][0:64, :], self.qr_s[h * 64:(h + 1) * 64, :], [("qr_s", h)], [("a_qr", i2)])

            def front(e, h, qb, kc):
                i2 = h % 2
                c0 = 512 * qb
                pS, pSk = self.bank(e % 4)
                MM(S, pS[:, :], kn[i2][:, kc * 128:(kc + 1) * 128], qn[i2][:, c0:c0 + 512], True, False, [("a_kn", i2), ("a_qn", i2)], [pSk])
                MM(S, pS[:, :], self.krr[:, kc * 128:(kc + 1) * 128], qr[i2][:, c0:c0 + 512], False, True, ["krr", ("a_qr", i2)], [pSk])
                ACT(S, pT[e % 4][:, :], pS[:, :], AF.Exp, [pSk], [("a_pT", e % 4)], scale=scale)

            def back(e, h, qb, kc):
                i2 = h % 2
                c0 = 512 * qb
                n = h * 4 + qb
                p_, pk_ = pT[e % 4], ("a_pT", e % 4)
                pO, pOk = self.bank(4 + 2 * (n % 2))
                MM(S, pO[:, :], vh[i2][:, kc, :], p_[:, :], kc == 0, kc == 33, [("a_v", i2), pk_], [pOk])
                a_ = acc[2 * (n % 2) + kc % 2]
                ak = ("a_acc", 2 * (n % 2) + kc % 2)
                if kc < 2:
                    COPY(S, "dve", a_[:, :], p_[:, :], [pk_], [ak])
                else:
                    TT(S, "dve", a_[:, :], a_[:, :], p_[:, :], ALU.add, [ak, pk_], [ak])
                if kc == 33:
                    pD, pDk = self.bank(5 + 2 * (n % 2))
                    for j in range(2):
                        MM(S, pD[:, :], ones_f[:], acc[2 * (n % 2) + j][:, :], j == 0, j == 1, ["ones_f", ("a_acc", 2 * (n % 2) + j)], [pDk])
                    RECIP(S, rd[n % 2][:, :], pD[:, :], [pDk], [("a_rd", n % 2)])
                    TT(S, "dve", ost[n % 2][:, :], pO[:, :], rd[n % 2][:, :], ALU.mult, [pOk, ("a_rd", n % 2)], [("a_o", n % 2)])
                    DMA(S, "sp", self.mixT[512 + h * 128:640 + h * 128, c0:c0 + 512], ost[n % 2][:, :], [("a_o", n % 2)], [("mixT", 4 + h, qb)])

            items = [(h, qb, kc) for h in range(12) for qb in range(4) for kc in range(34)]
            LAG = 2
            for e in range(len(items) + LAG):
                if e < len(items):
                    h, qb, kc = items[e]
                    if qb == 0 and kc == 0:
                        load_head(h)
                    front(e, h, qb, kc)
                if e >= LAG:
                    back(e - LAG, *items[e - LAG])

    def final_norm(self):
        S = self.S
        with contextlib.ExitStack() as es:
            xTb = self.sb("z_xTb", [128, 16, 512], F32, es)
            sq = self.sb("z_sq", [128, 16, 512], BF16, es)
            yT = self.sb("z_yT", [128, 16, 512], F32, es)
            r1 = self.sb("z_r1", [128, 512], F32, es)
            rstd = self.sb("z_rstd", [128, 512], F32, es)
            ot = [self.sb(f"z_ot{i}", [128, D], F32, es) for i in range(2)]
            e = 0
            n = 0
            for b in range(4):
                c0 = 512 * b
                DMA(S, "sp", xTb[:], self.xTc[:, c0:c0 + 512].rearrange("(k p) t -> p k t", p=128), [(self.xk, k, b) for k in range(16)], ["z_xTb"])
                ACT(S, sq[:], xTb[:], AF.Square, ["z_xTb"], ["z_sq"])
                ps, pk = self.bank(7)
                for kc in range(16):
                    MM(S, ps[:, :], self.ones_b[:], sq[:, kc, :], kc == 0, kc == 15, ["ones_b", "z_sq"], [pk])
                ACT(S, r1[:], ps[:, :], AF.Sqrt, [pk, "epsb"], ["z_r1"], bias=self.epsb[:], scale=1.0 / D)
                RECIP(S, rstd[:], r1[:], ["z_r1"], ["z_rstd"])
                for kc in range(16):
                    STT(S, "dve", yT[:, kc, :], xTb[:, kc, :], self.vecs[:, self.V_FG + kc:self.V_FG + kc + 1], rstd[:], ALU.mult, ALU.mult,
                        ["z_xTb", "vecs", "z_rstd"], [("z_yT", kc)])
                for t in range(4):
                    o_, ok_ = ot[n % 2], ("z_ot", n % 2)
                    n += 1
                    for k4 in range(4):
                        pt, ptk = self.bank(e % 4)
                        for j in range(4):
                            kc = k4 * 4 + j
                            TR(S, pt[:, j * 128:(j + 1) * 128], yT[:, kc, t * 128:(t + 1) * 128], self.ident_f[:], [("z_yT", kc), "ident_f"], [ptk])
                        COPY(S, "act" if e % 2 == 0 else "dve", o_[:, k4 * 512:(k4 + 1) * 512], pt[:, :], [ptk], [ok_])
                        e += 1
                    DMA(S, "sp", self.out[c0 + t * 128:c0 + (t + 1) * 128, :], o_[:, :], [ok_], [("out", b, t)])

    def outproj(self, l, W, ntok):
        S = self.S
        nb = (ntok + 511) // 512
        with contextlib.ExitStack() as es:
            mixr = self.sb("mixr", [128, 16, NTOK], BF16, es)
            wts = [self.sb(f"wo{i}", [128, 16, 512], BF16, es) for i in range(2)]
            xres = [self.sb(f"xres{i}", [128, NTOK], F32, es) for i in range(2)]
            for b in range(nb):
                c0 = 512 * b
                nt = min(512, ntok - c0)
                DMA(S, "sp", mixr[:, :, c0:c0 + nt], self.mixT[:, c0:c0 + nt].rearrange("(k p) t -> p k t", p=128),
                    [("mixT", k, b) for k in range(16)], [("mixr", b)])
            e = 0
            for cg in range(4):
                wt, wk = wts[cg % 2], ("wo", cg % 2)
                DMA(S, "pool", wt[:], W[:, cg * 512:(cg + 1) * 512].rearrange("(k p) n -> p k n", p=128), [], [wk])
                for j in range(4):
                    dc = cg * 4 + j
                    xr, xk = xres[dc % 2], ("xres", dc % 2)
                    DMA(S, "sp", xr[:, 0:ntok], self.xTc[dc * 128:(dc + 1) * 128, 0:ntok], [(self.xk, dc, b) for b in range(nb)], [xk])
                    for b in range(nb):
                        c0 = 512 * b
                        nt = min(512, ntok - c0)
                        m = 1 if c0 >= HALF else 0
                        ps, pk = self.bank(e % 4)
                        for kc in range(16):
                            MM(S, ps[:, 0:nt], wt[:, kc, j * 128:(j + 1) * 128], mixr[:, kc, c0:c0 + nt], kc == 0, kc == 15, [wk, ("mixr", b)], [pk])
                        STT(S, "dve", xr[:, c0:c0 + nt], ps[:, 0:nt], self.mcol(l, 2, dc, m), xr[:, c0:c0 + nt], ALU.mult, ALU.add,
                            [pk, xk, ("modT", l)], [xk])
                        e += 1
                    DMA(S, "sp", self.xTc[dc * 128:(dc + 1) * 128, 0:ntok], xr[:, 0:ntok], [xk], [(self.xk, dc, b) for b in range(nb)])

    def ffn(self, l, ntok):
        S, I = self.S, self.I
        nb = (ntok + 511) // 512
        with contextlib.ExitStack() as es:
            with contextlib.ExitStack() as es1:
                h2 = self.sb("h2r", [128, 16, NTOK], BF16, es1)
                with contextlib.ExitStack() as es2:
                    xTb = self.sb("f_xTb", [128, 16, 512], F32, es2)
                    sq = self.sb("f_sq", [128, 16, 512], BF16, es2)
                    tmp = self.sb("f_tmp", [128, 2, 512], F32, es2)
                    r1 = self.sb("f_r1", [128, 512], F32, es2)
                    rstd = self.sb("f_rstd", [128, 512], F32, es2)
                    for b in range(nb):
                        c0 = 512 * b
                        nt = min(512, ntok - c0)
                        m = 1 if c0 >= HALF else 0
                        DMA(S, "sp", xTb[:, :, 0:nt], self.xTc[:, c0:c0 + nt].rearrange("(k p) t -> p k t", p=128),
                            [(self.xk, k, b) for k in range(16)], ["f_xTb"])

                        def out_fn(kc, c0=c0, nt=nt, b=b):
                            return h2[:, kc, c0:c0 + nt], ("h2r", b)
                        self._norm_multi(xTb, ["f_xTb"], nt, l, 1, m, sq, tmp, rstd, r1, out_fn, 7)
                    self.barrier()
                wg = [self.sb(f"wg{i}", [128, 16, 512], BF16, es1) for i in range(2)]
                wu = [self.sb(f"wu{i}", [128, 16, 512], BF16, es1) for i in range(2)]
                sg = [self.sb(f"sg{i}", [128, 512], F32, es1) for i in range(2)]
                ast = [self.sb(f"ast{i}", [128, NTOK], BF16, es1) for i in range(2)]
                e = 0
                for cg in range(11):
                    g_, gk = wg[cg % 2], ("wg", cg % 2)
                    u_, uk = wu[cg % 2], ("wu", cg % 2)
                    DMA(S, "pool", g_[:], I["w_ffn_gate"][l, :, cg * 512:(cg + 1) * 512].rearrange("(k p) n -> p k n", p=128), [], [gk])
                    DMA(S, "pool", u_[:], I["w_ffn_up"][l, :, cg * 512:(cg + 1) * 512].rearrange("(k p) n -> p k n", p=128), [], [uk])
                    for j in range(4):
                        fc = cg * 4 + j
                        a_, ak = ast[fc % 2], ("ast", fc % 2)
                        for b in range(nb):
                            c0 = 512 * b
                            nt = min(512, ntok - c0)
                            pg, pgk = self.bank(2 * (e % 4))
                            pu, puk = self.bank(2 * (e % 4) + 1)
                            for kc in range(16):
                                MM(S, pg[:, 0:nt], g_[:, kc, j * 128:(j + 1) * 128], h2[:, kc, c0:c0 + nt], kc == 0, kc == 15, [gk, ("h2r", b)], [pgk])
                            for kc in range(16):
                                MM(S, pu[:, 0:nt], u_[:, kc, j * 128:(j + 1) * 128], h2[:, kc, c0:c0 + nt], kc == 0, kc == 15, [uk, ("h2r", b)], [puk])
                            s_, sk = sg[e % 2], ("sg", e % 2)
                            ACT(S, s_[:, 0:nt], pg[:, 0:nt], AF.Silu, [pgk], [sk])
                            TT(S, "dve", a_[:, c0:c0 + nt], s_[:, 0:nt], pu[:, 0:nt], ALU.mult, [sk, puk], [ak])
                            e += 1
                        DMA(S, "sp", self.aT[fc * 128:(fc + 1) * 128, 0:ntok], a_[:, 0:ntok], [ak], [("aT", fc)])
            self.barrier()
            with contextlib.ExitStack() as es1:
                aTr = self.sb("aTr", [128, 44, 768], BF16, es1)
                wd = [self.sb(f"wd{i}", [128, 44, 256], BF16, es1) for i in range(2)]
                xres = [self.sb(f"dxres{i}", [128, 768], F32, es1) for i in range(2)]
                allA = [("aT", fc) for fc in range(44)]
                nsb = (ntok + 767) // 768
                n = 0
                e = 0
                for sbk in range(nsb):
                    c0 = 768 * sbk
                    nt = min(768, ntok - c0)
                    DMA(S, "sp", aTr[:, :, 0:nt], self.aT[:, c0:c0 + nt].rearrange("(k p) t -> p k t", p=128), allA, ["aTr"])
                    segs = [(0, min(512, nt))] + ([(512, nt - 512)] if nt > 512 else [])
                    blks = sorted(set([(c0 + a) // 512 for a, _ in segs] + [(c0 + a + w - 1) // 512 for a, w in segs]))
                    for cg in range(8):
                        w_, wk = wd[n % 2], ("wd", n % 2)
                        n += 1
                        DMA(S, "pool", w_[:], I["w_ffn_down"][l, :, cg * 256:(cg + 1) * 256].rearrange("(k p) n -> p k n", p=128), [], [wk])
                        for j in range(2):
                            dc = cg * 2 + j
                            xr, xk = xres[dc % 2], ("dxres", dc % 2)
                            xkeys = [(self.xk, dc, b) for b in blks]
                            DMA(S, "sp", xr[:, 0:nt], self.xTc[dc * 128:(dc + 1) * 128, c0:c0 + nt], xkeys, [xk])
                            pS = self.pp[e % 4]
                            pks = [("ps", 2 * (e % 4)), ("ps", 2 * (e % 4) + 1)]
                            for kc in range(44):
                                for (a, w) in segs:
                                    MM(S, pS[:, a:a + w], w_[:, kc, j * 128:(j + 1) * 128], aTr[:, kc, a:a + w], kc == 0, kc == 43, [wk, "aTr"],
                                       [pks[0] if a == 0 else pks[1]])
                            for (a, w) in segs:
                                m = 1 if c0 + a >= HALF else 0
                                STT(S, "dve", xr[:, a:a + w], pS[:, a:a + w], self.mcol(l, 5, dc, m), xr[:, a:a + w], ALU.mult, ALU.add,
                                    [pks[0] if a == 0 else pks[1], xk, ("modT", l)], [xk])
                            e += 1
                            DMA(S, "sp", self.xTc[dc * 128:(dc + 1) * 128, c0:c0 + nt], xr[:, 0:nt], [xk], xkeys)


def make_inputs(core, inp):
    b, s = core // 2, core % 2
    f = np.float32
    vecs = np.zeros((128, 400), f)
    c2 = np.stack([_fm(inp["c"][b]), _fm(inp["c_ctx"])], axis=-1)
    vecs[:, 0:32] = c2.reshape(128, 32)
    for l in range(2):
        vecs[:, 32 + 16 * l:48 + 16 * l] = _fm(inp["norm1_g"][l])
        vecs[:, 64 + 16 * l:80 + 16 * l] = _fm(inp["norm2_g"][l])
        vecs[:, 112 + 96 * l:208 + 96 * l] = _fm(inp["b_mod"][l])
    vecs[:, 96:112] = _fm(inp["final_g"])
    vecs[:, 304:308] = _fm(inp["pool_scale"][0])
    vecs[:, 308:314] = _fm(inp["mla_gq"][0])
    vecs[:, 314:318] = _fm(inp["mla_gkv"][0])
    xb = inp["x"][b]
    own0 = HALF * s

    def halo(s_):
        o = HALF * s_
        hb = xb[o - 256:o] if s_ == 1 else xb[0:256]
        ha = xb[o + HALF:o + HALF + 256] if s_ == 0 else xb[SEQ - 256:SEQ]
        return np.ascontiguousarray(np.concatenate([hb, ha], 0))
    rpb = np.asarray(inp["na_rpb"][0], f)
    return {
        "x_halo2": halo(1 - s),
        "csn2": _dft_tables(1 - s),
        "nabias2": _na_class_tables(rpb, 1 - s),
        "rope2": _rope_tables(1 - s),
        "x_own": np.ascontiguousarray(xb[own0:own0 + HALF]),
        "x_par": np.ascontiguousarray(xb[HALF * (1 - s):HALF * (1 - s) + HALF]),
        "x_halo": halo(s),
        "ctxb": np.ascontiguousarray(inp["ctx"][b]),
        "vecs": vecs,
        "ident": np.eye(128, dtype=f),
        "w_mod": inp["w_mod"],
        "w_in_ab": inp["w_in_ab"][0],
        "w_four": inp["w_four"][0],
        "w_out_ab": inp["w_out_ab"][0],
        "w_ffn_gate": inp["w_ffn_gate"],
        "w_ffn_up": inp["w_ffn_up"],
        "w_ffn_down": inp["w_ffn_down"],
        "csn": _dft_tables(s),
        "csx": _dft_ctx(),
        "csc": _dft_ch(),
        "nabias": _na_class_tables(np.asarray(inp["na_rpb"][0], f), s),
        "w_in_cd": inp["w_in_cd"][0],
        "w_pool": inp["w_pool"][0],
        "w_uq": inp["w_uq"][0],
        "w_ukv": inp["w_ukv"][0],
        "w_out_cd": inp["w_out_cd"][0],
        "rope": _rope_tables(s),
        "ptab": _pool_tables(s),
    }


def run_all(inp, cores):
    inp = {k: np.asarray(v) for k, v in inp.items()}
    full = {c: make_inputs(c, inp) for c in cores}
    ids = list(range(len(cores)))
    ncA = Prog("A", 3).build()
    namesA = list(_NAMES["A"])
    rA = run_bass_kernel_spmd(ncA, [{k: full[c][k] for k in namesA} for c in cores], core_ids=ids)
    xT = {c: np.asarray(r["xT_out"]) for c, r in zip(cores, rA.results)}
    ncP = Prog("B", 1).build()
    namesP = list(_NAMES["B"])
    dummy = np.zeros((2, 704, NTOK), ml_dtypes.bfloat16)
    mapsP = []
    for c in cores:
        d = dict(full[c])
        d["xT_in"] = xT[c]
        d["exg"] = dummy
        mapsP.append({k: d[k] for k in namesP})
    rP = run_bass_kernel_spmd(ncP, mapsP, core_ids=ids)
    ex = {c: np.asarray(r["ex_out"]) for c, r in zip(cores, rP.results)}
    ncB = Prog("B", 6).build()
    namesB = list(_NAMES["B"])
    mapsB = []
    for c in cores:
        d = dict(full[c])
        d["xT_in"] = xT[c]
        d["exg"] = np.ascontiguousarray(np.stack([ex[c - c % 2], ex[c - c % 2 + 1]], 0))
        mapsB.append({k: d[k] for k in namesB})
    rB = run_bass_kernel_spmd(ncB, mapsB, core_ids=ids)
    return {c: np.asarray(r["out"]) for c, r in zip(cores, rB.results)}


_NAMES = {}


def run_b(inp, cores, xT, ex, stage):
    inp = {k: np.asarray(v) for k, v in inp.items()}
    ncB = Prog("B", stage, dbg=True).build()
    namesB = list(_NAMES["B"])
    mapsB = []
    for c in cores:
        d = make_inputs(c, inp)
        d["xT_in"] = xT[c]
        d["exg"] = np.ascontiguousarray(np.stack([ex[c - c % 2], ex[c - c % 2 + 1]], 0))
        mapsB.append({k: d[k] for k in namesB})
    rB = run_bass_kernel_spmd(ncB, mapsB, core_ids=list(range(len(cores))))
    return rB.results


def run_fused(inp, cores):
    inp = {k: np.asarray(v) for k, v in inp.items()}
    nc = Prog("F", 6).build()
    names = list(_NAMES["F"])
    maps = []
    for c in cores:
        d = make_inputs(c, inp)
        maps.append({k: d[k] for k in names})
    r = run_bass_kernel_spmd(nc, maps, core_ids=list(range(len(cores))))
    return {c: np.asarray(rr["out"]) for c, rr in zip(cores, r.results)}


def kernel(**inputs):
    cores = list(range(8))
    res = run_fused(inputs, cores)
    out = np.zeros((4, SEQ, D), np.float32)
    for c in cores:
        out[c // 2, HALF * (c % 2):HALF * (c % 2) + HALF] = res[c]
    return out
```

```python
import contextlib
import numpy as np
import ml_dtypes
import concourse.bass as bass
import concourse.mybir as mybir
from concourse.bass_utils import run_bass_kernel_spmd

F32 = mybir.dt.float32
BF16 = mybir.dt.bfloat16
AF = mybir.ActivationFunctionType
ALU = mybir.AluOpType

EPOCH = 4000
DMA_RING = 8

D = 2048
SEQ = 4096
HALF = 2048
CTX = 256
NTOK = HALF + CTX
NKV = NTOK + 512
DFF = 5632
NEG = -30000.0
EPS = 1e-6


class _Op:
    __slots__ = ("eng", "fn", "idx", "deps", "dma", "flag", "ring_slot", "ring_val")

    def __init__(self, eng, fn, dma):
        self.eng = eng
        self.fn = fn
        self.dma = dma
        self.deps = []
        self.flag = False


class Sched:
    ENGS = ("pe", "act", "dve", "pool", "sp")

    def __init__(self, nc):
        self.nc = nc
        self.ops = {e: [] for e in self.ENGS}
        self.state = {}
        self.ndma = {e: 0 for e in self.ENGS}
        self.bar = {e: [] for e in self.ENGS}

    def barrier(self):
        last = []
        for e in self.ENGS:
            ops = self.ops[e]
            if not ops:
                continue
            last.append(ops[-1])
            k = 0
            for op in reversed(ops):
                if op.dma:
                    last.append(op)
                    k += 1
                    if k >= DMA_RING:
                        break
        for e in self.ENGS:
            self.bar[e] = list(last)

    def add(self, eng, fn, reads=(), writes=(), dma=False):
        op = _Op(eng, fn, dma)
        op.idx = len(self.ops[eng])
        deps = []
        if self.bar[eng]:
            deps.extend(self.bar[eng])
            self.bar[eng] = []
        for b in reads:
            st = self.state.get(b)
            if st is None:
                st = self.state[b] = [None, {}, []]
            if st[0] is not None:
                deps.append(st[0])
            if dma:
                st[2].append(op)
            else:
                st[1][eng] = op
        for b in writes:
            st = self.state.get(b)
            if st is None:
                st = self.state[b] = [None, {}, []]
            if st[0] is not None:
                deps.append(st[0])
            for r in st[1].values():
                deps.append(r)
            deps.extend(st[2])
            st[0] = op
            st[1] = {}
            st[2] = []
        if dma:
            j = self.ndma[eng]
            self.ndma[eng] += 1
            op.ring_slot = j % DMA_RING
            op.ring_val = 16 * (j // DMA_RING + 1)
        seen = set()
        for d in deps:
            if d is op or id(d) in seen:
                continue
            seen.add(id(d))
            if d.eng == eng and not d.dma and eng == "pe":
                continue
            op.deps.append(d)
        self.ops[eng].append(op)
        return op

    def emit(self):
        nc = self.nc
        for e in self.ENGS:
            waited = {}
            waited_dma = set()
            for op in self.ops[e]:
                nd = []
                for d in op.deps:
                    if d.dma:
                        if id(d) in waited_dma:
                            continue
                        waited_dma.add(id(d))
                        nd.append(d)
                    else:
                        if waited.get(d.eng, -1) >= d.idx:
                            continue
                        waited[d.eng] = d.idx
                        d.flag = True
                        nd.append(d)
                op.deps = nd
        count = {}
        nsem = {}
        for e in self.ENGS:
            c = 0
            for op in self.ops[e]:
                if op.flag and not op.dma:
                    count[id(op)] = c
                    c += 1
            nsem[e] = (c + EPOCH - 1) // EPOCH
        with contextlib.ExitStack() as es:
            csem = {e: [es.enter_context(nc.semaphore(f"c_{e}_{k}")) for k in range(nsem[e])] for e in self.ENGS}
            dsem = {e: [es.enter_context(nc.semaphore(f"d_{e}_{k}")) for k in range(DMA_RING)]
                    for e in self.ENGS if self.ndma[e] > 0}
            block = es.enter_context(nc.Block())
            engobj = {"pe": block.tensor, "act": block.scalar, "dve": block.vector, "pool": block.gpsimd, "sp": block.sync}

            def make(e):
                def body(eng):
                    for op in self.ops[e]:
                        for d in op.deps:
                            if d.dma:
                                eng.wait_ge(dsem[d.eng][d.ring_slot], d.ring_val)
                            else:
                                c = count[id(d)]
                                eng.wait_ge(csem[d.eng][c // EPOCH], c % EPOCH + 1)
                        if op.dma:
                            if op.ring_val > 16:
                                eng.wait_ge(dsem[e][op.ring_slot], op.ring_val - 16)
                            ins = op.fn(eng)
                            ins.then_inc(dsem[e][op.ring_slot], 16)
                        else:
                            ins = op.fn(eng)
                            if op.flag:
                                c = count[id(op)]
                                ins.then_inc(csem[e][c // EPOCH], 1)
                    if self.ndma[e] > 0:
                        n = self.ndma[e]
                        for s in range(min(DMA_RING, n)):
                            last = ((n - 1 - s) // DMA_RING) + 1
                            eng.wait_ge(dsem[e][s], 16 * last)
                return body

            for e in self.ENGS:
                if self.ops[e]:
                    engobj[e](make(e))


def DMA(S, q, out, in_, reads, writes):
    return S.add(q, lambda e: e.dma_start(out=out, in_=in_), reads, writes, dma=True)


def MM(S, out, lhsT, rhs, start, stop, reads, writes):
    return S.add("pe", lambda e: e.matmul(out, lhsT=lhsT, rhs=rhs, start=start, stop=stop), reads, writes)


def TR(S, out, in_, ident, reads, writes):
    return S.add("pe", lambda e: e.transpose(out=out, in_=in_, identity=ident), reads, writes)


def ACT(S, out, in_, func, reads, writes, bias=None, scale=None):
    kw = {}
    if bias is not None:
        kw["bias"] = bias
    if scale is not None:
        kw["scale"] = scale
    return S.add("act", lambda e: e.activation(out=out, in_=in_, func=func, **kw), reads, writes)


def COPY(S, eng, out, in_, reads, writes):
    if eng == "act":
        return S.add("act", lambda e: e.copy(out=out, in_=in_), reads, writes)
    return S.add(eng, lambda e: e.tensor_copy(out=out, in_=in_), reads, writes)


def STT(S, eng, out, in0, scalar, in1, op0, op1, reads, writes):
    return S.add(eng, lambda e: e.scalar_tensor_tensor(out=out, in0=in0, scalar=scalar, in1=in1, op0=op0, op1=op1), reads, writes)


def TT(S, eng, out, in0, in1, op, reads, writes):
    return S.add(eng, lambda e: e.tensor_tensor(out=out, in0=in0, in1=in1, op=op), reads, writes)


def RECIP(S, out, in_, reads, writes):
    return S.add("dve", lambda e: e.reciprocal(out=out, in_=in_), reads, writes)


def _dft_tables(s, s_out=None):
    s_out = s if s_out is None else s_out
    n_in = np.concatenate([np.arange(HALF) + HALF * s, np.arange(HALF) + HALF * (1 - s)]).astype(np.int64)[:, None]
    n_out = (np.arange(HALF, dtype=np.int64) + HALF * s_out)[None, :]
    ang = 2.0 * np.pi * ((n_in * n_out) % SEQ).astype(np.float64) / SEQ
    cs = np.stack([np.cos(ang) / 64.0, -np.sin(ang) / 64.0], axis=1)
    return cs.astype(ml_dtypes.bfloat16)


def _dft_ctx():
    n = np.arange(CTX, dtype=np.int64)
    ang = 2.0 * np.pi * ((n[:, None] * n[None, :]) % CTX).astype(np.float64) / CTX
    cs = np.stack([np.cos(ang) / 16.0, -np.sin(ang) / 16.0], axis=1)
    return cs.astype(ml_dtypes.bfloat16)


def _dft_ch():
    c = np.arange(128, dtype=np.int64)
    ang = 2.0 * np.pi * ((c[:, None] * c[None, :]) % 128).astype(np.float64) / 128
    cs = np.concatenate([np.cos(ang), np.sin(ang)], axis=1) / np.sqrt(128.0)
    return cs.astype(ml_dtypes.bfloat16)


def _na_win(ti):
    if ti == 0:
        return -4, 12
    if ti == 1:
        return -2, 10
    if ti == 15:
        return 24, 11
    return 2 * ti - 4, 9


NA_CLS_TILES = [0, 1, 2, 14, 15]


def _na_cls(ti):
    return {0: 0, 1: 1, 14: 3, 15: 4}.get(ti, 2)


def _na_class_tables(rpb, s):
    h = rpb.shape[0]
    flat = np.concatenate([rpb.reshape(h, -1), np.full((h, 1), NEG, np.float32)], axis=1)
    out = []
    q = np.arange(128)
    p = np.arange(128)
    for ti in NA_CLS_TILES:
        gi = 16 * s + ti
        w0, nrow = _na_win(ti)
        r = 2 * gi + q // 64
        qc = q % 64
        rs = np.clip(r - 4, 0, 56)
        cst = np.clip(qc - 8, 0, 48)
        idx = np.full((128, 6, 128), 15 * 31, np.int64)
        for c in range((nrow + 1) // 2):
            lrow = w0 + 2 * c + p // 64
            krow = 32 * s + lrow
            kcol = p % 64
            okr = (krow[:, None] >= rs[None, :]) & (krow[:, None] < rs[None, :] + 8) & (krow[:, None] <= 63) & (krow[:, None] >= 0)
            okr = okr & ((lrow < w0 + nrow)[:, None])
            okc = (kcol[:, None] >= cst[None, :]) & (kcol[:, None] < cst[None, :] + 16)
            ri = np.clip(krow[:, None] - r[None, :] + 7, 0, 14)
            ci = np.clip(kcol[:, None] - qc[None, :] + 15, 0, 30)
            lin = ri * 31 + ci
            idx[:, c, :] = np.where(okr & okc, lin, 15 * 31)
        out.append(flat[:, idx])
    return np.ascontiguousarray(np.stack(out, 0))


def _rope_tables(s):
    t = np.arange(HALF, dtype=np.float64) + HALF * s
    inv = 10000.0 ** (-np.arange(16, dtype=np.float64) / 16.0)
    out = np.zeros((2, 64, HALF), np.float64)
    for half, pos in enumerate([np.floor(t / 64.0), np.mod(t, 64.0)]):
        ang = inv[:, None] * pos[None, :]
        b = 32 * half
        out[0, b:b + 16] = np.cos(ang)
        out[0, b + 16:b + 32] = np.cos(ang)
        out[1, b:b + 16] = -np.sin(ang)
        out[1, b + 16:b + 32] = np.sin(ang)
    return out.astype(np.float32)


ROPE_PERM = list(range(16, 32)) + list(range(0, 16)) + list(range(48, 64)) + list(range(32, 48))


def _pool_tables(s):
    out = np.zeros((128, 3, 4, 3, 128), np.float64)
    i = np.arange(128)
    for ci, gt in enumerate([16 * s, 16 * s + 1 if s == 0 else 16 * s + 14, 16 * s + 15]):
        if ci == 1:
            gt = 8
        for g, w in enumerate((2, 4, 8, 16)):
            to = 128 * gt + i
            lo = np.maximum(to - w // 2, 0)
            hi = np.minimum(to + w // 2, SEQ)
            cnt = (hi - lo).astype(np.float64)
            for rel in range(3):
                tin = 128 * (gt + rel - 1) + i
                m = (tin[:, None] >= lo[None, :]) & (tin[:, None] < hi[None, :])
                val = np.where(m, 1.0 / cnt[None, :], 0.0) - (tin[:, None] == to[None, :]).astype(np.float64)
                out[:, ci, g, rel, :] = val
    return out.astype(ml_dtypes.bfloat16)


def _fm(v):
    return np.ascontiguousarray(np.asarray(v, np.float32).reshape(-1, 128).T)


IN_SHAPES = {
    "x_own": ([HALF, D], F32), "x_par": ([HALF, D], F32), "x_halo": ([512, D], F32), "ctxb": ([CTX, D], F32),
    "vecs": ([128, 400], F32), "ident": ([128, 128], F32), "w_mod": ([2, D, 6 * D], F32),
    "w_in_ab": ([D, 5120], F32), "w_four": ([4, 128, 128], F32), "w_out_ab": ([D, D], F32),
    "w_ffn_gate": ([2, D, DFF], F32), "w_ffn_up": ([2, D, DFF], F32), "w_ffn_down": ([2, DFF, D], F32),
    "csn": ([SEQ, 2, HALF], BF16), "csx": ([CTX, 2, CTX], BF16), "csc": ([128, 256], BF16),
    "nabias": ([5, 12, 128, 6, 128], F32),
    "w_in_cd": ([D, 1856], F32), "w_pool": ([4, 128, 128], F32), "w_uq": ([768, 2304], F32), "w_ukv": ([512, 3072], F32),
    "w_out_cd": ([D, D], F32), "rope": ([2, 64, HALF], F32), "ptab": ([128, 3, 4, 3, 128], BF16),
    "xT_in": ([D, NTOK], F32), "exg": ([2, 704, NTOK], BF16),
    "x_halo2": ([512, D], F32), "csn2": ([SEQ, 2, HALF], BF16), "nabias2": ([5, 12, 128, 6, 128], F32), "rope2": ([2, 64, HALF], F32),
}


class _LazyIn(dict):
    def __init__(self, nc):
        super().__init__()
        self.nc = nc

    def __missing__(self, name):
        shape, dt = IN_SHAPES[name]
        ap = self.nc.dram_tensor(name, list(shape), dt, kind="ExternalInput").ap()
        self[name] = ap
        return ap


class Prog:
    def __init__(self, mode, stage=3, dbg=False):
        self.mode = mode
        self.stage = stage
        self.dbg = dbg

    def build(self):
        nc = bass.Bass("TRN2", target_bir_lowering=False)
        self.nc = nc

        def dscr(name, shape, dt):
            return nc.dram_tensor(name, list(shape), dt).ap()

        def dout(name, shape, dt):
            return nc.dram_tensor(name, list(shape), dt, kind="ExternalOutput").ap()

        I = _LazyIn(nc)
        self.I = I
        A = self.mode == "A"
        Fm = self.mode == "F"
        self.xT = dout("xT_out", [D, NTOK], F32) if A else dscr("xT_s", [D, NTOK], F32)
        self.exb = dout("ex_out", [704, NTOK], BF16) if (A or self.stage < 6) else dscr("exb_s", [704, NTOK], BF16)
        self.xT2 = dscr("xT2_s", [D, NTOK], F32)
        self.exb2 = dscr("exb2_s", [704, NTOK], BF16)
        self.xTc, self.xk, self.role = self.xT, "xT", 0
        self.ABs = dscr("AB_s", [128, 34 * 4 * 256], BF16)
        self.hT = dscr("hT_s", [D, NKV], BF16)
        self.mixT = dscr("mixT_s", [D, NTOK], BF16)
        self.qT = dscr("qT_s", [1536, NTOK], BF16)
        self.kT = dscr("kT_s", [1536, NKV], BF16)
        self.vS = dscr("v_s", [NKV, 1536], BF16)
        self.aT = dscr("aT_s", [DFF, NTOK], BF16)
        self.l1raw = dscr("l1raw_s", [1408, NTOK], F32)
        self.qn_s = dscr("qn_s", [1536, HALF], BF16)
        self.qr_s = dscr("qr_s", [768, HALF], BF16)
        self.kn_s = dscr("kn_s", [1536, SEQ + CTX], BF16)
        self.v1_s = dscr("v1_s", [SEQ + CTX, 1536], BF16)
        if not A:
            self.out = dout("out", [HALF, D], F32)
            if not Fm:
                self.exg = I["exg"]

        S = Sched(nc)
        self.S = S
        with contextlib.ExitStack() as es:
            self.es = es
            self.pp = [es.enter_context(nc.psum_tensor(f"pp{i}", [128, 1024], F32)) for i in range(4)]
            self.ident_f = self.sb("ident_f", [128, 128], F32)
            self.ones_b = self.sb("ones_b", [128, 128], BF16)
            self.vecs = self.sb("vecs_sb", [128, 400], F32)
            self.scb = self.sb("scb", [128, 16, 2], BF16)
            self.modT = self.sb("modT", [128, 2, 96, 2], F32)
            self.G = self.sb("Gmod", [128, 2, 2, 2, 16], F32)
            self.epsb = self.sb("epsb", [128, 1], F32)
            DMA(S, "sp", self.ident_f[:], I["ident"], [], ["ident_f"])
            DMA(S, "sp", self.vecs[:], I["vecs"], [], ["vecs"])
            S.add("pool", lambda e: e.memset(self.ones_b[:], 1.0), [], ["ones_b"])
            S.add("pool", lambda e: e.memset(self.epsb[:], EPS), [], ["epsb"])
            if not A and not Fm:
                for b in range(5):
                    c0 = 512 * b
                    nt = min(512, NTOK - c0)
                    DMA(S, "sp", self.xTc[:, c0:c0 + nt], I["xT_in"][:, c0:c0 + nt], [], [(self.xk, k, b) for k in range(16)])
            self.phase_mod()
            if Fm:
                self.set_role(0)
                self.layer0()
                self.set_role(1)
                self.layer0()
                with contextlib.ExitStack() as es_l1:
                    self.set_role(0)
                    self.l1_part1(es_l1)
                    self.set_role(1)
                    self.l1_part1(es_l1)
                    self.set_role(0)
                    self.l1_part2(es_l1)
                    self.l1_attn()
                self.barrier()
                self.outproj(1, I["w_out_cd"], HALF)
                self.barrier()
                self.ffn(1, HALF)
                self.barrier()
                self.final_norm()
            elif A:
                self.layer0()
                if self.stage >= 4:
                    with contextlib.ExitStack() as es_l1:
                        self.l1_part1(es_l1)
            else:
                st = self.stage
                with contextlib.ExitStack() as es_l1:
                    self.l1_part1(es_l1)
                    if st >= 2:
                        self.l1_part2(es_l1)
                    if st >= 3:
                        self.l1_attn()
                self.barrier()
                if st >= 4:
                    self.outproj(1, I["w_out_cd"], HALF)
                    self.barrier()
                if st >= 5:
                    self.ffn(1, HALF)
                    self.barrier()
                if st >= 6:
                    self.final_norm()
                elif self.dbg:
                    xd = dout("xT_dbg", [D, NTOK], F32)
                    md = dout("mix_dbg", [D, NTOK], BF16)
                    self.barrier()
                    DMA(S, "sp", xd, self.xT, [], ["xd"])
                    DMA(S, "sp", md, self.mixT, [], ["md"])
                    rd_ = dout("raw_dbg", [1408, NTOK], F32)
                    DMA(S, "sp", rd_, self.l1raw, [], ["rawd"])
            S.emit()
        _NAMES[self.mode] = list(I.keys())
        return nc

    def set_role(self, r):
        self.role = r
        self.xTc, self.xk = (self.xT, "xT") if r == 0 else (self.xT2, "xT2")

    def sb(self, name, shape, dt, es=None):
        self._uid = getattr(self, "_uid", 0) + 1
        return (es or self.es).enter_context(self.nc.sbuf_tensor(f"{name}_u{self._uid}", list(shape), dt))

    def bank(self, i):
        return self.pp[i // 2][:, (i % 2) * 512:(i % 2) * 512 + 512], ("ps", i)

    def barrier(self):
        self.S.barrier()

    V_C2 = 0
    V_N1 = 32
    V_N2 = 64
    V_FG = 96
    V_BM = 112
    V_PS = 304
    V_GQ = 308
    V_GKV = 314

    def phase_mod(self):
        S, nc, I = self.S, self.nc, self.I
        with contextlib.ExitStack() as es:
            wts = [self.sb(f"wmod{i}", [128, 16, 512], BF16, es) for i in range(4)]
            ACT(S, self.scb[:].rearrange("p k m -> p (k m)"), self.vecs[:, 0:32], AF.Silu, ["vecs"], ["scb"])
            n = 0
            for l in range(2):
                for cg in range(24):
                    wt = wts[n % 4]
                    wk = ("wmod", n % 4)
                    DMA(S, "pool", wt[:], I["w_mod"][l, :, cg * 512:(cg + 1) * 512].rearrange("(k p) n -> p k n", p=128), [], [wk])
                    ps, pk = self.bank(n % 2)
                    for j in range(4):
                        for kc in range(16):
                            MM(S, ps[:, j * 2:j * 2 + 2], wt[:, kc, j * 128:(j + 1) * 128], self.scb[:, kc, :], kc == 0, kc == 15,
                               [wk, "scb"], [pk])
                    for m in range(2):
                        TT(S, "dve", self.modT[:, l, cg * 4:cg * 4 + 4, m], ps[:, 0:8].rearrange("p (j m) -> p j m", m=2)[:, :, m],
                           self.vecs[:, self.V_BM + l * 96 + cg * 4:self.V_BM + l * 96 + cg * 4 + 4], ALU.add,
                           [pk, "vecs"], [("modT", l)])
                    n += 1
            for l in range(2):
                for nrm in range(2):
                    for m in range(2):
                        sc = self.modT[:, l, 16 + 48 * nrm:32 + 48 * nrm, m]
                        gcol = (self.V_N1 if nrm == 0 else self.V_N2) + l * 16
                        STT(S, "dve", self.G[:, l, nrm, m, :], sc, 1.0, self.vecs[:, gcol:gcol + 16], ALU.add, ALU.mult,
                            [("modT", l), "vecs"], [("G", l)])
            self.barrier()

    def mcol(self, l, which, kc, m):
        return self.modT[:, l, which * 16 + kc, m:m + 1]

    def layer0(self):
        S, nc, I = self.S, self.nc, self.I
        with contextlib.ExitStack() as es:
            AB = self.sb("AB", [128, 34, 4, 256], BF16, es)
            with contextlib.ExitStack() as es1:
                xin = self.sb("xin", [128, 4, D], F32, es1)
                xTb = self.sb("xTb", [128, 16, 512], F32, es1)
                sq = self.sb("sq", [128, 16, 512], BF16, es1)
                tmp = self.sb("tmp", [128, 2, 512], F32, es1)
                hTb = self.sb("hTb", [128, 16, 512], BF16, es1)
                uTb = self.sb("uTb", [128, 4, 512], BF16, es1)
                r1 = self.sb("r1", [128, 512], F32, es1)
                rstd = self.sb("rstd", [128, 512], F32, es1)
                Wu = self.sb("Wu", [128, 16, 512], BF16, es1)
                csc = self.sb("csc_sb", [128, 256], BF16, es1)
                DMA(S, "pool", Wu[:], I["w_in_ab"][:, 0:512].rearrange("(k p) n -> p k n", p=128), [], ["Wu"])
                DMA(S, "sp", csc[:], I["csc"], [], ["csc"])
                blocks = []
                role = self.role
                x_o, x_p = (I["x_own"], I["x_par"]) if role == 0 else (I["x_par"], I["x_own"])
                for j in range(4):
                    blocks.append(("own", x_o, 512 * j, 512, 512 * j, 4 * j if role == 0 else None, 0))
                if role == 0:
                    blocks.append(("ctx", I["ctxb"], 0, 256, HALF, 32, 1))
                    for j in range(4):
                        blocks.append(("par", x_p, 512 * j, 512, None, 16 + 4 * j, 0))
                blocks.append(("halo", I["x_halo"] if role == 0 else I["x_halo2"], 0, 512, NTOK, None, 0))
                for bi, (kind, src, t0, nt, lc, ab0, m) in enumerate(blocks):
                    ntile = nt // 128
                    DMA(S, "sp", xin[:, 0:ntile, :], src[t0:t0 + nt, :].rearrange("(t p) f -> p t f", p=128), [], ["xin"])
                    for kc in range(16):
                        ps, pk = self.bank(kc % 2)
                        for t in range(ntile):
                            TR(S, ps[:, t * 128:(t + 1) * 128], xin[:, t, kc * 128:(kc + 1) * 128], self.ident_f[:], ["xin", "ident_f"], [pk])
                        COPY(S, "act" if kc % 2 == 0 else "dve", xTb[:, kc, 0:nt], ps[:, 0:nt], [pk], [("xTb", kc)])
                    xk = "xTb_all"
                    allk = [("xTb", kc) for kc in range(16)]
                    if kind == "own" or (kind == "ctx" and self.role == 0):
                        DMA(S, "sp", self.xTc[:, lc:lc + nt].rearrange("(k p) t -> p k t", p=128), xTb[:, :, 0:nt], allk,
                            [(self.xk, kc, lc // 512) for kc in range(16)])

                    def out_fn(kc):
                        return hTb[:, kc, 0:nt], ("hTb", kc)
                    self._norm_multi(xTb, allk, nt, 0, 0, m, sq, tmp, rstd, r1, out_fn, 2)
                    hk = [("hTb", kc) for kc in range(16)]
                    if lc is not None:
                        DMA(S, "sp", self.hT[:, lc:lc + nt].rearrange("(k p) t -> p k t", p=128), hTb[:, :, 0:nt], hk, [("hT", lc // 512)])
                    if ab0 is None:
                        continue
                    for g in range(4):
                        ps, pk = self.bank(3 + (g % 2))
                        for kc in range(16):
                            MM(S, ps[:, 0:nt], Wu[:, kc, g * 128:(g + 1) * 128], hTb[:, kc, 0:nt], kc == 0, kc == 15, ["Wu", ("hTb", kc)], [pk])
                        COPY(S, "dve", uTb[:, g, 0:nt], ps[:, 0:nt], [pk], [("uTb", g)])
                    for t in range(ntile):
                        pA = self.pp[3 if t % 2 == 0 else 2]
                        pAk = ("ps", 6 if t % 2 == 0 else 4)
                        pAk2 = ("ps", 7 if t % 2 == 0 else 5)
                        for g in range(4):
                            MM(S, pA[:, g * 256:(g + 1) * 256], uTb[:, g, t * 128:(t + 1) * 128], csc[:], True, True,
                               [("uTb", g), "csc"], [pAk, pAk2])
                        COPY(S, "act", AB[:, ab0 + t, :, :].rearrange("p g c -> p (g c)"), pA[:, :], [pAk, pAk2], [("AB", ab0 + t)])
            if self.role == 0:
                DMA(S, "sp", self.ABs, AB[:].rearrange("p t g c -> p (t g c)"), [("AB", t) for t in range(34)], ["ABs"])
            else:
                DMA(S, "sp", AB[:].rearrange("p t g c -> p (t g c)"), self.ABs, ["ABs"], [("AB", t) for t in range(34)])
            self.barrier()
            with contextlib.ExitStack() as es1:
                cs = [self.sb(f"csn{i}", [128, 32, 2, 512], BF16, es1) for i in range(1)]
                Wf = self.sb("Wf", [128, 4, 128], BF16, es1)
                yT = [self.sb(f"yT{i}", [128, 512], BF16, es1) for i in range(2)]
                zst = [self.sb(f"zst{i}", [128, 512], BF16, es1) for i in range(2)]
                csx = self.sb("csx_sb", [128, 2, 2, 256], BF16, es1)
                DMA(S, "pool", Wf[:], I["w_four"].rearrange("g c d -> c g d"), [], ["Wf"])
                for t in range(2):
                    DMA(S, "sp", csx[:, :, t, :], I["csx"][:, t, :].rearrange("(k p) n -> p k n", p=128), [], ["csx"])
                n = 0
                csn_in = I["csn"] if self.role == 0 else I["csn2"]
                for ob in range(5 if self.role == 0 else 4):
                    if ob < 4:
                        nt, nk, col0 = 512, 32, 512 * ob
                        cst = cs[0]
                        ck = "csn0"
                        for t in range(2):
                            DMA(S, "sp", cst[:, :, t, :], csn_in[:, t, col0:col0 + 512].rearrange("(k p) n -> p k n", p=128), [], [ck])
                        ab_base = 0

                        def rhs_of(kc, t):
                            return cst[:, kc, t, :]
                    else:
                        nt, nk, col0 = 256, 2, HALF
                        ck = "csx"
                        ab_base = 32

                        def rhs_of(kc, t):
                            return csx[:, kc, t, :]
                    for g in range(4):
                        ps, pk = self.bank(n % 2)
                        for kc in range(nk):
                            for t in range(2):
                                MM(S, ps[:, 0:nt], AB[:, ab_base + kc, g, t * 128:(t + 1) * 128], rhs_of(kc, t),
                                   kc == 0 and t == 0, kc == nk - 1 and t == 1, [("AB", ab_base + kc), ck], [pk])
                        y = yT[n % 2]
                        COPY(S, "act", y[:, 0:nt], ps[:, 0:nt], [pk], [("yT", n % 2)])
                        ps2, pk2 = self.bank(2 + n % 2)
                        MM(S, ps2[:, 0:nt], Wf[:, g, :], y[:, 0:nt], True, True, ["Wf", ("yT", n % 2)], [pk2])
                        z = zst[n % 2]
                        COPY(S, "dve", z[:, 0:nt], ps2[:, 0:nt], [pk2], [("zst", n % 2)])
                        DMA(S, "sp", self.mixT[g * 128:(g + 1) * 128, col0:col0 + nt], z[:, 0:nt], [("zst", n % 2)], [("mixT", g, ob)])
                        n += 1
        self.barrier()
        if self.stage < 2:
            return
        self.l0_qkv()
        self.barrier()
        self.l0_attn()
        self.barrier()
        if self.stage < 3:
            return
        nt0 = NTOK if self.role == 0 else HALF
        self.outproj(0, self.I["w_out_ab"], nt0)
        self.barrier()
        self.ffn(0, nt0)
        self.barrier()

    def _norm_multi(self, xTb, xkeys, nt, l, nrm, m, sq, tmp, rstd, r1, out_fn, bank_i):
        S = self.S
        ACT(S, sq[:, :, 0:nt], xTb[:, :, 0:nt], AF.Square, xkeys, ["sq"])
        ps, pk = self.bank(bank_i)
        for kc in range(16):
            MM(S, ps[:, 0:nt], self.ones_b[:], sq[:, kc, 0:nt], kc == 0, kc == 15, ["ones_b", "sq"], [pk])
        ACT(S, r1[:, 0:nt], ps[:, 0:nt], AF.Sqrt, [pk, "epsb"], ["r1"], bias=self.epsb[:], scale=1.0 / D)
        RECIP(S, rstd[:, 0:nt], r1[:, 0:nt], ["r1"], ["rstd"])
        for kc in range(16):
            gap = self.G[:, l, nrm, m, kc:kc + 1]
            STT(S, "dve", tmp[:, kc % 2, 0:nt], xTb[:, kc, 0:nt], gap, rstd[:, 0:nt], ALU.mult, ALU.mult,
                list(xkeys) + [("G", l), "rstd"], [("tmp", kc % 2)])
            o, ok = out_fn(kc)
            ACT(S, o, tmp[:, kc % 2, 0:nt], AF.Identity, [("tmp", kc % 2), ("modT", l)], [ok], bias=self.mcol(l, 3 * nrm, kc, m), scale=1.0)

    def l0_qkv(self):
        S, I = self.S, self.I
        with contextlib.ExitStack() as es:
            hTr = self.sb("hTr", [128, 16, NKV], BF16, es)
            wts = [self.sb(f"wqkv{i}", [128, 16, 512], BF16, es) for i in range(2)]
            qst = [self.sb(f"qst{i}", [128, NKV], BF16, es) for i in range(2)]
            vst = [self.sb(f"vst{i}", [128, 22, 512], BF16, es) for i in range(1)]
            tbs = [(0, 512, 0), (512, 512, 1), (1024, 512, 2), (1536, 512, 3), (2048, 256, 4), (2304, 512, 5)]
            for (c0, nt, b) in tbs:
                DMA(S, "sp", hTr[:, :, c0:c0 + nt], self.hT[:, c0:c0 + nt].rearrange("(k p) t -> p k t", p=128), [("hT", 4 if b == 5 else b)] if False else [("hT", c0 // 512)], [("hTr", b)])
            n = 0
            e = 0
            for part, dst, ntb in (("q", self.qT, 5 if self.role == 0 else 4), ("k", self.kT, 6)):
                cbase = 512 if part == "q" else 2048
                for cg in range(3):
                    wt, wk = wts[n % 2], ("wqkv", n % 2)
                    DMA(S, "pool", wt[:], I["w_in_ab"][:, cbase + cg * 512:cbase + (cg + 1) * 512].rearrange("(k p) n -> p k n", p=128), [], [wk])
                    n += 1
                    for j in range(4):
                        hd = cg * 4 + j
                        st, sk = qst[hd % 2], ("qst", hd % 2)
                        for (c0, nt, b) in tbs[:ntb]:
                            ps, pk = self.bank(e % 4)
                            for kc in range(16):
                                MM(S, ps[:, 0:nt], wt[:, kc, j * 128:(j + 1) * 128], hTr[:, kc, c0:c0 + nt], kc == 0, kc == 15, [wk, ("hTr", b)], [pk])
                            COPY(S, "act" if e % 2 == 0 else "dve", st[:, c0:c0 + nt], ps[:, 0:nt], [pk], [sk])
                            e += 1
                        ncol = (NTOK if self.role == 0 else HALF) if part == "q" else NKV
                        DMA(S, "sp", dst[hd * 128:(hd + 1) * 128, 0:ncol], st[:, 0:ncol], [sk], [(part + "T", hd)])
            for cg in range(3):
                wt, wk = wts[n % 2], ("wqkv", n % 2)
                DMA(S, "pool", wt[:], I["w_in_ab"][:, 3584 + cg * 512:3584 + (cg + 1) * 512].rearrange("(k p) n -> p k n", p=128), [], [wk])
                n += 1
                v = vst[0]
                for t in range(22):
                    ps, pk = self.bank(4 + e % 4)
                    hb = t // 4 if t < 16 else (4 if t < 18 else 5)
                    for kc in range(16):
                        MM(S, ps[:, :], hTr[:, kc, t * 128:(t + 1) * 128], wt[:, kc, :], kc == 0, kc == 15, [wk, ("hTr", hb)], [pk])
                    COPY(S, "act" if e % 2 == 0 else "dve", v[:, t, :], ps[:, :], [pk], ["vst"])
                    e += 1
                DMA(S, "sp", self.vS[:, cg * 512:(cg + 1) * 512].rearrange("(t p) c -> p t c", p=128), v[:], ["vst"], [("vS", cg)])

    @staticmethod
    def _loc(lrow):
        if lrow < 0:
            return NTOK + (lrow + 4) * 64
        if lrow >= 32:
            return NTOK + 256 + (lrow - 32) * 64
        return lrow * 64

    def l0_attn(self):
        S, I = self.S, self.I
        scale = 128.0 ** -0.5
        with contextlib.ExitStack() as es:
            kwin = [self.sb(f"kwin{i}", [128, 12, 1024], BF16, es) for i in range(2)]
            vwin = [self.sb(f"vwin{i}", [128, 8, 1536], BF16, es) for i in range(2)]
            bia = [self.sb(f"bia{i}", [128, 12, 768], F32, es) for i in range(2)]
            qw = [self.sb(f"qw{i}", [128, 12, 128], BF16, es) for i in range(2)]
            ost = [self.sb(f"ost{i}", [128, 12, 128], BF16, es) for i in range(2)]
            sbs = [self.sb(f"sbs{i}", [128, 768], F32, es) for i in range(2)]
            pT = [self.sb(f"pT{i}", [128, 1024], BF16, es) for i in range(3)]
            rd = [self.sb(f"rd{i}", [128, 128], F32, es) for i in range(2)]
            allq = [("qT", h) for h in range(12)]
            allk = [("kT", h) for h in range(12)]
            allv = [("vS", c) for c in range(3)]
            nab = I["nabias"] if self.role == 0 else I["nabias2"]
            ntile = 18 if self.role == 0 else 16
            info = {}

            def load_tile(ti):
                i2 = ti % 2
                isctx = ti >= 16
                kw_, vw_, bi_, qw_ = kwin[i2], vwin[i2], bia[i2], qw[i2]
                kk, vk, bk, qk = ("kwin", i2), ("vwin", i2), ("bia", i2), ("qw", i2)
                qc0 = ti * 128
                if not isctx:
                    w0, nrow = _na_win(ti)
                    chunks = []
                    for c in range((nrow + 1) // 2):
                        chunks.append((self._loc(w0 + 2 * c), 128 if 2 * c + 1 < nrow else 64))
                    nl = len(chunks)
                    chunks += [(HALF, 128), (HALF + 128, 128)]
                    DMA(S, "sp", bi_[:, :, 0:nl * 128].rearrange("p h (c q) -> p h c q", q=128),
                        nab[_na_cls(ti)][:, :, 0:nl, :].rearrange("h p c q -> p h c q"), [], [bk])
                else:
                    chunks = [(HALF, 128), (HALF + 128, 128)]
                for c, (l0, nk) in enumerate(chunks):
                    DMA(S, "sp", kw_[:, :, c * 128:c * 128 + nk], self.kT[:, l0:l0 + nk].rearrange("(h p) t -> p h t", p=128), allk, [kk])
                    DMA(S, "sp", vw_[0:nk, c, :], self.vS[l0:l0 + nk, :], allv, [vk])
                DMA(S, "sp", qw_[:], self.qT[:, qc0:qc0 + 128].rearrange("(h p) t -> p h t", p=128), allq, [qk])
                info[ti] = chunks

            def front(n, ti, h):
                i2 = ti % 2
                chunks = info[ti]
                nch = len(chunks)
                nloc = nch - 2
                kw_, bi_, qw_ = kwin[i2], bia[i2], qw[i2]
                kk, bk, qk = ("kwin", i2), ("bia", i2), ("qw", i2)
                j2 = n % 2
                pS = self.pp[j2]
                pSk = [("ps", 2 * j2), ("ps", 2 * j2 + 1)]
                for c, (l0, nk) in enumerate(chunks):
                    MM(S, pS[0:nk, c * 128:(c + 1) * 128], kw_[:, h, c * 128:c * 128 + nk], qw_[:, h, :], True, True, [kk, qk], pSk)
                p_, pk_ = pT[n % 3], ("pT", n % 3)
                if nloc > 0:
                    sb_ = sbs[j2]
                    nlc = nloc * 128
                    STT(S, "dve", sb_[:, 0:nlc], pS[:, 0:nlc], scale, bi_[:, h, 0:nlc], ALU.mult, ALU.add, pSk + [bk], [("sbs", j2)])
                    ACT(S, p_[:, 0:nlc], sb_[:, 0:nlc], AF.Exp, [("sbs", j2)], [pk_])
                ACT(S, p_[:, nloc * 128:nch * 128], pS[:, nloc * 128:nch * 128], AF.Exp, pSk, [pk_], scale=scale)

            def back(n, ti, h):
                i2 = ti % 2
                chunks = info[ti]
                nch = len(chunks)
                vw_, os_ = vwin[i2], ost[i2]
                vk, ok = ("vwin", i2), ("ost", i2)
                j2 = n % 2
                p_, pk_ = pT[n % 3], ("pT", n % 3)
                pO, pOk = self.bank(4 + 2 * j2)
                pD, pDk = self.bank(5 + 2 * j2)
                for c, (l0, nk) in enumerate(chunks):
                    MM(S, pO[:, 0:128], vw_[0:nk, c, h * 128:(h + 1) * 128], p_[0:nk, c * 128:(c + 1) * 128], c == 0, c == nch - 1, [vk, pk_], [pOk])
                for c, (l0, nk) in enumerate(chunks):
                    MM(S, pD[:, 0:128], self.ones_b[0:nk, :], p_[0:nk, c * 128:(c + 1) * 128], c == 0, c == nch - 1, ["ones_b", pk_], [pDk])
                RECIP(S, rd[j2][:, :], pD[:, 0:128], [pDk], [("rd", j2)])
                TT(S, "dve", os_[:, h, :], pO[:, 0:128], rd[j2][:, :], ALU.mult, [pOk, ("rd", j2)], [ok])
                if h == 11:
                    qc0 = ti * 128
                    DMA(S, "sp", self.mixT[512:2048, qc0:qc0 + 128].rearrange("(h p) t -> p h t", p=128), os_[:], [ok],
                        [("mixT", 4 + hh, ti // 4) for hh in range(12)])

            items = [(ti, h) for ti in range(ntile) for h in range(12)]
            for n in range(len(items) + 1):
                if n < len(items):
                    ti, h = items[n]
                    if h == 0:
                        load_tile(ti)
                    front(n, ti, h)
                if n >= 1:
                    ti, h = items[n - 1]
                    back(n - 1, ti, h)

    def l1_part1(self, es_l1):
        S, I = self.S, self.I
        role = self.role
        lite = role == 1
        if not lite:
            self.cqn = self.sb("cqn", [128, 6, HALF], BF16, es_l1)
            self.upool = self.sb("upool", [128, 18, 512], BF16, es_l1)
        exb = self.exb if not lite else self.exb2
        exk = "exb" if not lite else "exb2"
        nblk = 5 if not lite else 4
        W = I["w_in_cd"]
        with contextlib.ExitStack() as es:
            h1 = self.sb("h1r", [128, 16, NTOK], BF16, es)
            with contextlib.ExitStack() as es2:
                xTb = self.sb("n_xTb", [128, 16, 512], F32, es2)
                sq = self.sb("n_sq", [128, 16, 512], BF16, es2)
                tmp = self.sb("n_tmp", [128, 2, 512], F32, es2)
                r1 = self.sb("n_r1", [128, 512], F32, es2)
                rstd = self.sb("n_rstd", [128, 512], F32, es2)
                for b in range(nblk):
                    c0 = 512 * b
                    nt = min(512, NTOK - c0)
                    m = 1 if c0 >= HALF else 0
                    DMA(S, "sp", xTb[:, :, 0:nt], self.xTc[:, c0:c0 + nt].rearrange("(k p) t -> p k t", p=128),
                        [(self.xk, k, b) for k in range(16)], ["n_xTb"])

                    def out_fn(kc, c0=c0, nt=nt, b=b):
                        return h1[:, kc, c0:c0 + nt], ("h1r", b)
                    self._norm_multi(xTb, ["n_xTb"], nt, 1, 0, m, sq, tmp, rstd, r1, out_fn, 7)
            self.barrier()
            wts = [self.sb(f"w1_{i}", [128, 16, 512], BF16, es) for i in range(2)]
            stg = [self.sb(f"stg{i}", [128, NTOK], F32, es) for i in range(2)]
            wv = lambda a, b_: W[:, a:b_].rearrange("(k p) n -> p k n", p=128)
            DMA(S, "pool", wts[0][:], wv(0, 512), [], [("w1", 0)])
            e = 0
            for t in (range(16) if not lite else (0, 15)):
                ps, pk = self.bank(4 + e % 4)
                for kc in range(16):
                    MM(S, ps[:, :], h1[:, kc, t * 128:(t + 1) * 128], wts[0][:, kc, :], kc == 0, kc == 15, [("w1", 0), ("h1r", t // 4)], [pk])
                ui = 1 + t if not lite else (17 if t == 0 else 0)
                COPY(S, "act" if e % 2 == 0 else "dve", self.upool[:, ui, :], ps[:, :], [pk], [("upool", ui)])
                e += 1
            jobs = []
            for j in range(4):
                jobs.append((1, j * 128, 128, j * 128, 4))
            jobs += [(2, 0, 128, 512, 4), (2, 128, 128, 640, 4), (2, 256, 128, 768, 5), (2, 384, 128, 896, 5)]
            jobs += [(3, 0, 128, 1024, 5), (3, 128, 128, 1152, 5), (3, 256, 64, 1280, 5), (3, 320, 64, 1344, 5)]
            loaded = {}
            n = 0
            for (ti, co, M, r0, ntb) in jobs:
                if lite and r0 < 768:
                    continue
                ntb = min(ntb, nblk)
                if ti not in loaded:
                    w_, wk = wts[ti % 2], ("w1", ti % 2)
                    if ti == 1:
                        DMA(S, "pool", w_[:], wv(512, 1024), [], [wk])
                    elif ti == 2:
                        DMA(S, "pool", w_[:], wv(1024, 1536), [], [wk])
                    else:
                        DMA(S, "pool", w_[:, :, 0:320], wv(1536, 1856), [], [wk])
                        for q4 in range(4):
                            src0 = 1792 + ROPE_PERM[q4 * 16]
                            DMA(S, "pool", w_[:, :, 320 + q4 * 16:336 + q4 * 16], wv(src0, src0 + 16), [], [wk])
                    loaded[ti] = True
                w_, wk = wts[ti % 2], ("w1", ti % 2)
                st, sk = stg[n % 2], ("stg", n % 2)
                n += 1
                for b in range(ntb):
                    c0 = 512 * b
                    nt = min(512, NTOK - c0)
                    ps, pk = self.bank(e % 4)
                    for kc in range(16):
                        MM(S, ps[0:M, 0:nt], w_[:, kc, co:co + M], h1[:, kc, c0:c0 + nt], kc == 0, kc == 15, [wk, ("h1r", b)], [pk])
                    COPY(S, "act" if e % 2 == 0 else "dve", st[0:M, c0:c0 + nt], ps[0:M, 0:nt], [pk], [sk])
                    e += 1
                ncol = HALF if ntb == 4 else NTOK
                DMA(S, "sp", self.l1raw[r0:r0 + M, 0:ncol], st[0:M, 0:ncol], [sk], [("l1raw", r0 // 128)])
        self.barrier()
        with contextlib.ExitStack() as es:
            cqb = self.sb("cqb", [128, 6, 512], F32, es)
            ckb = self.sb("ckb", [128, 4, 512], F32, es)
            krb = self.sb("krb", [64, 2, 512], F32, es)
            sq6 = self.sb("sq6", [128, 6, 512], BF16, es)
            r1 = self.sb("p_r1", [128, 512], F32, es)
            rstd = self.sb("p_rstd", [128, 512], F32, es)
            ckn = self.sb("ckn", [128, 4, 512], BF16, es)
            kro = self.sb("kro", [64, 512], BF16, es)
            t1 = self.sb("rt1", [64, 512], F32, es)
            t2 = self.sb("rt2", [64, 512], F32, es)
            rope = self.sb("rope_sb", [64, 2, HALF], F32, es)
            DMA(S, "sp", rope[:], (I["rope"] if not lite else I["rope2"]).rearrange("a p t -> p a t"), [], ["rope"])
            rawk = [("l1raw", i) for i in range(11)]
            for b in range(nblk):
                c0 = 512 * b
                nt = min(512, NTOK - c0)
                for (nch, buf, bk, r0, dim, gcol) in ((6, cqb, "cqb", 0, 768, self.V_GQ), (4, ckb, "ckb", 768, 512, self.V_GKV)):
                    if nch == 6 and (b == 4 or lite):
                        continue
                    DMA(S, "sp", buf[:, :, 0:nt], self.l1raw[r0:r0 + nch * 128, c0:c0 + nt].rearrange("(k p) t -> p k t", p=128), rawk, [bk])
                    ACT(S, sq6[:, 0:nch, 0:nt], buf[:, :, 0:nt], AF.Square, [bk], ["sq6"])
                    ps, pk = self.bank(6)
                    for kc in range(nch):
                        MM(S, ps[:, 0:nt], self.ones_b[:], sq6[:, kc, 0:nt], kc == 0, kc == nch - 1, ["ones_b", "sq6"], [pk])
                    ACT(S, r1[:, 0:nt], ps[:, 0:nt], AF.Sqrt, [pk, "epsb"], ["p_r1"], bias=self.epsb[:], scale=1.0 / dim)
                    RECIP(S, rstd[:, 0:nt], r1[:, 0:nt], ["p_r1"], ["p_rstd"])
                    for kc in range(nch):
                        if nch == 6:
                            o, ok = self.cqn[:, kc, c0:c0 + nt], ("cqn", b)
                        else:
                            o, ok = ckn[:, kc, 0:nt], "ckn"
                        STT(S, "dve", o, buf[:, kc, 0:nt], self.vecs[:, gcol + kc:gcol + kc + 1], rstd[:, 0:nt], ALU.mult, ALU.mult,
                            [bk, "vecs", "p_rstd"], [ok])
                    if nch == 4:
                        DMA(S, "sp", exb[0:512, c0:c0 + nt].rearrange("(k p) t -> p k t", p=128), ckn[:, :, 0:nt], ["ckn"], [(exk, b)])
                DMA(S, "sp", krb[:, :, 0:nt], self.l1raw[1280:1408, c0:c0 + nt].rearrange("(a p) t -> p a t", p=64), rawk, ["krb"])
                if b < 4:
                    TT(S, "dve", t1[:, 0:nt], krb[:, 0, 0:nt], rope[:, 0, c0:c0 + nt], ALU.mult, ["krb", "rope"], ["rt1"])
                    TT(S, "pool", t2[:, 0:nt], krb[:, 1, 0:nt], rope[:, 1, c0:c0 + nt], ALU.mult, ["krb", "rope"], ["rt2"])
                    TT(S, "dve", kro[:, 0:nt], t1[:, 0:nt], t2[:, 0:nt], ALU.add, ["rt1", "rt2"], ["kro"])
                else:
                    COPY(S, "dve", kro[:, 0:nt], krb[:, 0, 0:nt], ["krb"], ["kro"])
                DMA(S, "sp", exb[512:576, c0:c0 + nt], kro[:, 0:nt], ["kro"], [(exk, b)])
            if not lite and self.mode != "F":
                DMA(S, "sp", exb[576:704, 0:512], self.upool[:, 1, :], [("upool", 1)], [("exb", 5)])
                DMA(S, "sp", exb[576:704, 512:1024], self.upool[:, 16, :], [("upool", 16)], [("exb", 5)])
        self.barrier()

    def l1_part2(self, es_l1):
        S, I = self.S, self.I
        fused = self.mode == "F"
        if fused:
            G = [self.exb, self.exb2]
            gk = [[("exb", b) for b in range(4)], [("exb2", b) for b in range(4)]]
        else:
            G = [self.exg[0], self.exg[1]]
            gk = [[("exg", 0)], [("exg", 0)]]
        self.krr = self.sb("krr", [128, SEQ + CTX], BF16, es_l1)
        S.add("pool", lambda e: e.memset(self.krr[:], 0.0), [], ["krr"])
        for r in range(2):
            DMA(S, "sp", self.krr[0:64, r * HALF:(r + 1) * HALF], G[r][512:576, 0:HALF], gk[r], ["krr"])
        DMA(S, "sp", self.krr[0:64, SEQ:SEQ + CTX], self.exb[512:576, HALF:NTOK], [("exb", 4)], ["krr"])
        if not fused:
            DMA(S, "sp", self.upool[:, 0, :], G[0][576:704, 512:1024], gk[0], [("upool", 0)])
            DMA(S, "sp", self.upool[:, 17, :], G[1][576:704, 0:512], gk[1], [("upool", 17)])
        with contextlib.ExitStack() as es:
            ckv = self.sb("ckv_all", [128, 4, SEQ + CTX], BF16, es)
            for r in range(2):
                DMA(S, "sp", ckv[:, :, r * HALF:(r + 1) * HALF], G[r][0:512, 0:HALF].rearrange("(k p) t -> p k t", p=128), gk[r], [("ckv", r)])
            DMA(S, "sp", ckv[:, :, SEQ:SEQ + CTX], self.exb[0:512, HALF:NTOK].rearrange("(k p) t -> p k t", p=128), [("exb", 4)], [("ckv", 2)])
            with contextlib.ExitStack() as es1:
                PT = self.sb("PT", [128, 3, 4, 3, 128], BF16, es1)
                Wp = self.sb("Wpool", [128, 4, 128], BF16, es1)
                pu = [self.sb(f"pu{i}", [128, 512], BF16, es1) for i in range(2)]
                zst = [self.sb(f"pz{i}", [128, 512], BF16, es1) for i in range(2)]
                DMA(S, "sp", PT[:].rearrange("p a g r t -> p (a g r t)"), I["ptab"].rearrange("p a g r t -> p (a g r t)"), [], ["PT"])
                DMA(S, "pool", Wp[:], I["w_pool"].rearrange("g c d -> c g d"), [], ["Wpool"])
                n = 0
                for tb in range(4):
                    for g in range(4):
                        ps, pk = self.bank(n % 2)
                        for tt in range(4):
                            ti = tb * 4 + tt
                            cls = 0 if ti == 0 else (2 if ti == 15 else 1)
                            for rel in range(3):
                                MM(S, ps[:, tt * 128:(tt + 1) * 128], self.upool[:, ti + rel, g * 128:(g + 1) * 128], PT[:, cls, g, rel, :],
                                   rel == 0, rel == 2, [("upool", ti + rel), "PT"], [pk])
                        p_, pk_ = pu[n % 2], ("pu", n % 2)
                        COPY(S, "act", p_[:, :], ps[:, :], [pk], [pk_])
                        ps2, pk2 = self.bank(2 + n % 2)
                        MM(S, ps2[:, :], Wp[:, g, :], p_[:, :], True, True, ["Wpool", pk_], [pk2])
                        z_, zk = zst[n % 2], ("pz", n % 2)
                        S.add("dve", lambda e, z_=z_, ps2=ps2, g=g: e.tensor_scalar(out=z_[:, :], in0=ps2[:, :], scalar1=self.vecs[:, self.V_PS + g:self.V_PS + g + 1],
                                                                             scalar2=None, op0=ALU.mult), [pk2, "vecs"], [zk])
                        DMA(S, "sp", self.mixT[g * 128:(g + 1) * 128, tb * 512:(tb + 1) * 512], z_[:, :], [zk], [("mixT", g, tb)])
                        n += 1
            self.barrier()
            with contextlib.ExitStack() as es1:
                Wq = self.sb("Wuq", [128, 6, 2304], BF16, es1)
                Wqs = self.sb("Wuqs", [128, 6, 12, 64], BF16, es1)
                rope = self.sb("rope_sb2", [64, 2, HALF], F32, es1)
                qnst = [self.sb(f"qnst{i}", [128, HALF], BF16, es1) for i in range(2)]
                qrst = [self.sb(f"qrst{i}", [64, HALF], BF16, es1) for i in range(2)]
                t1 = [self.sb(f"qt1_{i}", [64, 512], F32, es1) for i in range(2)]
                t2 = [self.sb(f"qt2_{i}", [64, 512], F32, es1) for i in range(2)]
                DMA(S, "sp", rope[:], I["rope"].rearrange("a p t -> p a t"), [], ["rope2"])
                DMA(S, "pool", Wq[:], I["w_uq"].rearrange("(k p) n -> p k n", p=128), [], ["Wuq"])
                wq4 = I["w_uq"].rearrange("(k p) (h c) -> k p h c", p=128, c=192)
                for k in range(6):
                    for q4 in range(4):
                        src0 = 128 + ROPE_PERM[q4 * 16]
                        DMA(S, "pool", Wqs[:, k, :, q4 * 16:q4 * 16 + 16], wq4[k, :, :, src0:src0 + 16], [], ["Wuqs"])
                e = 0
                for h in range(12):
                    qn_, qnk = qnst[h % 2], ("qnst", h % 2)
                    qr_, qrk = qrst[h % 2], ("qrst", h % 2)
                    for b in range(4):
                        c0 = 512 * b
                        ps, pk = self.bank(e % 2)
                        for kc in range(6):
                            MM(S, ps[:, :], Wq[:, kc, h * 192:h * 192 + 128], self.cqn[:, kc, c0:c0 + 512], kc == 0, kc == 5, ["Wuq", ("cqn", b)], [pk])
                        COPY(S, "act", qn_[:, c0:c0 + 512], ps[:, :], [pk], [qnk])
                        pr, prk = self.bank(2 + e % 2)
                        pw, pwk = self.bank(4 + e % 2)
                        for kc in range(6):
                            MM(S, pr[0:64, :], Wq[:, kc, h * 192 + 128:h * 192 + 192], self.cqn[:, kc, c0:c0 + 512], kc == 0, kc == 5, ["Wuq", ("cqn", b)], [prk])
                        for kc in range(6):
                            MM(S, pw[0:64, :], Wqs[:, kc, h, :], self.cqn[:, kc, c0:c0 + 512], kc == 0, kc == 5, ["Wuqs", ("cqn", b)], [pwk])
                        a1, a1k = t1[e % 2], ("qt1", e % 2)
                        a2, a2k = t2[e % 2], ("qt2", e % 2)
                        TT(S, "dve", a1[:, :], pr[0:64, :], rope[:, 0, c0:c0 + 512], ALU.mult, [prk, "rope2"], [a1k])
                        TT(S, "dve", a2[:, :], pw[0:64, :], rope[:, 1, c0:c0 + 512], ALU.mult, [pwk, "rope2"], [a2k])
                        TT(S, "pool", qr_[:, c0:c0 + 512], a1[:, :], a2[:, :], ALU.add, [a1k, a2k], [qrk])
                        e += 1
                    DMA(S, "sp", self.qn_s[h * 128:(h + 1) * 128, :], qn_[:, :], [qnk], [("qn_s", h)])
                    DMA(S, "sp", self.qr_s[h * 64:(h + 1) * 64, :], qr_[:, :], [qrk], [("qr_s", h)])
            self.barrier()
            with contextlib.ExitStack() as es1:
                NK = SEQ + CTX
                Wk = self.sb("Wukv", [128, 4, 3072], BF16, es1)
                knst = [self.sb(f"knst{i}", [128, NK], BF16, es1) for i in range(2)]
                vst = [self.sb(f"v1st{i}", [128, 1536], BF16, es1) for i in range(2)]
                DMA(S, "pool", Wk[:], I["w_ukv"].rearrange("(k p) n -> p k n", p=128), [], ["Wukv"])
                Wk4 = Wk[:].rearrange("p k (h t d) -> p k h t d", t=2, d=128)
                e = 0
                for h in range(12):
                    kn_, knk = knst[h % 2], ("knst", h % 2)
                    for b in range(9):
                        c0 = 512 * b
                        nt = min(512, NK - c0)
                        ps, pk = self.bank(e % 4)
                        for kc in range(4):
                            MM(S, ps[:, 0:nt], Wk[:, kc, h * 256:h * 256 + 128], ckv[:, kc, c0:c0 + nt], kc == 0, kc == 3, ["Wukv", ("ckv", min(c0 // HALF, 2))], [pk])
                        COPY(S, "act" if e % 2 == 0 else "dve", kn_[:, c0:c0 + nt], ps[:, 0:nt], [pk], [knk])
                        e += 1
                    DMA(S, "sp", self.kn_s[h * 128:(h + 1) * 128, :], kn_[:, :], [knk], [("kn_s", h)])
                for t in range(34):
                    v_, vk_ = vst[t % 2], ("v1st", t % 2)
                    for hg in range(3):
                        ps, pk = self.bank(4 + e % 4)
                        for kc in range(4):
                            MM(S, ps[:, :].rearrange("p (h d) -> p h d", d=128), ckv[:, kc, t * 128:(t + 1) * 128], Wk4[:, kc, hg * 4:hg * 4 + 4, 1, :],
                               kc == 0, kc == 3, ["Wukv", ("ckv", min(t // 16, 2))], [pk])
                        COPY(S, "act" if e % 2 == 0 else "dve", v_[:, hg * 512:(hg + 1) * 512], ps[:, :], [pk], [vk_])
                        e += 1
                    DMA(S, "sp", self.v1_s[t * 128:(t + 1) * 128, :], v_[:, :], [vk_], [("v1_s", t)])
        self.barrier()

    def l1_attn(self):
        S = self.S
        NK = SEQ + CTX
        scale = 192.0 ** -0.5
        with contextlib.ExitStack() as es:
            kn = [self.sb(f"a_kn{i}", [128, NK], BF16, es) for i in range(2)]
            vh = [self.sb(f"a_v{i}", [128, 34, 128], BF16, es) for i in range(2)]
            qn = [self.sb(f"a_qn{i}", [128, HALF], BF16, es) for i in range(2)]
            qr = [self.sb(f"a_qr{i}", [128, HALF], BF16, es) for i in range(2)]
            pT = [self.sb(f"a_pT{i}", [128, 512], BF16, es) for i in range(4)]
            acc = [self.sb(f"a_acc{i}", [128, 512], F32, es) for i in range(4)]
            rd = [self.sb(f"a_rd{i}", [128, 512], F32, es) for i in range(2)]
            ost = [self.sb(f"a_o{i}", [128, 512], BF16, es) for i in range(2)]
            ones_f = self.sb("ones_f", [128, 128], F32, es)
            S.add("pool", lambda e: e.memset(ones_f[:], 1.0), [], ["ones_f"])
            for i in range(2):
                S.add("pool", lambda e, i=i: e.memset(qr[i][:], 0.0), [], [("a_qr", i)])
            allv = [("v1_s", t) for t in range(34)]

            def load_head(h):
                i2 = h % 2
                DMA(S, "sp", kn[i2][:, :], self.kn_s[h * 128:(h + 1) * 128, :], [("kn_s", h)], [("a_kn", i2)])
                DMA(S, "sp", vh[i2][:, :, :], self.v1_s[:, h * 128:(h + 1) * 128].rearrange("(t p) d -> p t d", p=128), allv, [("a_v", i2)])
                DMA(S, "sp", qn[i2][:, :], self.qn_s[h * 128:(h + 1) * 128, :], [("qn_s", h)], [("a_qn", i2)])
                DMA(S, "sp", qr[i2][0:64, :], self.qr_s[h * 64:(h + 1) * 64, :], [("qr_s", h)], [("a_qr", i2)])

            def front(e, h, qb, kc):
                i2 = h % 2
                c0 = 512 * qb
                pS, pSk = self.bank(e % 4)
                MM(S, pS[:, :], kn[i2][:, kc * 128:(kc + 1) * 128], qn[i2][:, c0:c0 + 512], True, False, [("a_kn", i2), ("a_qn", i2)], [pSk])
                MM(S, pS[:, :], self.krr[:, kc * 128:(kc + 1) * 128], qr[i2][:, c0:c0 + 512], False, True, ["krr", ("a_qr", i2)], [pSk])
                ACT(S, pT[e % 4][:, :], pS[:, :], AF.Exp, [pSk], [("a_pT", e % 4)], scale=scale)

            def back(e, h, qb, kc):
                i2 = h % 2
                c0 = 512 * qb
                n = h * 4 + qb
                p_, pk_ = pT[e % 4], ("a_pT", e % 4)
                pO, pOk = self.bank(4 + 2 * (n % 2))
                MM(S, pO[:, :], vh[i2][:, kc, :], p_[:, :], kc == 0, kc == 33, [("a_v", i2), pk_], [pOk])
                a_ = acc[2 * (n % 2) + kc % 2]
                ak = ("a_acc", 2 * (n % 2) + kc % 2)
                if kc < 2:
                    COPY(S, "dve", a_[:, :], p_[:, :], [pk_], [ak])
                else:
                    TT(S, "dve", a_[:, :], a_[:, :], p_[:, :], ALU.add, [ak, pk_], [ak])
                if kc == 33:
                    pD, pDk = self.bank(5 + 2 * (n % 2))
                    for j in range(2):
                        MM(S, pD[:, :], ones_f[:], acc[2 * (n % 2) + j][:, :], j == 0, j == 1, ["ones_f", ("a_acc", 2 * (n % 2) + j)], [pDk])
                    RECIP(S, rd[n % 2][:, :], pD[:, :], [pDk], [("a_rd", n % 2)])
                    TT(S, "dve", ost[n % 2][:, :], pO[:, :], rd[n % 2][:, :], ALU.mult, [pOk, ("a_rd", n % 2)], [("a_o", n % 2)])
                    DMA(S, "sp", self.mixT[512 + h * 128:640 + h * 128, c0:c0 + 512], ost[n % 2][:, :], [("a_o", n % 2)], [("mixT", 4 + h, qb)])

            items = [(h, qb, kc) for h in range(12) for qb in range(4) for kc in range(34)]
            LAG = 2
            for e in range(len(items) + LAG):
                if e < len(items):
                    h, qb, kc = items[e]
                    if qb == 0 and kc == 0:
                        load_head(h)
                    front(e, h, qb, kc)
                if e >= LAG:
                    back(e - LAG, *items[e - LAG])

    def final_norm(self):
        S = self.S
        with contextlib.ExitStack() as es:
            xTb = self.sb("z_xTb", [128, 16, 512], F32, es)
            sq = self.sb("z_sq", [128, 16, 512], BF16, es)
            yT = self.sb("z_yT", [128, 16, 512], F32, es)
            r1 = self.sb("z_r1", [128, 512], F32, es)
            rstd = self.sb("z_rstd", [128, 512], F32, es)
            ot = [self.sb(f"z_ot{i}", [128, D], F32, es) for i in range(2)]
            e = 0
            n = 0
            for b in range(4):
                c0 = 512 * b
                DMA(S, "sp", xTb[:], self.xTc[:, c0:c0 + 512].rearrange("(k p) t -> p k t", p=128), [(self.xk, k, b) for k in range(16)], ["z_xTb"])
                ACT(S, sq[:], xTb[:], AF.Square, ["z_xTb"], ["z_sq"])
                ps, pk = self.bank(7)
                for kc in range(16):
                    MM(S, ps[:, :], self.ones_b[:], sq[:, kc, :], kc == 0, kc == 15, ["ones_b", "z_sq"], [pk])
                ACT(S, r1[:], ps[:, :], AF.Sqrt, [pk, "epsb"], ["z_r1"], bias=self.epsb[:], scale=1.0 / D)
                RECIP(S, rstd[:], r1[:], ["z_r1"], ["z_rstd"])
                for kc in range(16):
                    STT(S, "dve", yT[:, kc, :], xTb[:, kc, :], self.vecs[:, self.V_FG + kc:self.V_FG + kc + 1], rstd[:], ALU.mult, ALU.mult,
                        ["z_xTb", "vecs", "z_rstd"], [("z_yT", kc)])
                for t in range(4):
                    o_, ok_ = ot[n % 2], ("z_ot", n % 2)
                    n += 1
                    for k4 in range(4):
                        pt, ptk = self.bank(e % 4)
                        for j in range(4):
                            kc = k4 * 4 + j
                            TR(S, pt[:, j * 128:(j + 1) * 128], yT[:, kc, t * 128:(t + 1) * 128], self.ident_f[:], [("z_yT", kc), "ident_f"], [ptk])
                        COPY(S, "act" if e % 2 == 0 else "dve", o_[:, k4 * 512:(k4 + 1) * 512], pt[:, :], [ptk], [ok_])
                        e += 1
                    DMA(S, "sp", self.out[c0 + t * 128:c0 + (t + 1) * 128, :], o_[:, :], [ok_], [("out", b, t)])

    def outproj(self, l, W, ntok):
        S = self.S
        nb = (ntok + 511) // 512
        with contextlib.ExitStack() as es:
            mixr = self.sb("mixr", [128, 16, NTOK], BF16, es)
            wts = [self.sb(f"wo{i}", [128, 16, 512], BF16, es) for i in range(2)]
            xres = [self.sb(f"xres{i}", [128, NTOK], F32, es) for i in range(2)]
            for b in range(nb):
                c0 = 512 * b
                nt = min(512, ntok - c0)
                DMA(S, "sp", mixr[:, :, c0:c0 + nt], self.mixT[:, c0:c0 + nt].rearrange("(k p) t -> p k t", p=128),
                    [("mixT", k, b) for k in range(16)], [("mixr", b)])
            e = 0
            for cg in range(4):
                wt, wk = wts[cg % 2], ("wo", cg % 2)
                DMA(S, "pool", wt[:], W[:, cg * 512:(cg + 1) * 512].rearrange("(k p) n -> p k n", p=128), [], [wk])
                for j in range(4):
                    dc = cg * 4 + j
                    xr, xk = xres[dc % 2], ("xres", dc % 2)
                    DMA(S, "sp", xr[:, 0:ntok], self.xTc[dc * 128:(dc + 1) * 128, 0:ntok], [(self.xk, dc, b) for b in range(nb)], [xk])
                    for b in range(nb):
                        c0 = 512 * b
                        nt = min(512, ntok - c0)
                        m = 1 if c0 >= HALF else 0
                        ps, pk = self.bank(e % 4)
                        for kc in range(16):
                            MM(S, ps[:, 0:nt], wt[:, kc, j * 128:(j + 1) * 128], mixr[:, kc, c0:c0 + nt], kc == 0, kc == 15, [wk, ("mixr", b)], [pk])
                        STT(S, "dve", xr[:, c0:c0 + nt], ps[:, 0:nt], self.mcol(l, 2, dc, m), xr[:, c0:c0 + nt], ALU.mult, ALU.add,
                            [pk, xk, ("modT", l)], [xk])
                        e += 1
                    DMA(S, "sp", self.xTc[dc * 128:(dc + 1) * 128, 0:ntok], xr[:, 0:ntok], [xk], [(self.xk, dc, b) for b in range(nb)])

    def ffn(self, l, ntok):
        S, I = self.S, self.I
        nb = (ntok + 511) // 512
        with contextlib.ExitStack() as es:
            with contextlib.ExitStack() as es1:
                h2 = self.sb("h2r", [128, 16, NTOK], BF16, es1)
                wg = [self.sb(f"wg{i}", [128, 16, 512], BF16, es1) for i in range(2)]
                wu = [self.sb(f"wu{i}", [128, 16, 512], BF16, es1) for i in range(2)]
                sg = [self.sb(f"sg{i}", [128, 512], F32, es1) for i in range(2)]
                ast = [self.sb(f"ast{i}", [128, NTOK], BF16, es1) for i in range(4)]
                xTb = self.sb("f_xTb", [128, 16, 256], F32, es1)
                sq = self.sb("f_sq", [128, 16, 256], BF16, es1)
                tmp = self.sb("f_tmp", [128, 2, 512], F32, es1)
                r1 = self.sb("f_r1", [128, 512], F32, es1)
                rstd = self.sb("f_rstd", [128, 512], F32, es1)
                cnt = [0]

                def load_w(cg):
                    DMA(S, "pool", wg[cg % 2][:], I["w_ffn_gate"][l, :, cg * 512:(cg + 1) * 512].rearrange("(k p) n -> p k n", p=128), [], [("wg", cg % 2)])
                    DMA(S, "pool", wu[cg % 2][:], I["w_ffn_up"][l, :, cg * 512:(cg + 1) * 512].rearrange("(k p) n -> p k n", p=128), [], [("wu", cg % 2)])

                def gemm(cg, j, b):
                    g_, gk = wg[cg % 2], ("wg", cg % 2)
                    u_, uk = wu[cg % 2], ("wu", cg % 2)
                    fc = cg * 4 + j
                    a_, ak = ast[fc % 4], ("ast", fc % 4)
                    e = cnt[0]
                    cnt[0] += 1
                    c0 = 512 * b
                    nt = min(512, ntok - c0)
                    pg, pgk = self.bank(2 * (e % 3))
                    pu, puk = self.bank(2 * (e % 3) + 1)
                    for kc in range(16):
                        MM(S, pg[:, 0:nt], g_[:, kc, j * 128:(j + 1) * 128], h2[:, kc, c0:c0 + nt], kc == 0, kc == 15, [gk, ("h2r", b)], [pgk])
                    for kc in range(16):
                        MM(S, pu[:, 0:nt], u_[:, kc, j * 128:(j + 1) * 128], h2[:, kc, c0:c0 + nt], kc == 0, kc == 15, [uk, ("h2r", b)], [puk])
                    s_, sk = sg[e % 2], ("sg", e % 2)
                    ACT(S, s_[:, 0:nt], pg[:, 0:nt], AF.Silu, [pgk], [sk])
                    TT(S, "dve", a_[:, c0:c0 + nt], s_[:, 0:nt], pu[:, 0:nt], ALU.mult, [sk, puk], [ak])

                def store(cg, j):
                    fc = cg * 4 + j
                    DMA(S, "sp", self.aT[fc * 128:(fc + 1) * 128, 0:ntok], ast[fc % 4][:, 0:ntok], [("ast", fc % 4)], [("aT", fc)])

                load_w(0)
                for b in range(nb):
                    for hb in range(2):
                        c0 = 512 * b + 256 * hb
                        nt = min(256, ntok - c0)
                        if nt <= 0:
                            continue
                        m = 1 if c0 >= HALF else 0
                        DMA(S, "sp", xTb[:, :, 0:nt], self.xTc[:, c0:c0 + nt].rearrange("(k p) t -> p k t", p=128),
                            [(self.xk, k, b) for k in range(16)], ["f_xTb"])

                        def out_fn(kc, c0=c0, nt=nt, b=b):
                            return h2[:, kc, c0:c0 + nt], ("h2r", b)
                        self._norm_multi(xTb, ["f_xTb"], nt, l, 1, m, sq, tmp, rstd, r1, out_fn, 7)
                    if b == 0:
                        load_w(1)
                    for j in range(4):
                        gemm(0, j, b)
                for j in range(4):
                    store(0, j)
                for cg in range(1, 11):
                    if cg + 1 < 11:
                        load_w(cg + 1)
                    for j in range(4):
                        for b in range(nb):
                            gemm(cg, j, b)
                        store(cg, j)
            self.barrier()
            with contextlib.ExitStack() as es1:
                aTrs = [self.sb(f"aTr{i}", [128, 44, 768], BF16, es1) for i in range(2)]
                wd = [self.sb(f"wd{i}", [128, 44, 256], BF16, es1) for i in range(2)]
                xres = [self.sb(f"dxres{i}", [128, 768], F32, es1) for i in range(2)]
                allA = [("aT", fc) for fc in range(44)]
                nsb = (ntok + 767) // 768

                def load_a(sbk):
                    c0 = 768 * sbk
                    nt = min(768, ntok - c0)
                    DMA(S, "sp", aTrs[sbk % 2][:, :, 0:nt], self.aT[:, c0:c0 + nt].rearrange("(k p) t -> p k t", p=128), allA, [("aTr", sbk % 2)])

                n = 0
                e = 0
                load_a(0)
                for sbk in range(nsb):
                    c0 = 768 * sbk
                    nt = min(768, ntok - c0)
                    aTr, ak_ = aTrs[sbk % 2], ("aTr", sbk % 2)
                    if sbk + 1 < nsb:
                        load_a(sbk + 1)
                    segs = [(0, min(512, nt))] + ([(512, nt - 512)] if nt > 512 else [])
                    blks = sorted(set([(c0 + a) // 512 for a, _ in segs] + [(c0 + a + w - 1) // 512 for a, w in segs]))
                    for cg in range(8):
                        w_, wk = wd[n % 2], ("wd", n % 2)
                        n += 1
                        DMA(S, "pool", w_[:], I["w_ffn_down"][l, :, cg * 256:(cg + 1) * 256].rearrange("(k p) n -> p k n", p=128), [], [wk])
                        for j in range(2):
                            dc = cg * 2 + j
                            xr, xk = xres[dc % 2], ("dxres", dc % 2)
                            xkeys = [(self.xk, dc, b) for b in blks]
                            DMA(S, "sp", xr[:, 0:nt], self.xTc[dc * 128:(dc + 1) * 128, c0:c0 + nt], xkeys, [xk])
                            pS = self.pp[e % 4]
                            pks = [("ps", 2 * (e % 4)), ("ps", 2 * (e % 4) + 1)]
                            for kc in range(44):
                                for (a, w) in segs:
                                    MM(S, pS[:, a:a + w], w_[:, kc, j * 128:(j + 1) * 128], aTr[:, kc, a:a + w], kc == 0, kc == 43, [wk, ak_],
                                       [pks[0] if a == 0 else pks[1]])
                            for (a, w) in segs:
                                m = 1 if c0 + a >= HALF else 0
                                STT(S, "dve", xr[:, a:a + w], pS[:, a:a + w], self.mcol(l, 5, dc, m), xr[:, a:a + w], ALU.mult, ALU.add,
                                    [pks[0] if a == 0 else pks[1], xk, ("modT", l)], [xk])
                            e += 1
                            DMA(S, "sp", self.xTc[dc * 128:(dc + 1) * 128, c0:c0 + nt], xr[:, 0:nt], [xk], xkeys)


def make_inputs(core, inp):
    b, s = core // 2, core % 2
    f = np.float32
    vecs = np.zeros((128, 400), f)
    c2 = np.stack([_fm(inp["c"][b]), _fm(inp["c_ctx"])], axis=-1)
    vecs[:, 0:32] = c2.reshape(128, 32)
    for l in range(2):
        vecs[:, 32 + 16 * l:48 + 16 * l] = _fm(inp["norm1_g"][l])
        vecs[:, 64 + 16 * l:80 + 16 * l] = _fm(inp["norm2_g"][l])
        vecs[:, 112 + 96 * l:208 + 96 * l] = _fm(inp["b_mod"][l])
    vecs[:, 96:112] = _fm(inp["final_g"])
    vecs[:, 304:308] = _fm(inp["pool_scale"][0])
    vecs[:, 308:314] = _fm(inp["mla_gq"][0])
    vecs[:, 314:318] = _fm(inp["mla_gkv"][0])
    xb = inp["x"][b]
    own0 = HALF * s

    def halo(s_):
        o = HALF * s_
        hb = xb[o - 256:o] if s_ == 1 else xb[0:256]
        ha = xb[o + HALF:o + HALF + 256] if s_ == 0 else xb[SEQ - 256:SEQ]
        return np.ascontiguousarray(np.concatenate([hb, ha], 0))
    rpb = np.asarray(inp["na_rpb"][0], f)
    return {
        "x_halo2": halo(1 - s),
        "csn2": _dft_tables(s, 1 - s),
        "nabias2": _na_class_tables(rpb, 1 - s),
        "rope2": _rope_tables(1 - s),
        "x_own": np.ascontiguousarray(xb[own0:own0 + HALF]),
        "x_par": np.ascontiguousarray(xb[HALF * (1 - s):HALF * (1 - s) + HALF]),
        "x_halo": halo(s),
        "ctxb": np.ascontiguousarray(inp["ctx"][b]),
        "vecs": vecs,
        "ident": np.eye(128, dtype=f),
        "w_mod": inp["w_mod"],
        "w_in_ab": inp["w_in_ab"][0],
        "w_four": inp["w_four"][0],
        "w_out_ab": inp["w_out_ab"][0],
        "w_ffn_gate": inp["w_ffn_gate"],
        "w_ffn_up": inp["w_ffn_up"],
        "w_ffn_down": inp["w_ffn_down"],
        "csn": _dft_tables(s),
        "csx": _dft_ctx(),
        "csc": _dft_ch(),
        "nabias": _na_class_tables(np.asarray(inp["na_rpb"][0], f), s),
        "w_in_cd": inp["w_in_cd"][0],
        "w_pool": inp["w_pool"][0],
        "w_uq": inp["w_uq"][0],
        "w_ukv": inp["w_ukv"][0],
        "w_out_cd": inp["w_out_cd"][0],
        "rope": _rope_tables(s),
        "ptab": _pool_tables(s),
    }


def run_all(inp, cores):
    inp = {k: np.asarray(v) for k, v in inp.items()}
    full = {c: make_inputs(c, inp) for c in cores}
    ids = list(range(len(cores)))
    ncA = Prog("A", 3).build()
    namesA = list(_NAMES["A"])
    rA = run_bass_kernel_spmd(ncA, [{k: full[c][k] for k in namesA} for c in cores], core_ids=ids)
    xT = {c: np.asarray(r["xT_out"]) for c, r in zip(cores, rA.results)}
    ncP = Prog("B", 1).build()
    namesP = list(_NAMES["B"])
    dummy = np.zeros((2, 704, NTOK), ml_dtypes.bfloat16)
    mapsP = []
    for c in cores:
        d = dict(full[c])
        d["xT_in"] = xT[c]
        d["exg"] = dummy
        mapsP.append({k: d[k] for k in namesP})
    rP = run_bass_kernel_spmd(ncP, mapsP, core_ids=ids)
    ex = {c: np.asarray(r["ex_out"]) for c, r in zip(cores, rP.results)}
    ncB = Prog("B", 6).build()
    namesB = list(_NAMES["B"])
    mapsB = []
    for c in cores:
        d = dict(full[c])
        d["xT_in"] = xT[c]
        d["exg"] = np.ascontiguousarray(np.stack([ex[c - c % 2], ex[c - c % 2 + 1]], 0))
        mapsB.append({k: d[k] for k in namesB})
    rB = run_bass_kernel_spmd(ncB, mapsB, core_ids=ids)
    return {c: np.asarray(r["out"]) for c, r in zip(cores, rB.results)}


_NAMES = {}


def run_b(inp, cores, xT, ex, stage):
    inp = {k: np.asarray(v) for k, v in inp.items()}
    ncB = Prog("B", stage, dbg=True).build()
    namesB = list(_NAMES["B"])
    mapsB = []
    for c in cores:
        d = make_inputs(c, inp)
        d["xT_in"] = xT[c]
        d["exg"] = np.ascontiguousarray(np.stack([ex[c - c % 2], ex[c - c % 2 + 1]], 0))
        mapsB.append({k: d[k] for k in namesB})
    rB = run_bass_kernel_spmd(ncB, mapsB, core_ids=list(range(len(cores))))
    return rB.results


def run_fused(inp, cores):
    inp = {k: np.asarray(v) for k, v in inp.items()}
    nc = Prog("F", 6).build()
    names = list(_NAMES["F"])
    maps = []
    for c in cores:
        d = make_inputs(c, inp)
        maps.append({k: d[k] for k in names})
    r = run_bass_kernel_spmd(nc, maps, core_ids=list(range(len(cores))))
    return {c: np.asarray(rr["out"]) for c, rr in zip(cores, r.results)}


def kernel(**inputs):
    cores = list(range(8))
    res = run_fused(inputs, cores)
    out = np.zeros((4, SEQ, D), np.float32)
    for c in cores:
        out[c // 2, HALF * (c % 2):HALF * (c % 2) + HALF] = res[c]
    return out
```
